# Optimizing a Trainium2 kernel written in Bass

```python
import math
import jax, jax.numpy as jnp
from jax import lax
import numpy as np

D_MODEL = 2048
BATCH = 4
SEQ = 2048
DEPTH = 2
DEC_BATCH = 128
DEC_SEQ = 8
PAST_LEN = 16384
PAGE_SIZE = 128

N_AB = (DEPTH + 1) // 2
N_ML = DEPTH // 2
EPS = 1e-6

GLA_HEADS = 4
GLA_DV = (D_MODEL // 2) // GLA_HEADS
GLA_DK = GLA_DV // 2
GLA_RANK = 16
GLA_GATE_NORM = 16.0
GLA_CHUNK = 64
GLA_QK = GLA_HEADS * GLA_DK
GLA_V = GLA_HEADS * GLA_DV

S5_WIDTH = D_MODEL // 2
S5_GROUP = 16
S5_GROUPS = S5_WIDTH // S5_GROUP
S5_STATE = 64

ML_HEADS = 4
ML_DV = D_MODEL // ML_HEADS
ML_DQK = ML_DV // 2
ML_CHUNK = 64
ML_QK = ML_HEADS * ML_DQK
ML_V = ML_HEADS * ML_DV

D_FF = ((8 * D_MODEL + 3 * 256 - 1) // (3 * 256)) * 256

AB_SPLITS = (GLA_QK, 2 * GLA_QK, 2 * GLA_QK + GLA_V, 2 * GLA_QK + 2 * GLA_V, 2 * GLA_QK + 2 * GLA_V + GLA_RANK)
AB_IN = AB_SPLITS[-1] + S5_WIDTH
AB_OUT = GLA_V + S5_WIDTH
ML_SPLITS = (ML_QK, 2 * ML_QK, 2 * ML_QK + ML_V, 2 * ML_QK + 2 * ML_V, 2 * ML_QK + 2 * ML_V + ML_HEADS)
ML_IN = ML_SPLITS[-1] + ML_HEADS

kernel_name = "hybrid_gla_s5_mlstm_adaln_step"


def rmsnorm(x, g):
    xf = x.astype(jnp.float32)
    return xf * lax.rsqrt(jnp.mean(xf * xf, axis=-1, keepdims=True) + EPS) * g.astype(jnp.float32)


def head_rmsnorm(x, g):
    return x * lax.rsqrt(jnp.mean(x * x, axis=-1, keepdims=True) + EPS) * g.astype(jnp.float32)


def modulate(x, c, ada_w, ada_b, norm_g):
    mod = jax.nn.silu(c.astype(jnp.float32)) @ ada_w + ada_b
    shift, scale, gate = jnp.split(mod[:, None, :], 3, axis=-1)
    return rmsnorm(x, norm_g) * (1.0 + scale) + shift, gate


def swiglu(h, w1, w3, w2):
    return (jax.nn.silu(h @ w1) * (h @ w3)) @ w2


def _chunks(a, n_chunks, size):
    return jnp.swapaxes(a.reshape(a.shape[0], n_chunks, size, *a.shape[2:]), 0, 1)


def _unchunks(a):
    a = jnp.swapaxes(a, 0, 1)
    return a.reshape(a.shape[0], a.shape[1] * a.shape[2], *a.shape[3:])


def gla_mix(q, k, v, log_a, s0):
    t = q.shape[1]
    size = math.gcd(t, GLA_CHUNK)
    nc = t // size
    causal = jnp.tril(jnp.ones((size, size), dtype=bool))[None, :, :, None, None]

    def step(s, xs):
        qc, kc, vc, gc = xs
        b = jnp.cumsum(gc, axis=1)
        diff = jnp.where(causal, b[:, :, None] - b[:, None, :], -jnp.inf)
        att = jnp.einsum("bthd,bshd,btshd->bhts", qc, kc, jnp.exp(diff))
        o = jnp.einsum("bhts,bshv->bthv", att, vc) + jnp.einsum("bthd,bhdv->bthv", qc * jnp.exp(b), s)
        b_last = b[:, -1]
        s_new = jnp.exp(b_last)[..., None] * s + jnp.einsum(
            "bshd,bshv->bhdv", kc * jnp.exp(b_last[:, None] - b), vc)
        return s_new, o

    s_fin, o = lax.scan(step, s0, (_chunks(q, nc, size), _chunks(k, nc, size),
                                   _chunks(v, nc, size), _chunks(log_a, nc, size)))
    return _unchunks(o), s_fin


def s5_mix(u, lam_re, lam_im, log_dt, b_re, b_im, c_re, c_im, d_skip, h0_re, h0_im):
    f32 = jnp.float32
    lam_re = lam_re.astype(f32)
    lam_im = lam_im.astype(f32)
    dt = jnp.exp(log_dt.astype(f32))[:, None]
    mag = jnp.exp(lam_re * dt)
    a_re = mag * jnp.cos(lam_im * dt)
    a_im = mag * jnp.sin(lam_im * dt)
    den = lam_re * lam_re + lam_im * lam_im
    f_re = ((a_re - 1.0) * lam_re + a_im * lam_im) / den
    f_im = (a_im * lam_re - (a_re - 1.0) * lam_im) / den
    bb_re = f_re[..., None] * b_re - f_im[..., None] * b_im
    bb_im = f_re[..., None] * b_im + f_im[..., None] * b_re
    bu_re = jnp.einsum("btgc,gpc->btgp", u, bb_re)
    bu_im = jnp.einsum("btgc,gpc->btgp", u, bb_im)
    bu_re = bu_re.at[:, 0].add(a_re * h0_re - a_im * h0_im)
    bu_im = bu_im.at[:, 0].add(a_re * h0_im + a_im * h0_re)
    ar = jnp.broadcast_to(a_re, bu_re.shape)
    ai = jnp.broadcast_to(a_im, bu_im.shape)

    def combine(e1, e2):
        a1r, a1i, b1r, b1i = e1
        a2r, a2i, b2r, b2i = e2
        return (a2r * a1r - a2i * a1i,
                a2r * a1i + a2i * a1r,
                a2r * b1r - a2i * b1i + b2r,
                a2r * b1i + a2i * b1r + b2i)

    _, _, h_re, h_im = lax.associative_scan(combine, (ar, ai, bu_re, bu_im), axis=1)
    y = (jnp.einsum("btgp,gcp->btgc", h_re, c_re) - jnp.einsum("btgp,gcp->btgc", h_im, c_im)
         + d_skip * u)
    return y, h_re[:, -1], h_im[:, -1]


def mlstm_mix(q, k, v, i_pre, logf, c0, n0, m0):
    t = q.shape[1]
    size = math.gcd(t, ML_CHUNK)
    nc = t // size
    causal = jnp.tril(jnp.ones((size, size), dtype=bool))[None, :, :, None]

    def step(carry, xs):
        cm, nv, m = carry
        qc, kc, vc, ic, fc = xs
        b = jnp.cumsum(fc, axis=1)
        lw = jnp.where(causal, b[:, :, None] - b[:, None, :] + ic[:, None, :], -jnp.inf)
        l_inter = b + m[:, None]
        m_t = jnp.maximum(l_inter, jnp.max(lw, axis=2))
        w = jnp.exp(lw - m_t[:, :, None])
        g_inter = jnp.exp(l_inter - m_t)
        sw = jnp.einsum("bthd,bshd->btsh", qc, kc) * w
        num = jnp.einsum("btsh,bshv->bthv", sw, vc) + g_inter[..., None] * jnp.einsum("bhvd,bthd->bthv", cm, qc)
        den = jnp.sum(sw, axis=2) + g_inter * jnp.einsum("bhd,bthd->bth", nv, qc)
        h = num / jnp.maximum(jnp.abs(den), jnp.exp(-m_t))[..., None]
        b_last = b[:, -1]
        l_src = b_last[:, None] - b + ic
        m_new = jnp.maximum(b_last + m, jnp.max(l_src, axis=1))
        ws = jnp.exp(l_src - m_new[:, None])
        decay = jnp.exp(b_last + m - m_new)
        c_new = decay[..., None, None] * cm + jnp.einsum("bsh,bshv,bshd->bhvd", ws, vc, kc)
        n_new = decay[..., None] * nv + jnp.einsum("bsh,bshd->bhd", ws, kc)
        return (c_new, n_new, m_new), h

    (c_f, n_f, m_f), h = lax.scan(step, (c0, n0, m0), (
        _chunks(q, nc, size), _chunks(k, nc, size), _chunks(v, nc, size),
        _chunks(i_pre, nc, size), _chunks(logf, nc, size)))
    return _unchunks(h), c_f, n_f, m_f


def trunk(x, c, st_gla, st_re, st_im, st_c, st_n, st_m, p):
    f32 = jnp.float32
    bsz, t, _ = x.shape
    new_gla, new_re, new_im, new_c, new_n, new_m = [], [], [], [], [], []
    for li in range(DEPTH):
        j = li // 2
        if li % 2 == 0:
            h, gate = modulate(x, c, p["ab_ada_w"][j], p["ab_ada_b"][j], p["ab_norm_g"][j])
            z = (h @ p["ab_w_in"][j]).astype(f32)
            q, k, v, r, g_lr, u = jnp.split(z, AB_SPLITS, axis=-1)
            q = q.reshape(bsz, t, GLA_HEADS, GLA_DK) * GLA_DK ** -0.5
            k = k.reshape(bsz, t, GLA_HEADS, GLA_DK)
            v = v.reshape(bsz, t, GLA_HEADS, GLA_DV)
            log_a = (jax.nn.log_sigmoid(g_lr @ p["gla_w_gate"][j] + p["gla_b_gate"][j])
                     / GLA_GATE_NORM).reshape(bsz, t, GLA_HEADS, GLA_DK)
            o, s_gla = gla_mix(q, k, v, log_a, st_gla[j].astype(f32))
            o = head_rmsnorm(o, p["gla_norm_g"][j]) * jax.nn.silu(r.reshape(bsz, t, GLA_HEADS, GLA_DV))
            y, h_re, h_im = s5_mix(u.reshape(bsz, t, S5_GROUPS, S5_GROUP),
                                   p["s5_lam_re"][j], p["s5_lam_im"][j], p["s5_log_dt"][j],
                                   p["s5_b_re"][j], p["s5_b_im"][j], p["s5_c_re"][j], p["s5_c_im"][j],
                                   p["s5_d"][j], st_re[j].astype(f32), st_im[j].astype(f32))
            y = jax.nn.gelu(y.reshape(bsz, t, S5_WIDTH))
            y = y * jax.nn.sigmoid(y @ p["s5_w_glu"][j] + p["s5_b_glu"][j])
            out = jnp.concatenate([o.reshape(bsz, t, GLA_V), y], axis=-1) @ p["ab_w_out"][j]
            x = x + (gate * out).astype(x.dtype)
            new_gla.append(s_gla)
            new_re.append(h_re)
            new_im.append(h_im)
        else:
            h, gate = modulate(x, c, p["ml_ada_w"][j], p["ml_ada_b"][j], p["ml_norm_g"][j])
            z = (h @ p["ml_w_in"][j]).astype(f32)
            q, k, v, og, ig, fg = jnp.split(z, ML_SPLITS, axis=-1)
            q = q.reshape(bsz, t, ML_HEADS, ML_DQK)
            k = k.reshape(bsz, t, ML_HEADS, ML_DQK) * ML_DQK ** -0.5
            v = v.reshape(bsz, t, ML_HEADS, ML_DV)
            i_pre = ig + p["ml_b_i"][j]
            logf = jax.nn.log_sigmoid(fg + p["ml_b_f"][j])
            hh, c1, n1, m1 = mlstm_mix(q, k, v, i_pre, logf, st_c[j].astype(f32),
                                       st_n[j].astype(f32), st_m[j].astype(f32))
            hh = head_rmsnorm(hh, p["ml_out_norm_g"][j]) * jax.nn.sigmoid(og.reshape(bsz, t, ML_HEADS, ML_DV))
            out = hh.reshape(bsz, t, ML_V) @ p["ml_w_out"][j]
            x = x + (gate * out).astype(x.dtype)
            new_c.append(c1)
            new_n.append(n1)
            new_m.append(m1)
        h, gate = modulate(x, c, p["ffn_ada_w"][li], p["ffn_ada_b"][li], p["ffn_norm_g"][li])
        x = x + (gate * swiglu(h, p["ffn_w1"][li], p["ffn_w3"][li], p["ffn_w2"][li])).astype(x.dtype)
    y = rmsnorm(x, p["final_norm_g"]).astype(x.dtype)
    return (y, jnp.stack(new_gla), jnp.stack(new_re), jnp.stack(new_im),
            jnp.stack(new_c), jnp.stack(new_n), jnp.stack(new_m))


def setup_inputs(seed: int = 0) -> dict:
    key = jax.random.key(seed)
    keys = list(jax.random.split(key, 64))
    it = iter(keys)

    def nrm(shape, scale):
        return jax.random.normal(next(it), shape, jnp.float32) * scale

    D = D_MODEL
    ada_scale = 0.5 * D ** -0.5
    lam_im0 = jnp.pi * jnp.arange(S5_STATE, dtype=jnp.float32)
    f_bias0 = jnp.linspace(3.0, 6.0, ML_HEADS, dtype=jnp.float32)
    inp = {
        "x_prompt": nrm((BATCH, SEQ, D), 1.0),
        "x_sample": nrm((DEC_BATCH, DEC_SEQ, D), 1.0),
        "c_prompt": nrm((BATCH, D), 1.0),
        "c_sample": nrm((DEC_BATCH, D), 1.0),
        "state_gla": nrm((N_AB, DEC_BATCH, GLA_HEADS, GLA_DK, GLA_DV), 0.5),
        "state_s5_re": nrm((N_AB, DEC_BATCH, S5_GROUPS, S5_STATE), 0.3),
        "state_s5_im": nrm((N_AB, DEC_BATCH, S5_GROUPS, S5_STATE), 0.3),
        "state_mlstm_C": nrm((N_ML, DEC_BATCH, ML_HEADS, ML_DV, ML_DQK), 0.1),
        "state_mlstm_n": nrm((N_ML, DEC_BATCH, ML_HEADS, ML_DQK), 0.1),
        "state_mlstm_m": nrm((N_ML, DEC_BATCH, ML_HEADS), 1.0),
        "ab_ada_w": nrm((N_AB, D, 3 * D), ada_scale),
        "ab_ada_b": nrm((N_AB, 3 * D), 0.02),
        "ab_norm_g": 1.0 + nrm((N_AB, D), 0.02),
        "ab_w_in": nrm((N_AB, D, AB_IN), D ** -0.5),
        "gla_w_gate": nrm((N_AB, GLA_RANK, GLA_QK), GLA_RANK ** -0.5),
        "gla_b_gate": 1.0 + nrm((N_AB, GLA_QK), 0.5),
        "gla_norm_g": 1.0 + nrm((N_AB, GLA_HEADS, GLA_DV), 0.02),
        "s5_lam_re": -0.5 + nrm((N_AB, S5_GROUPS, S5_STATE), 0.01),
        "s5_lam_im": lam_im0 + nrm((N_AB, S5_GROUPS, S5_STATE), 0.01),
        "s5_log_dt": jax.random.uniform(next(it), (N_AB, S5_GROUPS), jnp.float32,
                                        minval=math.log(1e-3), maxval=math.log(1e-1)),
        "s5_b_re": nrm((N_AB, S5_GROUPS, S5_STATE, S5_GROUP), (2 * S5_GROUP) ** -0.5),
        "s5_b_im": nrm((N_AB, S5_GROUPS, S5_STATE, S5_GROUP), (2 * S5_GROUP) ** -0.5),
        "s5_c_re": nrm((N_AB, S5_GROUPS, S5_GROUP, S5_STATE), S5_STATE ** -0.5),
        "s5_c_im": nrm((N_AB, S5_GROUPS, S5_GROUP, S5_STATE), S5_STATE ** -0.5),
        "s5_d": nrm((N_AB, S5_GROUPS, S5_GROUP), 0.5),
        "s5_w_glu": nrm((N_AB, S5_WIDTH, S5_WIDTH), S5_WIDTH ** -0.5),
        "s5_b_glu": nrm((N_AB, S5_WIDTH), 0.02),
        "ab_w_out": nrm((N_AB, AB_OUT, D), AB_OUT ** -0.5),
        "ml_ada_w": nrm((N_ML, D, 3 * D), ada_scale),
        "ml_ada_b": nrm((N_ML, 3 * D), 0.02),
        "ml_norm_g": 1.0 + nrm((N_ML, D), 0.02),
        "ml_w_in": nrm((N_ML, D, ML_IN), D ** -0.5),
        "ml_b_i": nrm((N_ML, ML_HEADS), 0.1),
        "ml_b_f": f_bias0 + nrm((N_ML, ML_HEADS), 0.1),
        "ml_out_norm_g": 1.0 + nrm((N_ML, ML_HEADS, ML_DV), 0.02),
        "ml_w_out": nrm((N_ML, ML_V, D), ML_V ** -0.5),
        "ffn_ada_w": nrm((DEPTH, D, 3 * D), ada_scale),
        "ffn_ada_b": nrm((DEPTH, 3 * D), 0.02),
        "ffn_norm_g": 1.0 + nrm((DEPTH, D), 0.02),
        "ffn_w1": nrm((DEPTH, D, D_FF), D ** -0.5),
        "ffn_w3": nrm((DEPTH, D, D_FF), D ** -0.5),
        "ffn_w2": nrm((DEPTH, D_FF, D), D_FF ** -0.5),
        "final_norm_g": 1.0 + nrm((D,), 0.02),
    }
    return inp


def reference(x_prompt, x_sample, c_prompt, c_sample,
              state_gla, state_s5_re, state_s5_im, state_mlstm_C, state_mlstm_n, state_mlstm_m,
              ab_ada_w, ab_ada_b, ab_norm_g, ab_w_in, gla_w_gate, gla_b_gate, gla_norm_g,
              s5_lam_re, s5_lam_im, s5_log_dt, s5_b_re, s5_b_im, s5_c_re, s5_c_im, s5_d,
              s5_w_glu, s5_b_glu, ab_w_out,
              ml_ada_w, ml_ada_b, ml_norm_g, ml_w_in, ml_b_i, ml_b_f, ml_out_norm_g, ml_w_out,
              ffn_ada_w, ffn_ada_b, ffn_norm_g, ffn_w1, ffn_w3, ffn_w2, final_norm_g):
    p = {
        "ab_ada_w": ab_ada_w, "ab_ada_b": ab_ada_b, "ab_norm_g": ab_norm_g, "ab_w_in": ab_w_in,
        "gla_w_gate": gla_w_gate, "gla_b_gate": gla_b_gate, "gla_norm_g": gla_norm_g,
        "s5_lam_re": s5_lam_re, "s5_lam_im": s5_lam_im, "s5_log_dt": s5_log_dt,
        "s5_b_re": s5_b_re, "s5_b_im": s5_b_im, "s5_c_re": s5_c_re, "s5_c_im": s5_c_im, "s5_d": s5_d,
        "s5_w_glu": s5_w_glu, "s5_b_glu": s5_b_glu, "ab_w_out": ab_w_out,
        "ml_ada_w": ml_ada_w, "ml_ada_b": ml_ada_b, "ml_norm_g": ml_norm_g, "ml_w_in": ml_w_in,
        "ml_b_i": ml_b_i, "ml_b_f": ml_b_f, "ml_out_norm_g": ml_out_norm_g, "ml_w_out": ml_w_out,
        "ffn_ada_w": ffn_ada_w, "ffn_ada_b": ffn_ada_b, "ffn_norm_g": ffn_norm_g,
        "ffn_w1": ffn_w1, "ffn_w3": ffn_w3, "ffn_w2": ffn_w2, "final_norm_g": final_norm_g,
    }
    f32 = jnp.float32
    bp = x_prompt.shape[0]
    (y_prompt, gla_p, s5_re_p, s5_im_p, mlstm_C_p, mlstm_n_p, mlstm_m_p) = trunk(
        x_prompt, c_prompt,
        jnp.zeros((N_AB, bp, GLA_HEADS, GLA_DK, GLA_DV), f32),
        jnp.zeros((N_AB, bp, S5_GROUPS, S5_STATE), f32),
        jnp.zeros((N_AB, bp, S5_GROUPS, S5_STATE), f32),
        jnp.zeros((N_ML, bp, ML_HEADS, ML_DV, ML_DQK), f32),
        jnp.zeros((N_ML, bp, ML_HEADS, ML_DQK), f32),
        jnp.zeros((N_ML, bp, ML_HEADS), f32), p)
    (y_sample, gla_s, s5_re_s, s5_im_s, mlstm_C_s, mlstm_n_s, mlstm_m_s) = trunk(
        x_sample, c_sample, state_gla, state_s5_re, state_s5_im,
        state_mlstm_C, state_mlstm_n, state_mlstm_m, p)
    return (y_prompt, y_sample,
            gla_p, s5_re_p, s5_im_p, mlstm_C_p, mlstm_n_p, mlstm_m_p,
            gla_s, s5_re_s, s5_im_s, mlstm_C_s, mlstm_n_s, mlstm_m_s)
```

```python
import types
import numpy as np
from contextlib import ExitStack
import concourse.bass as bass
import concourse.mybir as mybir
from concourse.bass_utils import run_bass_kernel_spmd

F32 = mybir.dt.float32
BF16 = mybir.dt.bfloat16
AF = mybir.ActivationFunctionType
ALU = mybir.AluOpType

D = 2048
DFF = 5632
EPS = 1e-6
EPOCH = 4000
N_DMA_SEMS = 24
SAME_ENGINE_SYNC = True
NBLK = 2
TP = 1024


def _freeze(fn):
    if fn.__closure__ is None:
        return fn
    cells = []
    for c in fn.__closure__:
        try:
            cells.append(types.CellType(c.cell_contents))
        except ValueError:
            cells.append(c)
    return types.FunctionType(fn.__code__, fn.__globals__, fn.__name__, fn.__defaults__, tuple(cells))


class Sched:
    ENGS = ("pe", "act", "dve", "pool", "sp")

    def __init__(self, nc, es):
        self.nc = nc
        self.es = es
        self.prog = {e: [] for e in self.ENGS}
        self.cnt = {e: 0 for e in self.ENGS}
        self.sems = {e: [] for e in self.ENGS}
        self.waited = {e: {} for e in self.ENGS}
        self.last_write = {}
        self.reads = {}
        self.dma_sems = [es.enter_context(nc.semaphore("dma%d" % i)) for i in range(N_DMA_SEMS)]
        self.dma_cnt = [0] * N_DMA_SEMS
        self.dma_rr = 0
        self.dma_rr_pool = 0
        self.n_inst = {e: 0 for e in self.ENGS}

    def _eng_sem(self, eng, epoch):
        lst = self.sems[eng]
        while len(lst) <= epoch:
            lst.append(self.es.enter_context(self.nc.semaphore("s_%s_%d" % (eng, len(lst)))))
        return lst[epoch]

    def _wait(self, eng, dep):
        if dep[0] == 'e':
            _, peng, c = dep
            if peng == eng and (eng == 'pe' or not SAME_ENGINE_SYNC):
                return
            epoch = (c - 1) // EPOCH
            val = (c - 1) % EPOCH + 1
            key = ('e', peng, epoch)
            sem = self._eng_sem(peng, epoch)
        else:
            _, si, val = dep
            key = ('d', si)
            sem = self.dma_sems[si]
        w = self.waited[eng]
        if w.get(key, 0) >= val:
            return
        w[key] = val
        self.prog[eng].append(lambda e, sem=sem, val=val: e.wait_ge(sem, val))

    def _deps(self, eng, reads, writes):
        deps = []
        for r in reads:
            d = self.last_write.get(r)
            if d is not None:
                deps.append(d)
        for w in writes:
            d = self.last_write.get(w)
            if d is not None:
                deps.append(d)
            deps.extend(self.reads.get(w, {}).values())
        for d in deps:
            self._wait(eng, d)

    def _commit(self, myid, reads, writes):
        for r in reads:
            self.reads.setdefault(r, {})[myid[:2]] = myid
        for w in writes:
            self.last_write[w] = myid
            self.reads[w] = {}

    def op(self, eng, fn, reads=(), writes=(), signal=True):
        fn = _freeze(fn)
        self._deps(eng, reads, writes)
        c = self.cnt[eng] + 1
        myid = ('e', eng, c)
        self.n_inst[eng] += 1
        if signal:
            self.cnt[eng] = c
            sem = self._eng_sem(eng, (c - 1) // EPOCH)
            self.prog[eng].append(lambda e, fn=fn, sem=sem: fn(e).then_inc(sem, 1))
        else:
            self.prog[eng].append(lambda e, fn=fn: fn(e))
        self._commit(myid, reads, writes)
        return myid

    def dma(self, q, out, in_, reads=(), writes=(), **kw):
        half = N_DMA_SEMS // 2
        if q == 'pool':
            si = half + self.dma_rr_pool
            self.dma_rr_pool = (self.dma_rr_pool + 1) % half
        else:
            si = self.dma_rr
            self.dma_rr = (self.dma_rr + 1) % half
        if self.dma_cnt[si] > 0:
            self._wait(q, ('d', si, 16 * self.dma_cnt[si]))
        self._deps(q, reads, writes)
        self.dma_cnt[si] += 1
        val = 16 * self.dma_cnt[si]
        sem = self.dma_sems[si]
        myid = ('d', si, val)
        self.n_inst[q] += 1
        self.prog[q].append(lambda e, out=out, in_=in_, sem=sem, kw=kw: e.dma_start(out=out, in_=in_, **kw).then_inc(sem, 16))
        self._commit(myid, reads, writes)
        return myid

    def barrier(self, pool=True):
        for eng in (("pe", "act", "dve", "pool", "sp") if pool else ("pe", "act", "dve", "sp")):
            for peng in ("pe", "act", "dve", "pool"):
                if self.cnt[peng] > 0:
                    self._wait(eng, ('e', peng, self.cnt[peng]))
            for si in range(N_DMA_SEMS):
                if self.dma_cnt[si] > 0:
                    self._wait(eng, ('d', si, 16 * self.dma_cnt[si]))

    def finish(self):
        for si in range(N_DMA_SEMS):
            if self.dma_cnt[si] > 0:
                self._wait('sp', ('d', si, 16 * self.dma_cnt[si]))
        nc = self.nc
        with nc.Block() as block:
            @block.tensor
            def _(e):
                for f in self.prog['pe']:
                    f(e)

            @block.scalar
            def _(e):
                for f in self.prog['act']:
                    f(e)

            @block.vector
            def _(e):
                for f in self.prog['dve']:
                    f(e)

            @block.gpsimd
            def _(e):
                for f in self.prog['pool']:
                    f(e)

            @block.sync
            def _(e):
                for f in self.prog['sp']:
                    f(e)


def _col_layout():
    off = {}
    n = 0

    def add(name, k):
        nonlocal n
        off[name] = (n, k)
        n += k
    for s in range(4):
        add("adab%d" % s, 48)
        add("normg%d" % s, 16)
    add("glabg", 4)
    add("glang", 8)
    add("s5d", 8)
    add("s5bglu", 8)
    add("mlng", 16)
    add("mlbi", 1)
    add("mlbf", 1)
    add("lamre", 32)
    add("lamim", 32)
    add("logdt", 32)
    return off, n


COLOFF, NCOLS = _col_layout()


def K3(x):
    return [x, x + 'r', x + 'i']


def _fm(v):
    v = np.asarray(v, np.float32).reshape(-1, 128)
    return np.ascontiguousarray(v.T)


class K:
    def __init__(self, with_mixers=True, dbg=None):
        self.with_mixers = with_mixers
        self.dbg = dbg or {}
        self.nc = bass.Bass("TRN2", target_bir_lowering=False)
        self.es = ExitStack()

    def dram_in(self, name, shape, dt=F32):
        return self.nc.dram_tensor(name, list(shape), dt, kind="ExternalInput").ap()

    def dram_out(self, name, shape, dt=F32):
        return self.nc.dram_tensor(name, list(shape), dt, kind="ExternalOutput").ap()

    def sb(self, name, shape, dt):
        return self.es.enter_context(self.nc.sbuf_tensor(name, list(shape), dt))

    def build(self):
        nc = self.nc
        with self.es:
            self._decl()
            self.S = Sched(nc, self.es)
            self._setup()
            for blk in range(self.dbg.get('nblk', NBLK)):
                self._block(blk)
            self.S.finish()
        return nc

    def _decl(self):
        di, do = self.dram_in, self.dram_out
        self.xp = di("xp", [NBLK * TP, D])
        self.xs = di("xs", [128, D])
        self.crep = di("crep", [256, D])
        self.cols_d = di("cols", [128, NCOLS])
        self.gfin_d = di("gfin", [128, D])
        self.cst_d = di("cst", [128, 1040])
        self.ada_w = [di("ada_w%d" % s, [D, 3 * D]) for s in range(4)]
        self.ffn_w1 = [di("ffn_w1_%d" % l, [D, DFF]) for l in range(2)]
        self.ffn_w3 = [di("ffn_w3_%d" % l, [D, DFF]) for l in range(2)]
        self.ffn_w2 = [di("ffn_w2_%d" % l, [DFF, D]) for l in range(2)]
        self.w_ab_in = di("w_ab_in", [D, 4112])
        self.w_gate = di("w_gate", [16, 512])
        self.w_ab_out = di("w_ab_out", [D, D])
        self.w_glu = di("w_glu", [1024, 1024])
        self.rst_d = di("rst", [128, 1152])
        self.sgla = di("sgla", [16, 4, 128, 256])
        self.o_gla_p = do("o_gla_p", [4, 128, 256])
        self.o_gla_s = do("o_gla_s", [16, 4, 128, 256])
        self.bpad = di("bpad", [32, 2, 128, 128])
        self.cpad = di("cpad", [32, 2, 128, 128])
        self.s5in = di("s5in", [32, 2, 128, 16])
        self.o_s5_p = do("o_s5_p", [2, 128, 32])
        self.o_s5_s = do("o_s5_s", [32, 2, 128, 16])
        self.scr_bbT = self.nc.dram_tensor("scr_bbT", [32, 2, 128, 128], BF16, kind="Internal").ap()
        self.w_ml_in = di("w_ml_in", [D, 6152])
        self.w_ml_out = di("w_ml_out", [D, D])
        self.sC0T = di("sC0T", [16, 4, 256, 512])
        self.sn_in = di("sn_in", [4, 128, 2, 16])
        self.sm_in = di("sm_in", [4, 16])
        self.o_ml_C_p = do("o_ml_C_p", [4, 256, 512])
        self.o_ml_n_p = do("o_ml_n_p", [4, 128, 2])
        self.o_ml_m_p = do("o_ml_m_p", [4, 1])
        self.o_ml_C_s = do("o_ml_C_s", [16, 4, 256, 512])
        self.o_ml_n_s = do("o_ml_n_s", [4, 128, 2, 16])
        self.o_ml_m_s = do("o_ml_m_s", [4, 16])
        self.scr_mlC = self.nc.dram_tensor("scr_mlC", [4, 256, 512], F32, kind="Internal").ap()
        self.scr_mln = self.nc.dram_tensor("scr_mln", [4, 128, 2], F32, kind="Internal").ap()
        self.scr_mod = self.nc.dram_tensor("scr_mod", [4, 128, 8448], F32, kind="Internal").ap()
        self.scr_gla = self.nc.dram_tensor("scr_gla", [4, 128, 256], F32, kind="Internal").ap()
        self.yp = do("yp", [NBLK * TP, D])
        self.ys = do("ys", [128, D])
        self.x = self.sb("x", [128, 9, D], F32)
        self.wb = [self.sb("wb%d" % i, [128, 4096], BF16) for i in range(2)]
        self.wb_rr = 0
        self.wb_cur = [w[:] for w in self.wb]
        self.GT = self.sb("GT", [128, 2, D], F32)
        self.cols = self.sb("colsb", [128, NCOLS], F32)
        self.cst = self.sb("cstb", [128, 1040], F32)
        self.identb = self.sb("identb", [128, 128], BF16)
        self.onesb = self.sb("onesb", [128, 128], BF16)
        self.small = self.sb("small", [128, 64], F32)
        self.rst = self.sb("rstb", [128, 1152], BF16)
        self.sm2 = self.sb("sm2", [128, 64], F32)
        self.sm3 = self.sb("sm3", [128, 128], F32)
        self.PW = self.sb("PW", [128, 3, 15, 32], F32)
        self.S5H = self.sb("S5H", [128, 2, 32], F32)
        self.ARW = 22350
        self.AR = self.sb("arena", [128, self.ARW], F32)
        self.ps = [self.es.enter_context(self.nc.psum_tensor("ps%d" % i, [128, 512], F32)) for i in range(8)]
        self.ps_rr = 0
        self.gen_banks = list(range(8))

    def af(self, off, n):
        return self.AR[:, off:off + n]

    def ab(self, off, n_bf):
        assert n_bf % 2 == 0
        return self.AR[:, off:off + n_bf // 2].bitcast(BF16)

    def bank(self):
        self.ps_rr = (self.ps_rr + 1) % len(self.gen_banks)
        i = self.gen_banks[self.ps_rr]
        return self.ps[i], "ps%d" % i

    def lbank(self, i):
        return self.ps[3 + i], "ps%d" % (3 + i)

    def wbuf(self):
        bufs = self.wb_cur
        self.wb_rr = (self.wb_rr + 1) % len(bufs)
        i = self.wb_rr
        return bufs[i], "wb%d" % i

    def col(self, name, j=0, n=1):
        o, k = COLOFF[name]
        return self.cols[:, o + j:o + j + n]

    def load_w(self, W, r0, kc, c0, ncols):
        buf, key = self.wbuf()
        v = buf[:, 0:kc * ncols].rearrange("p (k n) -> p k n", k=kc)
        self.S.dma('pool', v, W[r0:r0 + kc * 128, c0:c0 + ncols].rearrange("(k p) n -> p k n", p=128), writes=[key])
        return v, key

    def _setup(self):
        S = self.S
        S.dma('sp', self.cols[:], self.cols_d, writes=['cols'])
        S.dma('sp', self.cst[:], self.cst_d, writes=['cst'])
        S.dma('pool', self.rst[:], self.rst_d, writes=['rst'])
        S.op('dve', lambda e: e.tensor_copy(out=self.onesb[:], in_=self.cst[:, 400:528]), reads=['cst'], writes=['onesb'])
        S.op('dve', lambda e: e.tensor_copy(out=self.identb[:], in_=self.cst[:, 0:128]), reads=['cst'], writes=['identb'])
        S.barrier()
        if self.with_mixers and 0 in self.dbg.get('layers', range(2)) and self.dbg.get('s5', True):
            self.s5_setup()

    def _block(self, blk):
        S = self.S
        self.NT = 9 if blk == 0 else 8
        NT = self.NT
        self.T = NT * 128
        for i in range(8):
            S.dma('sp', self.x[:, i, :], self.xp[blk * TP + i * 128: blk * TP + (i + 1) * 128, :], writes=[('x', i)])
        if NT == 9:
            S.dma('sp', self.x[:, 8, :], self.xs, writes=[('x', 8)])
        for layer in self.dbg.get('layers', range(2)):
            if self.with_mixers:
                self.prenorm(2 * layer, blk)
                if layer == 0:
                    self.mixer_ab(blk)
                else:
                    self.mixer_ml(blk)
            if self.dbg.get('ffn', True):
                self.prenorm(2 * layer + 1, blk)
                self.ffn(layer)
        self.final_norm(blk)
        S.barrier(pool=False)

    def groups(self):
        g = [(0, 512), (512, 512)]
        if self.NT == 9:
            g.append((1024, 128))
        return g

    def make_cT(self):
        S = self.S
        o2 = 9216 + 4352
        self.cT = self.ab(18688, 16 * 256).rearrange("p (k t) -> p k t", k=16)
        tmpf = self.af(o2, D)
        tmpb = self.ab(o2 + D, D)
        for t in range(2):
            S.dma('sp', tmpf, self.crep[t * 128:(t + 1) * 128, :], writes=['tmpf'])
            S.op('act', lambda e: e.activation(out=tmpb, in_=tmpf, func=AF.Silu), reads=['tmpf'], writes=['tmpb'])
            for half in range(2):
                pb, pk = self.bank()
                pv = pb[:].bitcast(BF16).rearrange("p (k n) -> p k n", k=8)
                for k in range(8):
                    kk = half * 8 + k
                    S.op('pe', lambda e, k=k, kk=kk, pv=pv: e.transpose(out=pv[:, k, :], in_=tmpb[:, kk * 128:(kk + 1) * 128], identity=self.identb[:]),
                         reads=['tmpb', 'identb'], writes=[pk], signal=(k == 7))
                S.op('dve', lambda e, pv=pv, half=half, t=t: e.tensor_copy(out=self.cT[:, half * 8:(half + 1) * 8, t * 128:(t + 1) * 128], in_=pv),
                     reads=[pk], writes=['cT'])

        S.barrier()

    def prenorm(self, site, blk=0):
        S = self.S
        NT = self.NT
        S.barrier(pool=False)
        self.gen_banks = list(range(8))
        W = self.ada_w[site]
        cached = blk > 0
        if not cached:
            self.make_cT()
        self.hT = self.ab(0, 16 * 1152).rearrange("p (k t) -> p k t", k=16)
        o1 = 9216
        GXg = self.af(o1, 4096).rearrange("p (k t) -> p k t", k=16)
        GX = self.af(o1, 2176).rearrange("p (k t) -> p k t", k=16)
        SHx = self.af(o1 + 2176, 2176).rearrange("p (k t) -> p k t", k=16)
        o2 = o1 + 4352
        xn = [self.ab(o2 + i * 1024, D) for i in range(2)]
        junk = self.ab(o2 + 2048, D)
        adab = lambda j: self.col("adab%d" % site, j)
        ng = lambda j: self.col("normg%d" % site, j)

        if not cached:
            def mod_chunks(jlist, evac, c_lo, ncol):
                for j0 in range(jlist[0], jlist[-1] + 1, 2):
                    wv, wk = self.load_w(W, 0, 16, j0 * 128, 256)
                    for jj in range(2):
                        j = j0 + jj
                        pb, pk = self.bank()
                        for k in range(16):
                            S.op('pe', lambda e, k=k, jj=jj, wv=wv, pb=pb: e.matmul(pb[:, 0:ncol], lhsT=wv[:, k, jj * 128:(jj + 1) * 128], rhs=self.cT[:, k, c_lo:c_lo + ncol], start=(k == 0), stop=(k == 15)),
                                 reads=[wk, 'cT'], writes=[pk], signal=(k == 15))
                        evac(j, pb, pk)

            def ev_gate(j, pb, pk):
                S.op('act', lambda e: e.activation(out=GXg[:, j - 32, :], in_=pb[:, 0:256], func=AF.Identity, bias=adab(j), scale=1.0),
                     reads=[pk, 'cols'], writes=[('GX', j - 32)])
            mod_chunks(list(range(32, 48)), ev_gate, 0, 256)
            identf = self.cst[:, 0:128]
            for g in range(2):
                for q in range(4):
                    pb, pk = self.bank()
                    for kk in range(4):
                        k = q * 4 + kk
                        S.op('pe', lambda e, k=k, kk=kk, pb=pb, g=g: e.transpose(out=pb[:, kk * 128:(kk + 1) * 128], in_=GXg[:, k, g * 128:(g + 1) * 128], identity=identf),
                             reads=[('GX', k), 'cst'], writes=[pk], signal=(kk == 3))
                    S.op('act', lambda e, pb=pb, g=g, q=q: e.copy(out=self.GT[:, g, q * 512:(q + 1) * 512], in_=pb[:]), reads=[pk], writes=[('GT', g)])
            S.barrier(pool=False)

            def ev_shift(j, pb, pk):
                S.op('act', lambda e: e.activation(out=SHx[:, j, :], in_=pb[:, 0:136], func=AF.Identity, bias=adab(j), scale=1.0),
                     reads=[pk, 'cols'], writes=[('SHx', j)])
            mod_chunks(list(range(0, 16)), ev_shift, 120, 136)

            def ev_scale(j, pb, pk):
                jj = j - 16
                S.op('dve', lambda e: e.tensor_scalar(out=GX[:, jj, :], in0=pb[:, 0:136], scalar1=self.small[:, jj:jj + 1], scalar2=ng(jj), op0=ALU.add, op1=ALU.mult),
                     reads=[pk, 'cols', 'small'], writes=[('GX', jj)])
            S.op('dve', lambda e: e.tensor_scalar(out=self.small[:, 0:16], in0=self.col("adab%d" % site, 16, 16), scalar1=1.0, scalar2=None, op0=ALU.add),
                 reads=['cols'], writes=['small'])
            mod_chunks(list(range(16, 32)), ev_scale, 120, 136)
            S.dma('sp', self.scr_mod[site, :, 0:2176], self.af(o1, 2176), reads=[('GX', k) for k in range(16)], writes=['scr_mod'])
            S.dma('sp', self.scr_mod[site, :, 2176:4352], self.af(o1 + 2176, 2176), reads=[('SHx', k) for k in range(16)], writes=['scr_mod'])
            S.dma('sp', self.scr_mod[site, :, 4352:8448], self.GT[:].rearrange("p g d -> p (g d)"), reads=[('GT', 0), ('GT', 1)], writes=['scr_mod'])
        else:
            S.dma('sp', self.af(o1, 2176), self.scr_mod[site, :, 0:2176], reads=['scr_mod'], writes=[('GX', k) for k in range(16)])
            S.dma('sp', self.af(o1 + 2176, 2176), self.scr_mod[site, :, 2176:4352], reads=['scr_mod'], writes=[('SHx', k) for k in range(16)])
            S.dma('sp', self.GT[:].rearrange("p g d -> p (g d)"), self.scr_mod[site, :, 4352:8448], reads=['scr_mod'], writes=[('GT', 0), ('GT', 1)])
        ss = self.small[:, 16:16 + NT]
        rstd = self.small[:, 32:32 + NT]
        for i in range(NT):
            S.op('act', lambda e, i=i: e.activation(out=junk, in_=self.x[:, i, :], func=AF.Square, accum_out=self.small[:, 16 + i:17 + i]),
                 reads=[('x', i)], writes=['mtmp0', 'mtmp1', 'small'])
        S.op('dve', lambda e: e.tensor_scalar(out=rstd, in0=ss, scalar1=1.0 / D, scalar2=EPS, op0=ALU.mult, op1=ALU.add), reads=['small'], writes=['small'])
        S.op('act', lambda e: e.activation(out=rstd, in_=rstd, func=AF.Sqrt), reads=['small'], writes=['small'])
        S.op('dve', lambda e: e.reciprocal(out=rstd, in_=rstd), reads=['small'], writes=['small'])
        for i in range(NT):
            xb = xn[i % 2]
            xk = 'xn%d' % (i % 2)
            S.op('act', lambda e, i=i, xb=xb: e.activation(out=xb, in_=self.x[:, i, :], func=AF.Copy, scale=self.small[:, 32 + i:33 + i]),
                 reads=[('x', i), 'small'], writes=[xk])
            for half in range(2):
                pb, pk = self.bank()
                pv = pb[:].bitcast(BF16).rearrange("p (k n) -> p k n", k=8)
                for k in range(8):
                    kk = half * 8 + k
                    S.op('pe', lambda e, k=k, kk=kk, pv=pv, xb=xb: e.transpose(out=pv[:, k, :], in_=xb[:, kk * 128:(kk + 1) * 128], identity=self.identb[:]),
                         reads=[xk, 'identb'], writes=[pk], signal=(k == 7))
                dst = self.hT[:, half * 8:(half + 1) * 8, i * 128:(i + 1) * 128]
                tmp = self.af(o2 + 3072 + 1024 * half, 1024).rearrange("p (k n) -> p k n", k=8)
                tk = 'mtmp%d' % half
                S.op('dve', lambda e, pv=pv, tmp=tmp, half=half, i=i: e.tensor_tensor(out=tmp, in0=pv, in1=(GX[:, half * 8:(half + 1) * 8, 0:1].to_broadcast([128, 8, 128]) if i < 8 else GX[:, half * 8:(half + 1) * 8, 8:136]), op=ALU.mult),
                     reads=[pk] + [('GX', half * 8 + k) for k in range(8)], writes=[tk])
                S.op('dve', lambda e, dst=dst, tmp=tmp, half=half, i=i: e.tensor_tensor(out=dst, in0=tmp, in1=(SHx[:, half * 8:(half + 1) * 8, 0:1].to_broadcast([128, 8, 128]) if i < 8 else SHx[:, half * 8:(half + 1) * 8, 8:136]), op=ALU.add),
                     reads=[tk] + [('SHx', half * 8 + k) for k in range(8)], writes=[('hT', i)])
        S.barrier()

    def resid(self, i, c0, n, pb, pk):
        S = self.S
        g = 0 if i < 8 else 1
        tmp = self.rtmp[self.rt_rr % 2][:, 0:n]
        tk = 'rtmp%d' % (self.rt_rr % 2)
        self.rt_rr += 1
        S.op('dve', lambda e: e.tensor_tensor(out=tmp, in0=pb[:, 0:n], in1=self.GT[:, g, c0:c0 + n], op=ALU.mult), reads=[pk, ('GT', g)], writes=[tk])
        S.op('dve', lambda e: e.tensor_tensor(out=self.x[:, i, c0:c0 + n], in0=self.x[:, i, c0:c0 + n], in1=tmp, op=ALU.add), reads=[tk, ('x', i)], writes=[('x', i)])

    def ffn(self, layer):
        S = self.S
        NT = self.NT
        T = self.T
        o1 = 9216
        W1, W3, W2 = self.ffn_w1[layer], self.ffn_w3[layer], self.ffn_w2[layer]
        self.rtmp = [self.af(o1 + 4608 + i * 512, 512) for i in range(2)]
        self.rt_rr = 0
        sil = [self.af(o1 + 5632 + i * 512, 512) for i in range(2)]
        self.wb_cur = [w[:] for w in self.wb] + [self.ab(o1 + 6656 + i * 2048, 4096) for i in range(2)]
        pieces = [(0, 8), (8, 8), (16, 8), (24, 8), (32, 8), (40, 4)]
        for (j0, nj) in pieces:
            gT = self.ab(o1, 8 * 1152).rearrange("p (k t) -> p k t", k=8)
            for jp in range(j0, j0 + nj, 2):
                w1v, w1k = self.load_w(W1, 0, 16, jp * 128, 256)
                w3v, w3k = self.load_w(W3, 0, 16, jp * 128, 256)
                for jj in range(2):
                    jl = jp + jj - j0
                    for gi, (t0, tn) in enumerate(self.groups()):
                        pa, pak = self.bank()
                        pc, pck = self.bank()
                        for k in range(16):
                            S.op('pe', lambda e, k=k, jj=jj, w1v=w1v, pa=pa, t0=t0, tn=tn: e.matmul(pa[:, 0:tn], lhsT=w1v[:, k, jj * 128:(jj + 1) * 128], rhs=self.hT[:, k, t0:t0 + tn], start=(k == 0), stop=(k == 15)),
                                 reads=[w1k, 'hTall'], writes=[pak], signal=(k == 15))
                        for k in range(16):
                            S.op('pe', lambda e, k=k, jj=jj, w3v=w3v, pc=pc, t0=t0, tn=tn: e.matmul(pc[:, 0:tn], lhsT=w3v[:, k, jj * 128:(jj + 1) * 128], rhs=self.hT[:, k, t0:t0 + tn], start=(k == 0), stop=(k == 15)),
                                 reads=[w3k, 'hTall'], writes=[pck], signal=(k == 15))
                        sl = sil[self.rt_rr % 2][:, 0:tn]
                        sk = 'sil%d' % (self.rt_rr % 2)
                        self.rt_rr += 1
                        S.op('act', lambda e, sl=sl, pa=pa, tn=tn: e.activation(out=sl, in_=pa[:, 0:tn], func=AF.Silu), reads=[pak], writes=[sk])
                        S.op('dve', lambda e, sl=sl, pc=pc, tn=tn, jl=jl, t0=t0: e.tensor_tensor(out=gT[:, jl, t0:t0 + tn], in0=sl, in1=pc[:, 0:tn], op=ALU.mult),
                             reads=[sk, pck], writes=[('gT', jl)])
            for cp in range(4):
                w2v, w2k = self.load_w(W2, j0 * 128, nj, cp * 512, 512)
                for i in range(NT):
                    pb, pk = self.bank()
                    for k in range(nj):
                        S.op('pe', lambda e, k=k, w2v=w2v, pb=pb, i=i: e.matmul(pb[:, 0:512], lhsT=gT[:, k, i * 128:(i + 1) * 128], rhs=w2v[:, k, :], start=(k == 0), stop=(k == nj - 1)),
                             reads=[w2k] + [('gT', kk) for kk in range(nj)], writes=[pk], signal=(k == nj - 1))
                    self.resid(i, cp * 512, 512, pb, pk)
        S.barrier(pool=False)
        self.wb_cur = [w[:] for w in self.wb]
        self.wb_rr = 0

    def final_norm(self, blk):
        S = self.S
        NT = self.NT
        S.barrier(pool=False)
        gB = self.af(0, D)
        junk = self.ab(D, D)
        yt = [self.af(D + 1024 + i * D, D) for i in range(2)]
        S.dma('sp', gB, self.gfin_d, writes=['gB'])
        ss = self.small[:, 16:16 + NT]
        rstd = self.small[:, 32:32 + NT]
        for i in range(NT):
            S.op('act', lambda e, i=i: e.activation(out=junk, in_=self.x[:, i, :], func=AF.Square, accum_out=self.small[:, 16 + i:17 + i]),
                 reads=[('x', i)], writes=['mtmp0', 'mtmp1', 'small'])
        S.op('dve', lambda e: e.tensor_scalar(out=rstd, in0=ss, scalar1=1.0 / D, scalar2=EPS, op0=ALU.mult, op1=ALU.add), reads=['small'], writes=['small'])
        S.op('act', lambda e: e.activation(out=rstd, in_=rstd, func=AF.Sqrt), reads=['small'], writes=['small'])
        S.op('dve', lambda e: e.reciprocal(out=rstd, in_=rstd), reads=['small'], writes=['small'])
        for i in range(NT):
            y = yt[i % 2]
            yk = 'yt%d' % (i % 2)
            S.op('dve', lambda e, i=i, y=y: e.scalar_tensor_tensor(out=y, in0=self.x[:, i, :], scalar=self.small[:, 32 + i:33 + i], in1=gB, op0=ALU.mult, op1=ALU.mult),
                 reads=[('x', i), 'small', 'gB'], writes=[yk])
            if i < 8:
                S.dma('sp', self.yp[blk * TP + i * 128: blk * TP + (i + 1) * 128, :], y, reads=[yk], writes=['yp'])
            else:
                S.dma('sp', self.ys, y, reads=[yk], writes=['ys'])

    def proj_feat(self, W, c0, M, evac):
        S = self.S
        wv, wk = self.load_w(W, 0, 16, c0, M)
        for (t0, tn) in self.groups():
            pb, pk = self.bank()
            for k in range(16):
                S.op('pe', lambda e, k=k, pb=pb, t0=t0, tn=tn: e.matmul(pb[0:M, 0:tn], lhsT=wv[:, k, 0:M], rhs=self.hT[:, k, t0:t0 + tn], start=(k == 0), stop=(k == 15)),
                     reads=[wk], writes=[pk], signal=(k == 15))
            evac(t0, tn, pb, pk)

    def proj_feat_multi(self, W, c0, evacs):
        S = self.S
        n = len(evacs)
        wv, wk = self.load_w(W, 0, 16, c0, 128 * n)
        for jj in range(n):
            for (t0, tn) in self.groups():
                pb, pk = self.bank()
                for k in range(16):
                    S.op('pe', lambda e, k=k, pb=pb, t0=t0, tn=tn, jj=jj: e.matmul(pb[:, 0:tn], lhsT=wv[:, k, jj * 128:(jj + 1) * 128], rhs=self.hT[:, k, t0:t0 + tn], start=(k == 0), stop=(k == 15)),
                         reads=[wk], writes=[pk], signal=(k == 15))
                evacs[jj](t0, tn, pb, pk)

    def proj_tok(self, W, c0, n, evac):
        S = self.S
        wv, wk = self.load_w(W, 0, 16, c0, n)
        for i in range(self.NT):
            pb, pk = self.bank()
            for k in range(16):
                S.op('pe', lambda e, k=k, pb=pb, i=i: e.matmul(pb[:, 0:n], lhsT=self.hT[:, k, i * 128:(i + 1) * 128], rhs=wv[:, k, :], start=(k == 0), stop=(k == 15)),
                     reads=[wk], writes=[pk], signal=(k == 15))
            evac(i, pb, pk)

    def out_proj(self, W, r0, kc, srcT):
        S = self.S
        for cp in range(8):
            wv, wk = self.load_w(W, r0, kc, cp * 256, 256)
            for i in range(self.NT):
                pb, pk = self.bank()
                for k in range(kc):
                    S.op('pe', lambda e, k=k, pb=pb, i=i, wv=wv: e.matmul(pb[:, 0:256], lhsT=srcT[:, k, i * 128:(i + 1) * 128], rhs=wv[:, k, :], start=(k == 0), stop=(k == kc - 1)),
                         reads=[wk, 'srcT'], writes=[pk], signal=(k == kc - 1))
                self.resid(i, cp * 256, 256, pb, pk)

    def mixer_ab(self, blk):
        S = self.S
        self.gen_banks = [0, 1, 2]
        NT = self.NT
        T = self.T
        Wi = self.w_ab_in
        o = [9216]

        def alloc(n):
            r = o[0]
            o[0] += n
            assert o[0] <= self.ARW, o[0]
            return r
        glrT = self.ab(alloc(576), 1152)
        wgb = self.ab(alloc(256), 512)
        LB = self.af(alloc(1152), 1152)
        NB = self.af(alloc(1152), 1152)
        qk = self.af(alloc(1152), 1152)
        qtil = self.ab(alloc(576), 1152)
        ktil = self.ab(alloc(576), 1152)
        kdT = self.ab(alloc(576), 1152)
        E = [self.af(alloc(128), 128) for _ in range(3)]
        Sbf = [self.ab(alloc(128), 256) for _ in range(4)]
        attm_ = [self.ab(alloc(64), 128) for _ in range(2)]
        kdtok_ = [self.ab(alloc(64), 128) for _ in range(2)]
        km = [self.ab(alloc(64), 128) for _ in range(4)]
        oT_ = [self.af(alloc(256), 256).rearrange("p (c t) -> p c t", c=2) for _ in range(2)]
        sq_ = [self.ab(alloc(128), 256).rearrange("p (c t) -> p c t", c=2) for _ in range(2)]
        rs__ = [self.af(alloc(128), 128) for _ in range(2)]
        t1_ = [self.af(alloc(128), 128) for _ in range(2)]
        self.rtmp = [self.af(alloc(256), 256) for i in range(2)]
        self.rt_rr = 0
        Sin = [self.af(alloc(256), 256) for _ in range(4)]
        Sout = [self.af(alloc(256), 256) for _ in range(4)]
        Sg = self.af(alloc(256), 256)
        vtok = LB.bitcast(BF16)[:, 0:NT * 256].rearrange("p (i v) -> p i v", i=NT)
        rsT = NB.bitcast(BF16).rearrange("p (c t) -> p c t", c=2)
        ogT = qk.bitcast(BF16).rearrange("p (c t) -> p c t", c=2)
        maskP = self.cst[:, 128:256]
        maskS = self.cst[:, 256:384]
        selS = self.cst[:, 384:400]
        onesb = self.onesb[:]
        negbg = self.sm2[:, 32:36]
        ebl = self.sm2[:, 0:8]
        ebls = self.sm2[:, 8:24]
        S.barrier(pool=False)
        S.op('dve', lambda e: e.tensor_scalar(out=negbg, in0=self.col("glabg", 0, 4), scalar1=-1.0, scalar2=None, op0=ALU.mult), reads=['cols'], writes=['negbg'])
        wgf = self.af(alloc(512), 512)
        S.dma('sp', wgf[0:16, :], self.w_gate, writes=['wgf'])
        S.op('dve', lambda e: e.tensor_copy(out=wgb[0:16, :], in_=wgf[0:16, :]), reads=['wgf'], writes=['wgb'])

        def ev_glr(t0, tn, pb, pk):
            S.op('act', lambda e: e.copy(out=glrT[0:16, t0:t0 + tn], in_=pb[0:16, 0:tn]), reads=[pk], writes=['glrT'])
        self.proj_feat(Wi, 3072, 16, ev_glr)
        for h in self.dbg.get('heads', range(4)):
            for (t0, tn) in self.groups():
                pb, pk = self.bank()
                S.op('pe', lambda e, pb=pb, t0=t0, tn=tn: e.matmul(pb[:, 0:tn], lhsT=wgb[0:16, h * 128:(h + 1) * 128], rhs=glrT[0:16, t0:t0 + tn], start=True, stop=True),
                     reads=['wgb', 'glrT'], writes=[pk])
                S.op('act', lambda e, pb=pb, t0=t0, tn=tn: e.activation(out=LB[:, t0:t0 + tn], in_=pb[:, 0:tn], func=AF.Exp, bias=negbg[:, h:h + 1], scale=-1.0),
                     reads=[pk, 'negbg'], writes=['LB'])
            S.op('act', lambda e: e.activation(out=LB[:, 0:T], in_=LB[:, 0:T], func=AF.Ln, bias=1.0, scale=1.0), reads=['LB'], writes=['LB'])
            S.op('dve', lambda e: e.tensor_tensor_scan(out=NB[:, 0:T], data0=self.rst[:, 0:T], data1=LB[:, 0:T], initial=0.0, op0=ALU.mult, op1=ALU.add),
                 reads=['LB', 'rst'], writes=['NB'])
            S.op('dve', lambda e: e.tensor_scalar(out=LB[:, 0:T], in0=NB[:, 0:T], scalar1=1.0 / 16, scalar2=None, op0=ALU.mult), reads=['NB'], writes=['LB'])
            S.op('dve', lambda e: e.tensor_scalar(out=NB[:, 0:T], in0=NB[:, 0:T], scalar1=-1.0 / 16, scalar2=None, op0=ALU.mult), reads=['NB'], writes=['NB'])

            def etab(which, c, Eb, ek):
                sl = slice(c * 128, (c + 1) * 128)
                if c < 8:
                    if which == 'q':
                        src, bias = NB, (LB[:, c * 128 - 1:c * 128] if c > 0 else 0.0)
                    elif which == 'k':
                        src, bias = LB, (NB[:, c * 128 - 1:c * 128] if c > 0 else 0.0)
                    else:
                        src, bias = LB, NB[:, c * 128 + 127:c * 128 + 128]
                    S.op('act', lambda e: e.activation(out=Eb, in_=src[:, sl], func=AF.Exp, bias=bias, scale=1.0), reads=['LB', 'NB'], writes=[ek])
                else:
                    if which == 'q':
                        S.op('act', lambda e: e.activation(out=Eb, in_=NB[:, sl], func=AF.Exp), reads=['NB'], writes=[ek])
                    elif which == 'k':
                        S.op('act', lambda e: e.activation(out=Eb, in_=LB[:, sl], func=AF.Exp), reads=['LB'], writes=[ek])
                    else:
                        v3 = lambda a: a.rearrange("p (j i) -> p j i", i=8)
                        S.op('dve', lambda e: e.tensor_tensor(out=v3(Eb), in0=v3(LB[:, sl]), in1=v3(NB[:, sl])[:, :, 7:8].to_broadcast([128, 16, 8]), op=ALU.add),
                             reads=['LB', 'NB'], writes=[ek])
                        S.op('act', lambda e: e.activation(out=Eb, in_=Eb, func=AF.Exp), reads=[ek], writes=[ek])

            def ev_q(t0, tn, pb, pk):
                S.op('act', lambda e: e.activation(out=qk[:, t0:t0 + tn], in_=pb[:, 0:tn], func=AF.Copy, scale=128 ** -0.5), reads=[pk], writes=['qk'])
            self.proj_feat(Wi, h * 128, 128, ev_q)
            for c in range(NT):
                sl = slice(c * 128, (c + 1) * 128)
                Eb, ek = E[c % 3], 'E%d' % (c % 3)
                etab('q', c, Eb, ek)
                S.op('dve', lambda e, sl=sl, Eb=Eb: e.tensor_tensor(out=qtil[:, sl], in0=qk[:, sl], in1=Eb, op=ALU.mult), reads=['qk', ek], writes=['qtil'])
                if c < 8:
                    S.op('dve', lambda e, c=c, Eb=Eb: e.tensor_copy(out=ebl[:, c:c + 1], in_=Eb[:, 127:128]), reads=[ek], writes=['ebl'])
                else:
                    S.op('dve', lambda e, Eb=Eb: e.tensor_copy(out=ebls, in_=Eb.rearrange("p (j i) -> p j i", i=8)[:, :, 7]), reads=[ek], writes=['ebl'])

            def ev_k(t0, tn, pb, pk):
                S.op('act', lambda e: e.copy(out=qk[:, t0:t0 + tn], in_=pb[:, 0:tn]), reads=[pk, 'qtil'], writes=['qk'])
            self.proj_feat(Wi, 512 + h * 128, 128, ev_k)
            for c in range(NT):
                sl = slice(c * 128, (c + 1) * 128)
                Eb, ek = E[c % 3], 'E%d' % (c % 3)
                etab('k', c, Eb, ek)
                S.op('dve', lambda e, sl=sl, Eb=Eb: e.tensor_tensor(out=ktil[:, sl], in0=qk[:, sl], in1=Eb, op=ALU.mult), reads=['qk', ek], writes=['ktil'])
                Eb, ek = E[(c + 1) % 3], 'E%d' % ((c + 1) % 3)
                etab('d', c, Eb, ek)
                S.op('dve', lambda e, sl=sl, Eb=Eb: e.tensor_tensor(out=kdT[:, sl], in0=qk[:, sl], in1=Eb, op=ALU.mult), reads=['qk', ek], writes=['kdT'])
            S.barrier(pool=False)

            def ev_v(i, pb, pk):
                S.op('act', lambda e: e.copy(out=vtok[:, i, :], in_=pb[:, 0:256]), reads=[pk], writes=['vtok'])
            self.proj_tok(Wi, 1024 + h * 256, 256, ev_v)
            evs = []
            for vc in range(2):
                def ev_r(t0, tn, pb, pk, vc=vc):
                    S.op('act', lambda e: e.activation(out=rsT[:, vc, t0:t0 + tn], in_=pb[:, 0:tn], func=AF.Silu), reads=[pk], writes=['rsT'])
                evs.append(ev_r)
            self.proj_feat_multi(Wi, 2048 + h * 256, evs)
            S.barrier(pool=False)
            if blk == 0:
                S.op('dve', lambda e: e.memset(Sg, 0.0), writes=['Sg'])
            else:
                S.dma('sp', Sg, self.scr_gla[h], reads=['scr_gla'], writes=['Sg'])
            for c in range(NT):
                sl = slice(c * 128, (c + 1) * 128)
                cp_ = c % 2
                attm, kdtok, oT, sq, rs_, t1 = attm_[cp_], kdtok_[cp_], oT_[cp_], sq_[cp_], rs__[cp_], t1_[cp_]
                X = lambda nm: nm + str(cp_)
                pa, pak = self.bank()
                S.op('pe', lambda e, pa=pa, sl=sl: e.matmul(pa[:, 0:128], lhsT=ktil[:, sl], rhs=qtil[:, sl], start=True, stop=True), reads=['ktil', 'qtil'], writes=[pak])
                S.op('dve', lambda e, pa=pa, c=c: e.tensor_tensor(out=attm, in0=pa[:, 0:128], in1=(maskP if c < 8 else maskS), op=ALU.mult), reads=[pak, 'cst'], writes=[X('attm')])
                pt, ptk = self.bank()
                ptv = pt[:].bitcast(BF16)
                S.op('pe', lambda e, ptv=ptv, sl=sl: e.transpose(out=ptv[:, 0:128], in_=kdT[:, sl], identity=self.identb[:]), reads=['kdT', 'identb'], writes=[ptk])
                S.op('act', lambda e, ptv=ptv: e.copy(out=kdtok, in_=ptv[:, 0:128]), reads=[ptk], writes=[X('kdtok')])
                po = [self.lbank(2 * cp_), self.lbank(2 * cp_ + 1)]
                if c < 8:
                    S.op('act', lambda e: e.copy(out=Sbf[0], in_=Sg), reads=['Sg'], writes=['Sbf0'])
                    for vc in range(2):
                        pb, pk = po[vc]
                        S.op('pe', lambda e, pb=pb, vc=vc, c=c: e.matmul(pb[:, 0:128], lhsT=vtok[:, c, vc * 128:(vc + 1) * 128], rhs=attm, start=True, stop=False),
                             reads=['vtok', X('attm')], writes=[pk], signal=False)
                        S.op('pe', lambda e, pb=pb, vc=vc, sl=sl: e.matmul(pb[:, 0:128], lhsT=Sbf[0][:, vc * 128:(vc + 1) * 128], rhs=qtil[:, sl], start=False, stop=True),
                             reads=['Sbf0', 'qtil'], writes=[pk])
                    pS, pSk = self.bank()
                    S.op('pe', lambda e, pS=pS, c=c: e.matmul(pS[:, 0:256], lhsT=kdtok, rhs=vtok[:, c, :], start=True, stop=True), reads=[X('kdtok'), 'vtok'], writes=[pSk])
                    S.op('dve', lambda e, pS=pS, c=c: e.scalar_tensor_tensor(out=Sg, in0=Sg, scalar=ebl[:, c:c + 1], in1=pS[:, 0:256], op0=ALU.mult, op1=ALU.add),
                         reads=[pSk, 'ebl', 'Sg', 'Sbf0'], writes=['Sg'])
                else:
                    for vc in range(2):
                        pb, pk = po[vc]
                        S.op('pe', lambda e, pb=pb, vc=vc, c=c: e.matmul(pb[:, 0:128], lhsT=vtok[:, c, vc * 128:(vc + 1) * 128], rhs=attm, start=True, stop=False),
                             reads=['vtok', X('attm')], writes=[pk], signal=False)
                    for j in range(16):
                        r = j % 4
                        S.dma('sp', Sin[r], self.sgla[j, h], writes=['Sin%d' % r])
                        S.op('act', lambda e, r=r: e.copy(out=Sbf[r], in_=Sin[r]), reads=['Sin%d' % r], writes=['Sbf%d' % r])
                        for vc in range(2):
                            pb, pk = po[vc]
                            S.op('pe', lambda e, pb=pb, vc=vc, j=j, r=r: e.matmul(pb[:, 8 * j:8 * j + 8], lhsT=Sbf[r][:, vc * 128:(vc + 1) * 128], rhs=qtil[:, 1024 + 8 * j:1032 + 8 * j], start=False, stop=(j == 15)),
                                 reads=['Sbf%d' % r, 'qtil'], writes=[pk], signal=True)
                        S.op('dve', lambda e, j=j, r=r: e.tensor_scalar(out=km[r], in0=kdtok, scalar1=selS[:, j:j + 1], scalar2=None, op0=ALU.mult), reads=[X('kdtok'), 'cst'], writes=['km%d' % r])
                        pS, pSk = self.bank()
                        S.op('pe', lambda e, pS=pS, r=r, c=c: e.matmul(pS[:, 0:256], lhsT=km[r], rhs=vtok[:, c, :], start=True, stop=True), reads=['km%d' % r, 'vtok'], writes=[pSk])
                        S.op('dve', lambda e, pS=pS, j=j, r=r: e.scalar_tensor_tensor(out=Sout[r], in0=Sin[r], scalar=ebls[:, j:j + 1], in1=pS[:, 0:256], op0=ALU.mult, op1=ALU.add),
                             reads=[pSk, 'ebl', 'Sin%d' % r], writes=['Sout%d' % r])
                        S.dma('sp', self.o_gla_s[j, h], Sout[r], reads=['Sout%d' % r], writes=['o_gla_s'])
                pss, pssk = self.bank()
                for vc in range(2):
                    pb, pk = po[vc]
                    S.op('act', lambda e, pb=pb, vc=vc: e.copy(out=oT[:, vc, :], in_=pb[:, 0:128]), reads=[pk], writes=[X('oT%d' % vc)])
                    S.op('act', lambda e, vc=vc: e.activation(out=sq[:, vc, :], in_=oT[:, vc, :], func=AF.Square), reads=[X('oT%d' % vc)], writes=[X('sq%d' % vc)])
                    S.op('pe', lambda e, pss=pss, vc=vc: e.matmul(pss[:, 0:128], lhsT=onesb, rhs=sq[:, vc, :], start=(vc == 0), stop=(vc == 1)), reads=[X('sq%d' % vc), 'onesb'], writes=[pssk], signal=(vc == 1))
                S.op('dve', lambda e, pss=pss: e.tensor_scalar(out=rs_, in0=pss[:, 0:128], scalar1=1.0 / 256, scalar2=EPS, op0=ALU.mult, op1=ALU.add), reads=[pssk], writes=[X('rs_')])
                S.op('act', lambda e: e.activation(out=rs_, in_=rs_, func=AF.Sqrt), reads=[X('rs_')], writes=[X('rs_')])
                S.op('dve', lambda e: e.reciprocal(out=rs_, in_=rs_), reads=[X('rs_')], writes=[X('rs_')])
                for vc in range(2):
                    S.op('dve', lambda e, vc=vc: e.scalar_tensor_tensor(out=t1, in0=oT[:, vc, :], scalar=self.col("glang", 2 * h + vc), in1=rs_, op0=ALU.mult, op1=ALU.mult),
                         reads=[X('oT%d' % vc), X('rs_'), 'cols'], writes=[X('t1')])
                    S.op('dve', lambda e, vc=vc, sl=sl: e.tensor_tensor(out=ogT[:, vc, sl], in0=t1, in1=rsT[:, vc, sl], op=ALU.mult), reads=[X('t1'), 'rsT'], writes=['srcT'])
            if blk == self.dbg.get('nblk', NBLK) - 1:
                S.dma('sp', self.o_gla_p[h], Sg, reads=['Sg'], writes=['o_gla_p'])
            else:
                S.dma('sp', self.scr_gla[h], Sg, reads=['Sg'], writes=['scr_gla'])
            self.out_proj(self.w_ab_out, h * 256, 2, ogT)
            S.barrier(pool=False)
        if self.dbg.get('s5', True):
            self.s5(blk)

    def s5_setup(self):
        S = self.S
        PW = self.PW
        o = [0]

        def alloc(n):
            r = o[0]
            o[0] += n
            return r
        T_ = lambda: self.af(alloc(32), 32)
        dt, mag, th, tw, sn, cs, t0, t1, t2, fre, fim = [T_() for _ in range(11)]
        lr = self.col("lamre", 0, 32)
        li = self.col("lamim", 0, 32)
        PI = float(np.pi)
        n = [0]

        def dv(fn, r, w):
            S.op('dve', fn, reads=r, writes=w)

        def ac(fn, r, w):
            S.op('act', fn, reads=r, writes=w)
        ac(lambda e: e.activation(out=dt, in_=self.col("logdt", 0, 32), func=AF.Exp), ['cols'], ['dt'])
        dv(lambda e: e.tensor_tensor(out=t0, in0=lr, in1=dt, op=ALU.mult), ['dt', 'cols'], ['t0'])
        ac(lambda e: e.activation(out=mag, in_=t0, func=AF.Exp), ['t0'], ['mag'])
        dv(lambda e: e.tensor_tensor(out=th, in0=li, in1=dt, op=ALU.mult), ['dt', 'cols'], ['th'])

        def wrapped_sin(dst, shift, key):
            dv(lambda e: e.tensor_scalar(out=tw, in0=th, scalar1=shift, scalar2=None, op0=ALU.add), ['th'], ['tw'])
            for _ in range(4):
                dv(lambda e: e.tensor_scalar(out=t1, in0=tw, scalar1=PI, scalar2=-2 * PI, op0=ALU.is_gt, op1=ALU.mult), ['tw'], ['t1'])
                dv(lambda e: e.tensor_tensor(out=tw, in0=tw, in1=t1, op=ALU.add), ['t1', 'tw'], ['tw'])
            ac(lambda e: e.activation(out=dst, in_=tw, func=AF.Sin), ['tw'], [key])
        wrapped_sin(sn, 0.0, 'sn')
        wrapped_sin(cs, PI / 2, 'cs')
        are, aim, nim = PW[:, 0, 0, :], PW[:, 1, 0, :], PW[:, 2, 0, :]
        dv(lambda e: e.tensor_tensor(out=are, in0=mag, in1=cs, op=ALU.mult), ['mag', 'cs'], ['PW'])
        dv(lambda e: e.tensor_tensor(out=aim, in0=mag, in1=sn, op=ALU.mult), ['mag', 'sn'], ['PW'])
        dv(lambda e: e.tensor_tensor(out=t0, in0=lr, in1=lr, op=ALU.mult), ['cols'], ['t0'])
        dv(lambda e: e.tensor_tensor(out=t1, in0=li, in1=li, op=ALU.mult), ['cols'], ['t1'])
        dv(lambda e: e.tensor_tensor(out=t0, in0=t0, in1=t1, op=ALU.add), ['t0', 't1'], ['t0'])
        dv(lambda e: e.reciprocal(out=t0, in_=t0), ['t0'], ['t0'])
        dv(lambda e: e.tensor_scalar(out=t1, in0=are, scalar1=-1.0, scalar2=None, op0=ALU.add), ['PW'], ['t1'])
        dv(lambda e: e.tensor_tensor(out=fre, in0=t1, in1=lr, op=ALU.mult), ['t1', 'cols'], ['fre'])
        dv(lambda e: e.tensor_tensor(out=t2, in0=aim, in1=li, op=ALU.mult), ['PW', 'cols'], ['t2'])
        dv(lambda e: e.tensor_tensor(out=fre, in0=fre, in1=t2, op=ALU.add), ['fre', 't2'], ['fre'])
        dv(lambda e: e.tensor_tensor(out=fre, in0=fre, in1=t0, op=ALU.mult), ['fre', 't0'], ['fre'])
        dv(lambda e: e.tensor_tensor(out=fim, in0=aim, in1=lr, op=ALU.mult), ['PW', 'cols'], ['fim'])
        dv(lambda e: e.tensor_tensor(out=t2, in0=t1, in1=li, op=ALU.mult), ['t1', 'cols'], ['t2'])
        dv(lambda e: e.tensor_tensor(out=fim, in0=fim, in1=t2, op=ALU.subtract), ['fim', 't2'], ['fim'])
        dv(lambda e: e.tensor_tensor(out=fim, in0=fim, in1=t0, op=ALU.mult), ['fim', 't0'], ['fim'])

        def cmul(dst, x, y):
            xr, xi = PW[:, 0, x, :], PW[:, 1, x, :]
            yr, yi = PW[:, 0, y, :], PW[:, 1, y, :]
            dr, di_ = PW[:, 0, dst, :], PW[:, 1, dst, :]
            dv(lambda e: e.tensor_tensor(out=t0, in0=xr, in1=yr, op=ALU.mult), ['PW'], ['t0'])
            dv(lambda e: e.tensor_tensor(out=t1, in0=xi, in1=yi, op=ALU.mult), ['PW'], ['t1'])
            dv(lambda e: e.tensor_tensor(out=t2, in0=xr, in1=yi, op=ALU.mult), ['PW'], ['t2'])
            dv(lambda e: e.tensor_tensor(out=tw, in0=xi, in1=yr, op=ALU.mult), ['PW'], ['tw'])
            dv(lambda e: e.tensor_tensor(out=dr, in0=t0, in1=t1, op=ALU.subtract), ['t0', 't1'], ['PW'])
            dv(lambda e: e.tensor_tensor(out=di_, in0=t2, in1=tw, op=ALU.add), ['t2', 'tw'], ['PW'])
        for k in range(1, 8):
            cmul(k, k - 1, 0)
        for k in range(8, 15):
            cmul(k, k - 1 if k > 8 else 7, k - 1 if k > 8 else 7)
        dv(lambda e: e.tensor_scalar(out=PW[:, 2, :, :], in0=PW[:, 1, :, :], scalar1=-1.0, scalar2=None, op0=ALU.mult), ['PW'], ['PW'])
        dv(lambda e: e.memset(self.S5H[:], 0.0), [], K3('S5H'))
        bre = [self.af(alloc(128), 128) for _ in range(2)]
        bim = [self.af(alloc(128), 128) for _ in range(2)]
        o1_ = [self.af(alloc(128), 128) for _ in range(2)]
        ob = [self.ab(alloc(128), 256) for _ in range(2)]
        obT = [self.ab(alloc(128), 256) for _ in range(2)]
        for q in range(32):
            r = q % 2
            S.dma('sp', bre[r], self.bpad[q, 0], writes=['bre%d' % r])
            S.dma('sp', bim[r], self.bpad[q, 1], writes=['bim%d' % r])
            fr, fi = fre[:, q:q + 1], fim[:, q:q + 1]
            dv(lambda e: e.tensor_scalar(out=o1_[r], in0=bim[r], scalar1=fi, scalar2=-1.0, op0=ALU.mult, op1=ALU.mult), ['bim%d' % r, 'fim'], ['o1_%d' % r])
            dv(lambda e: e.scalar_tensor_tensor(out=ob[r][:, 0:128], in0=bre[r], scalar=fr, in1=o1_[r], op0=ALU.mult, op1=ALU.add), ['bre%d' % r, 'o1_%d' % r, 'fre'], ['ob%d' % r])
            dv(lambda e: e.tensor_scalar(out=o1_[r], in0=bre[r], scalar1=fi, scalar2=None, op0=ALU.mult), ['bre%d' % r, 'fim', 'ob%d' % r], ['o1_%d' % r])
            dv(lambda e: e.scalar_tensor_tensor(out=ob[r][:, 128:256], in0=bim[r], scalar=fr, in1=o1_[r], op0=ALU.mult, op1=ALU.add), ['bim%d' % r, 'o1_%d' % r, 'fre'], ['ob%d' % r])
            pt, ptk = self.bank()
            ptv = pt[:].bitcast(BF16)
            for c in range(2):
                S.op('pe', lambda e: e.transpose(out=ptv[:, c * 128:(c + 1) * 128], in_=ob[r][:, c * 128:(c + 1) * 128], identity=self.identb[:]), reads=['ob%d' % r, 'identb'], writes=[ptk], signal=(c == 1))
            S.op('act', lambda e: e.copy(out=obT[r], in_=ptv[:, 0:256]), reads=[ptk], writes=['obT%d' % r])
            S.dma('sp', self.scr_bbT[q].rearrange("c p n -> p c n"), obT[r].rearrange("p (c n) -> p c n", c=2), reads=['obT%d' % r], writes=['scr_bbT'])
        S.barrier()

    def s5(self, blk):
        S = self.S
        NT = self.NT
        T = self.T
        NC = T // 8
        PW = self.PW
        last = (blk == self.dbg.get('nblk', NBLK) - 1)
        S.barrier()
        uT = self.ab(9216, 8 * 1152).rearrange("p (c t) -> p c t", c=8)
        ygT = self.ab(9216 + 4608, 8 * 1152).rearrange("p (c t) -> p c t", c=8)
        yy = [self.af(9216 + 9216 + i * 512, 512) for i in range(2)]
        y2 = [self.af(9216 + 9216 + 1024 + i * 512, 512) for i in range(2)]
        self.rtmp = [self.af(9216 + 9216 + 2048 + i * 256, 256) for i in range(2)]
        self.rt_rr = 0
        for b2 in range(0, 8, 2):
            evs = []
            for b in (b2, b2 + 1):
                def ev_u(t0, tn, pb, pk, b=b):
                    S.op('act', lambda e: e.copy(out=uT[:, b, t0:t0 + tn], in_=pb[:, 0:tn]), reads=[pk], writes=['uT'])
                evs.append(ev_u)
            self.proj_feat_multi(self.w_ab_in, 3088 + b2 * 128, evs)
        S.barrier()
        o = [0]

        def alloc(n):
            r = o[0]
            o[0] += n
            assert o[0] <= 9216, o[0]
            return r
        groups = self.groups()
        H3 = lambda a: a[:, 0:T].rearrange("p (c i) -> p c i", i=8)
        B = []
        for sl_ in range(2):
            d = {}
            d['Hre'] = self.af(alloc(1152), 1152)
            d['Him'] = self.af(alloc(1152), 1152)
            d['Hb'] = [self.ab(alloc(576), 1152) for _ in range(2)]
            d['G'] = [[self.af(alloc(128), 128) for _ in range(2)] for _ in range(2)]
            d['hin'] = [self.af(alloc(144), 144) for _ in range(2)]
            d['bbT'] = self.ab(alloc(128), 256).rearrange("p (c n) -> p c n", c=2)
            d['cp'] = self.ab(alloc(128), 256).rearrange("p (c n) -> p c n", c=2)
            d['FS'] = self.af(alloc(32), 32).rearrange("p (c j) -> p c j", c=2)
            d['hr'], d['hi'] = H3(d['Hre']), H3(d['Him'])
            d['s'] = str(sl_)
            B.append(d)

        def cmac2(items):
            ops = [[], [], [], []]
            for (dre, dim, sre, sim, k, key_s, key_d, q) in items:
                a_r, a_i, n_i = PW[:, 0, k, q:q + 1], PW[:, 1, k, q:q + 1], PW[:, 2, k, q:q + 1]
                kr, ki = key_d + 'r', key_d + 'i'
                ops[0].append((lambda e, dre=dre, sre=sre, a_r=a_r: e.scalar_tensor_tensor(out=dre, in0=sre, scalar=a_r, in1=dre, op0=ALU.mult, op1=ALU.add), K3(key_s) + [key_d, kr, 'PW'], [kr]))
                ops[1].append((lambda e, dim=dim, sim=sim, a_r=a_r: e.scalar_tensor_tensor(out=dim, in0=sim, scalar=a_r, in1=dim, op0=ALU.mult, op1=ALU.add), K3(key_s) + [key_d, ki, 'PW'], [ki]))
                ops[2].append((lambda e, dre=dre, sim=sim, n_i=n_i: e.scalar_tensor_tensor(out=dre, in0=sim, scalar=n_i, in1=dre, op0=ALU.mult, op1=ALU.add), K3(key_s) + [key_d, kr, 'PW'], [kr]))
                ops[3].append((lambda e, dim=dim, sre=sre, a_i=a_i: e.scalar_tensor_tensor(out=dim, in0=sre, scalar=a_i, in1=dim, op0=ALU.mult, op1=ALU.add), K3(key_s) + [key_d, ki, 'PW'], [ki]))
            for grp in ops:
                for (fn, r_, w_) in grp:
                    S.op('dve', fn, reads=r_, writes=w_)

        for qq in range(0, 32, 2):
            P = [(qq + j, B[j]) for j in range(2)]
            b = qq // 4
            for q, d in P:
                sk = d['s']
                S.dma('sp', d['bbT'], self.scr_bbT[q].rearrange("c p n -> p c n"), reads=['scr_bbT'], writes=['bbT' + sk])
                S.dma('pool', d['cp'], self.cpad[q].rearrange("c p n -> p c n"), writes=['cp' + sk])
                if NT == 9:
                    S.dma('sp', d['hin'][0][:, 128:144], self.s5in[q, 0], writes=K3('hin' + sk))
                    S.dma('sp', d['hin'][1][:, 128:144], self.s5in[q, 1], writes=K3('hin' + sk))
            for q, d in P:
                sk = d['s']
                for (t0, tn) in groups:
                    for c, Hx in ((0, d['Hre']), (1, d['Him'])):
                        pb, pk = self.bank()
                        S.op('pe', lambda e: e.matmul(pb[:, 0:tn], lhsT=d['bbT'][:, c, :], rhs=uT[:, b, t0:t0 + tn], start=True, stop=True), reads=['bbT' + sk, 'uT'], writes=[pk])
                        S.op('act', lambda e: e.copy(out=Hx[:, t0:t0 + tn], in_=pb[:, 0:tn]), reads=[pk], writes=K3('H' + sk))
            for i in range(1, 8):
                cmac2([(d['hr'][:, :, i], d['hi'][:, :, i], d['hr'][:, :, i - 1], d['hi'][:, :, i - 1], 0, 'H' + d['s'], 'H' + d['s'], q) for q, d in P])
            for q, d in P:
                sk = d['s']
                g0 = d['G'][0]
                S.op('act', lambda e: e.copy(out=g0[0], in_=d['hr'][:, 0:128, 7]), reads=K3('H' + sk), writes=K3('G0' + sk))
                S.op('act', lambda e: e.copy(out=g0[1], in_=d['hi'][:, 0:128, 7]), reads=K3('H' + sk), writes=K3('G0' + sk))
            if blk > 0:
                cmac2([(d['G'][0][0][:, 0:1], d['G'][0][1][:, 0:1], self.S5H[:, 0, q:q + 1], self.S5H[:, 1, q:q + 1], 7, 'S5H', 'G0' + d['s'], q) for q, d in P])
            cur = 0
            for k in range(7):
                s_ = 1 << k
                for q, d in P:
                    sk = d['s']
                    src, dst = d['G'][cur], d['G'][1 - cur]
                    S.op('act', lambda e: e.copy(out=dst[0], in_=src[0]), reads=K3('G%d' % cur + sk), writes=K3('G%d' % (1 - cur) + sk))
                    S.op('act', lambda e: e.copy(out=dst[1], in_=src[1]), reads=K3('G%d' % cur + sk), writes=K3('G%d' % (1 - cur) + sk))
                cmac2([(d['G'][1 - cur][0][:, s_:128], d['G'][1 - cur][1][:, s_:128], d['G'][cur][0][:, 0:128 - s_], d['G'][cur][1][:, 0:128 - s_], 7 + k, 'G%d' % cur + d['s'], 'G%d' % (1 - cur) + d['s'], q) for q, d in P])
                cur = 1 - cur
            for q, d in P:
                sk = d['s']
                Sf = d['G'][cur]
                kf = 'G%d' % cur + sk
                hin = d['hin']
                hp_re, hp_im = self.S5H[:, 0, q:q + 1], self.S5H[:, 1, q:q + 1]
                S.op('act', lambda e: e.copy(out=hin[0][:, 0:1], in_=hp_re), reads=K3('S5H'), writes=K3('hin' + sk))
                S.op('act', lambda e: e.copy(out=hin[1][:, 0:1], in_=hp_im), reads=K3('S5H'), writes=K3('hin' + sk))
                S.op('act', lambda e: e.copy(out=hin[0][:, 1:128], in_=Sf[0][:, 0:127]), reads=K3(kf), writes=K3('hin' + sk))
                S.op('act', lambda e: e.copy(out=hin[1][:, 1:128], in_=Sf[1][:, 0:127]), reads=K3(kf), writes=K3('hin' + sk))
                S.op('act', lambda e: e.copy(out=hp_re, in_=Sf[0][:, 127:128]), reads=K3(kf) + K3('hin' + sk), writes=K3('S5H'))
                S.op('act', lambda e: e.copy(out=hp_im, in_=Sf[1][:, 127:128]), reads=K3(kf) + K3('hin' + sk), writes=K3('S5H'))
            for i in range(8):
                cmac2([(d['hr'][:, :, i], d['hi'][:, :, i], d['hin'][0][:, 0:NC], d['hin'][1][:, 0:NC], i, 'hin' + d['s'], 'H' + d['s'], q) for q, d in P])
            for q, d in P:
                sk = d['s']
                pb4 = q % 4
                if NT == 9:
                    S.op('act', lambda e: e.copy(out=d['FS'][:, 0, :], in_=d['hr'][:, 128:144, 7]), reads=K3('H' + sk), writes=['FS' + sk])
                    S.op('act', lambda e: e.copy(out=d['FS'][:, 1, :], in_=d['hi'][:, 128:144, 7]), reads=K3('H' + sk), writes=['FS' + sk])
                    S.dma('sp', self.o_s5_s[q].rearrange("c p j -> p c j"), d['FS'], reads=['FS' + sk], writes=['o_s5_s'])
                S.op('act', lambda e: e.copy(out=d['Hb'][0][:, 0:T], in_=d['Hre'][:, 0:T]), reads=K3('H' + sk), writes=['Hb' + sk])
                S.op('act', lambda e: e.activation(out=d['Hb'][1][:, 0:T], in_=d['Him'][:, 0:T], func=AF.Copy, scale=-1.0), reads=K3('H' + sk), writes=['Hb' + sk])
                for gi, (t0, tn) in enumerate(groups):
                    lb, lk = self.lbank(gi)
                    S.op('pe', lambda e: e.matmul(lb[:, 0:tn], lhsT=d['cp'][:, 0, :], rhs=d['Hb'][0][:, t0:t0 + tn], start=(pb4 == 0), stop=False), reads=['cp' + sk, 'Hb' + sk], writes=[lk], signal=False)
                    S.op('pe', lambda e: e.matmul(lb[:, 0:tn], lhsT=d['cp'][:, 1, :], rhs=d['Hb'][1][:, t0:t0 + tn], start=False, stop=(pb4 == 3)), reads=['cp' + sk, 'Hb' + sk], writes=[lk])
            if P[1][0] % 4 == 3:
                for gi, (t0, tn) in enumerate(groups):
                    lb, lk = self.lbank(gi)
                    y_, y2_ = yy[gi % 2][:, 0:tn], y2[gi % 2][:, 0:tn]
                    ky, ky2 = 'yy%d' % (gi % 2), 'y2%d' % (gi % 2)
                    S.op('dve', lambda e: e.scalar_tensor_tensor(out=y_, in0=uT[:, b, t0:t0 + tn], scalar=self.col("s5d", b), in1=lb[:, 0:tn], op0=ALU.mult, op1=ALU.add), reads=[lk, 'uT', 'cols'], writes=[ky])
                    S.op('dve', lambda e: e.tensor_tensor(out=y2_, in0=y_, in1=y_, op=ALU.mult), reads=[ky], writes=[ky2])
                    S.op('dve', lambda e: e.tensor_scalar(out=y2_, in0=y2_, scalar1=0.044715, scalar2=1.0, op0=ALU.mult, op1=ALU.add), reads=[ky2], writes=[ky2])
                    S.op('dve', lambda e: e.tensor_tensor(out=y2_, in0=y2_, in1=y_, op=ALU.mult), reads=[ky, ky2], writes=[ky2])
                    S.op('act', lambda e: e.activation(out=y2_, in_=y2_, func=AF.Sigmoid, scale=1.5957691216057308), reads=[ky2], writes=[ky2])
                    S.op('dve', lambda e: e.tensor_tensor(out=ygT[:, b, t0:t0 + tn], in0=y_, in1=y2_, op=ALU.mult), reads=[ky, ky2], writes=['ygT'])
        if last:
            S.dma('sp', self.o_s5_p.rearrange("c p q -> p c q"), self.S5H[:], reads=K3('S5H'), writes=['o_s5_p'])
        S.barrier()
        for oc in range(8):
            wv, wk = self.load_w(self.w_glu, 0, 8, oc * 128, 128)
            for gi, (t0, tn) in enumerate(groups):
                pb, pk = self.bank()
                for k in range(8):
                    S.op('pe', lambda e: e.matmul(pb[:, 0:tn], lhsT=wv[:, k, :], rhs=ygT[:, k, t0:t0 + tn], start=(k == 0), stop=(k == 7)), reads=[wk, 'ygT'], writes=[pk], signal=(k == 7))
                y2_ = y2[gi % 2][:, 0:tn]
                ky2 = 'y2%d' % (gi % 2)
                S.op('act', lambda e: e.activation(out=y2_, in_=pb[:, 0:tn], func=AF.Sigmoid, bias=self.col("s5bglu", oc), scale=1.0), reads=[pk, 'cols'], writes=[ky2])
                S.op('dve', lambda e: e.tensor_tensor(out=uT[:, oc, t0:t0 + tn], in0=ygT[:, oc, t0:t0 + tn], in1=y2_, op=ALU.mult), reads=[ky2, 'ygT'], writes=['srcT'])
        self.out_proj(self.w_ab_out, 1024, 8, uT)
        S.barrier()

    def mixer_ml(self, blk):
        S = self.S
        self.gen_banks = [0, 1, 2]
        NT = self.NT
        T = self.T
        W = self.w_ml_in
        last = (blk == self.dbg.get('nblk', NBLK) - 1)
        o = [9216]

        def alloc(n):
            r = o[0]
            o[0] += n
            assert o[0] <= self.ARW, o[0]
            return r
        E1 = self.af(alloc(1152), 1152)
        A0 = alloc(2304)
        ipT = self.af(A0, 1152)
        cum = self.af(A0 + 1152, 1152)
        qT = self.ab(A0, 2 * 1152).rearrange("p (c t) -> p c t", c=2)
        kT = self.ab(A0 + 1152, 2 * 1152).rearrange("p (c t) -> p c t", c=2)
        vtok = self.ab(alloc(NT * 256), NT * 512).rearrange("p (i v) -> p i v", i=NT)
        osT = self.ab(alloc(2304), 4 * 1152).rearrange("p (c t) -> p c t", c=4)
        CT = self.af(alloc(1024), 1024).rearrange("p (c v) -> p c v", c=2)
        CTbf = self.ab(alloc(512), 1024).rearrange("p (c v) -> p c v", c=2)
        E2C = self.af(alloc(36), 36).rearrange("p (c h) -> p c h", h=4)
        E3C = self.af(alloc(36), 36).rearrange("p (c h) -> p c h", h=4)
        U = [self.af(alloc(128), 128) for _ in range(6)]
        Wm = self.af(alloc(128), 128)
        attm = self.ab(alloc(64), 128)
        qtl = self.ab(alloc(128), 256).rearrange("p (c t) -> p c t", c=2)
        kdtok = self.ab(alloc(128), 256)
        nrep = self.ab(alloc(128), 256).rearrange("p (c t) -> p c t", c=2)
        rden = self.af(alloc(128), 128)
        sq = self.ab(alloc(256), 512).rearrange("p (c t) -> p c t", c=4)
        rs_ = self.af(alloc(128), 128)
        t1 = self.af(alloc(128), 128)
        Cin = self.af(alloc(512), 512)
        Cout = self.af(alloc(512), 512)
        Cbf = self.ab(alloc(256), 512)
        km = self.ab(alloc(64), 128)
        N0 = self.af(alloc(32), 32).rearrange("p (c j) -> p c j", c=2)
        NOUT = self.af(alloc(32), 32).rearrange("p (c j) -> p c j", c=2)
        colS = self.af(alloc(32), 32)
        self.rtmp = [Cin[:, 0:256], Cin[:, 256:512]]
        self.rt_rr = 0
        maskP = self.cst[:, 128:256]
        maskS = self.cst[:, 256:384]
        selS = self.cst[:, 384:400]
        onesb = self.onesb[:]
        id4 = self.cst[0:4, 0:4]
        R = lambda a: a[0:4, :]
        mrun = self.sm3[0:4, 0:1]
        MS = self.sm3[0:4, 1:9]
        BLS = self.sm3[0:4, 9:17]
        M0 = self.sm3[0:4, 17:33]
        MFS = self.sm3[0:4, 33:49]
        DEC = self.sm3[0:4, 49:65]
        EM0 = self.sm3[0:4, 65:81]
        tt = self.sm3[0:4, 81:97]
        EMF = self.sm3[0:4, 97:98]
        ncol = self.sm3[:, 98:100]
        emfcol = self.sm3[:, 100:101]
        negbf = self.sm3[0:4, 101:102]
        S.barrier(pool=False)
        S.op('dve', lambda e: e.tensor_scalar(out=negbf, in0=self.col("mlbf")[0:4, :], scalar1=-1.0, scalar2=None, op0=ALU.mult), reads=['cols'], writes=['negbf'])
        if blk == 0:
            S.op('dve', lambda e: e.memset(mrun, 0.0), writes=['mrun'])

        def ev_i(t0, tn, pb, pk):
            S.op('act', lambda e: e.activation(out=ipT[0:4, t0:t0 + tn], in_=pb[0:4, 0:tn], func=AF.Identity, bias=self.col("mlbi")[0:4, :], scale=1.0), reads=[pk, 'cols'], writes=['ipT'])
        self.proj_feat(W, 6144, 4, ev_i)

        def ev_f(t0, tn, pb, pk):
            S.op('act', lambda e: e.activation(out=E1[0:4, t0:t0 + tn], in_=pb[0:4, 0:tn], func=AF.Exp, bias=negbf, scale=-1.0), reads=[pk, 'negbf'], writes=['E1'])
        self.proj_feat(W, 6148, 4, ev_f)
        S.op('act', lambda e: e.activation(out=E1[0:4, 0:T], in_=E1[0:4, 0:T], func=AF.Ln, bias=1.0, scale=1.0), reads=['E1'], writes=['E1'])
        S.op('dve', lambda e: e.tensor_tensor_scan(out=cum[0:4, 0:T], data0=self.rst[0:4, 0:T], data1=E1[0:4, 0:T], initial=0.0, op0=ALU.mult, op1=ALU.add),
             reads=['E1', 'rst'], writes=['cum'])
        if NT == 9:
            S.dma('sp', M0, self.sm_in, writes=['M0'])
        for c in range(NT):
            sl = slice(c * 128, (c + 1) * 128)
            u1, u2, u3, e2, e3, ux = [R(a) for a in U]
            v3 = lambda a: a.rearrange("p (j i) -> p j i", i=8)
            if c < 8:
                if c == 0:
                    S.op('dve', lambda e: e.tensor_copy(out=u1, in_=cum[0:4, sl]), reads=['cum'], writes=['u1'])
                else:
                    S.op('dve', lambda e, sl=sl, c=c: e.tensor_scalar(out=u1, in0=cum[0:4, sl], scalar1=cum[0:4, c * 128 - 1:c * 128], scalar2=None, op0=ALU.subtract), reads=['cum'], writes=['u1'])
                S.op('dve', lambda e, sl=sl: e.tensor_tensor(out=u2, in0=ipT[0:4, sl], in1=u1, op=ALU.add), reads=['ipT', 'u1'], writes=['u2'])
                S.op('dve', lambda e: e.tensor_scalar(out=u3, in0=u2, scalar1=u1[:, 127:128], scalar2=None, op0=ALU.subtract), reads=['u2', 'u1'], writes=['u3'])
                S.op('dve', lambda e, c=c: e.tensor_reduce(out=MS[:, c:c + 1], in_=u3, axis=mybir.AxisListType.X, op=ALU.max), reads=['u3'], writes=['MS'])
                S.op('dve', lambda e, c=c: e.tensor_copy(out=BLS[:, c:c + 1], in_=u1[:, 127:128]), reads=['u1'], writes=['BLS'])
                S.op('dve', lambda e, c=c: e.tensor_tensor(out=tt[:, 0:1], in0=mrun, in1=BLS[:, c:c + 1], op=ALU.subtract), reads=['mrun', 'BLS'], writes=['tt'])
                S.op('dve', lambda e, c=c: e.tensor_tensor(out=mrun, in0=tt[:, 0:1], in1=MS[:, c:c + 1], op=ALU.max), reads=['tt', 'MS'], writes=['mrun'])
            else:
                S.op('dve', lambda e, sl=sl: e.tensor_copy(out=u1, in_=cum[0:4, sl]), reads=['cum'], writes=['u1'])
                S.op('dve', lambda e, sl=sl: e.tensor_tensor(out=u2, in0=ipT[0:4, sl], in1=u1, op=ALU.add), reads=['ipT', 'u1'], writes=['u2'])
                S.op('dve', lambda e: e.tensor_tensor(out=v3(u3), in0=v3(u2), in1=v3(u1)[:, :, 7:8].to_broadcast([4, 16, 8]), op=ALU.subtract), reads=['u2', 'u1'], writes=['u3'])
                S.op('dve', lambda e: e.tensor_reduce(out=MFS, in_=v3(u3), axis=mybir.AxisListType.X, op=ALU.max), reads=['u3'], writes=['MFS'])
                S.op('dve', lambda e: e.tensor_tensor(out=tt, in0=M0, in1=v3(u1)[:, :, 7], op=ALU.subtract), reads=['M0', 'u1'], writes=['tt'])
                S.op('dve', lambda e: e.tensor_tensor(out=MFS, in0=MFS, in1=tt, op=ALU.max), reads=['tt', 'MFS'], writes=['MFS'])
                S.op('dve', lambda e: e.tensor_tensor(out=tt, in0=tt, in1=MFS, op=ALU.subtract), reads=['tt', 'MFS'], writes=['tt'])
                S.op('act', lambda e: e.activation(out=DEC, in_=tt, func=AF.Exp), reads=['tt'], writes=['DEC'])
                S.op('act', lambda e: e.activation(out=EM0, in_=M0, func=AF.Exp), reads=['M0'], writes=['EM0'])
                S.op('dve', lambda e: e.tensor_tensor(out=v3(u3), in0=v3(u3), in1=MFS.rearrange("p (j i) -> p j i", i=1).to_broadcast([4, 16, 8]), op=ALU.subtract), reads=['u3', 'MFS'], writes=['u3'])
                S.dma('sp', self.o_ml_m_s, MFS, reads=['MFS'], writes=['o_ml_m_s'])
            S.op('act', lambda e, sl=sl: e.activation(out=E1[0:4, sl], in_=u1, func=AF.Exp, scale=-1.0), reads=['u1'], writes=['E1'])
            S.op('act', lambda e: e.activation(out=e2, in_=u2, func=AF.Exp), reads=['u2'], writes=['e2'])
            S.op('act', lambda e: e.activation(out=e3, in_=u3, func=AF.Exp), reads=['u3'], writes=['e3'])
            for (src, dst, nm) in ((e2, E2C, 'e2'), (e3, E3C, 'e3')):
                pb, pk = self.bank()
                S.op('pe', lambda e, pb=pb, src=src: e.matmul(pb[:, 0:4], lhsT=src, rhs=id4, start=True, stop=True), reads=[nm, 'cst'], writes=[pk])
                S.op('act', lambda e, pb=pb, dst=dst, c=c: e.copy(out=dst[:, c, :], in_=pb[:, 0:4]), reads=[pk], writes=['EC'])
        if last:
            S.op('act', lambda e: e.activation(out=EMF, in_=mrun, func=AF.Exp, scale=-1.0), reads=['mrun'], writes=['EMF'])
            S.dma('sp', self.o_ml_m_p, mrun, reads=['mrun'], writes=['o_ml_m_p'])
        S.barrier(pool=False)

        for h in self.dbg.get('heads', range(4)):
            selh = self.cst[0:4, 528 + h * 128:528 + (h + 1) * 128]
            evq, evk = [], []
            for dc in range(2):
                def ev_q(t0, tn, pb, pk, dc=dc):
                    S.op('act', lambda e: e.copy(out=qT[:, dc, t0:t0 + tn], in_=pb[:, 0:tn]), reads=[pk], writes=['qT'])
                evq.append(ev_q)

                def ev_k(t0, tn, pb, pk, dc=dc):
                    S.op('act', lambda e: e.activation(out=kT[:, dc, t0:t0 + tn], in_=pb[:, 0:tn], func=AF.Copy, scale=1.0 / 16), reads=[pk], writes=['kT'])
                evk.append(ev_k)
            self.proj_feat_multi(W, h * 256, evq)
            self.proj_feat_multi(W, 1024 + h * 256, evk)
            for half in range(2):
                def ev_v(i, pb, pk, half=half):
                    S.op('act', lambda e: e.copy(out=vtok[:, i, half * 256:(half + 1) * 256], in_=pb[:, 0:256]), reads=[pk], writes=['vtok'])
                self.proj_tok(W, 2048 + h * 512 + half * 256, 256, ev_v)
            for v2 in range(0, 4, 2):
                evs = []
                for vc in (v2, v2 + 1):
                    def ev_o(t0, tn, pb, pk, vc=vc):
                        S.op('act', lambda e: e.activation(out=osT[:, vc, t0:t0 + tn], in_=pb[:, 0:tn], func=AF.Sigmoid), reads=[pk], writes=['srcT'])
                    evs.append(ev_o)
                self.proj_feat_multi(W, 4096 + h * 512 + v2 * 128, evs)
            if blk == 0:
                S.op('dve', lambda e: e.memset(CT[:, 0, :], 0.0), writes=['CT'])
                S.op('dve', lambda e: e.memset(CT[:, 1, :], 0.0), writes=['CT'])
                S.op('dve', lambda e: e.memset(ncol, 0.0), writes=['ncol'])
            else:
                S.dma('sp', CT, self.scr_mlC[h].rearrange("(c p) v -> p c v", p=128), reads=['scr_mlC'], writes=['CT'])
                S.dma('sp', ncol, self.scr_mln[h], reads=['scr_mln'], writes=['ncol'])
            if NT == 9:
                S.dma('sp', N0, self.sn_in[h], writes=['N0'])
                for (rows, c0) in ((DEC, 0), (EM0, 16)):
                    pb, pk = self.bank()
                    S.op('pe', lambda e, pb=pb, rows=rows: e.matmul(pb[:, 0:16], lhsT=selh, rhs=rows, start=True, stop=True), reads=['DEC', 'EM0', 'cst'], writes=[pk])
                    S.op('act', lambda e, pb=pb, c0=c0: e.copy(out=colS[:, c0:c0 + 16], in_=pb[:, 0:16]), reads=[pk], writes=['colS'])
            Wm_, attm_, qtl_, kdtok_, rden_ = Wm, attm, qtl, kdtok, rden
            alt = (U[0], U[4].bitcast(BF16)[:, 0:128], U[1].bitcast(BF16).rearrange("p (c t) -> p c t", c=2), U[2].bitcast(BF16), U[3])
            for c in range(NT):
                sl = slice(c * 128, (c + 1) * 128)
                cp_ = c % 2
                Wm, attm, qtl, kdtok, rden = (Wm_, attm_, qtl_, kdtok_, rden_) if cp_ == 0 else alt
                X = lambda nm: nm + str(cp_)
                per, perk = self.bank()
                S.op('pe', lambda e, per=per, sl=sl: e.matmul(per[:, 0:128], lhsT=selh, rhs=E1[0:4, sl], start=True, stop=True), reads=['E1', 'cst'], writes=[perk])
                S.op('dve', lambda e, per=per, c=c: e.tensor_tensor(out=Wm, in0=per[:, 0:128], in1=(maskP if c < 8 else maskS), op=ALU.mult), reads=[perk, 'cst'], writes=[X('Wm')])
                for dc in range(2):
                    S.op('dve', lambda e, per=per, dc=dc, sl=sl: e.tensor_tensor(out=qtl[:, dc, :], in0=per[:, 0:128], in1=qT[:, dc, sl], op=ALU.mult), reads=[perk, 'qT'], writes=[X('qtl')])
                pa, pak = self.bank()
                for dc in range(2):
                    S.op('pe', lambda e, pa=pa, dc=dc, sl=sl: e.matmul(pa[:, 0:128], lhsT=kT[:, dc, sl], rhs=qT[:, dc, sl], start=(dc == 0), stop=(dc == 1)), reads=['kT', 'qT'], writes=[pak], signal=(dc == 1))
                S.op('dve', lambda e, pa=pa, c=c: e.scalar_tensor_tensor(out=attm, in0=pa[:, 0:128], scalar=E2C[:, c, h:h + 1], in1=Wm, op0=ALU.mult, op1=ALU.mult), reads=[pak, 'EC', X('Wm')], writes=[X('attm')])
                pt, ptk = self.bank()
                ptv = pt[:].bitcast(BF16)
                for dc in range(2):
                    S.op('pe', lambda e, ptv=ptv, dc=dc, sl=sl: e.transpose(out=ptv[:, dc * 128:(dc + 1) * 128], in_=kT[:, dc, sl], identity=self.identb[:]), reads=['kT', 'identb'], writes=[ptk], signal=(dc == 1))
                S.op('act', lambda e, ptv=ptv, c=c: e.activation(out=kdtok, in_=ptv[:, 0:256], func=AF.Copy, scale=E3C[:, c, h:h + 1]), reads=[ptk, 'EC'], writes=[X('kdtok')])
                po = [self.lbank(i) for i in range(4)]
                pd, pdk = self.lbank(4)
                for vc in range(4):
                    pb, pk = po[vc]
                    S.op('pe', lambda e, pb=pb, vc=vc, c=c: e.matmul(pb[:, 0:128], lhsT=vtok[:, c, vc * 128:(vc + 1) * 128], rhs=attm, start=True, stop=False), reads=['vtok', X('attm')], writes=[pk], signal=False)
                S.op('pe', lambda e, pd=pd: e.matmul(pd[:, 0:128], lhsT=onesb, rhs=attm, start=True, stop=False), reads=[X('attm'), 'onesb'], writes=[pdk], signal=False)
                if c < 8:
                    S.op('act', lambda e: e.copy(out=CTbf, in_=CT), reads=['CT'], writes=['CTbf'])
                    for dc in range(2):
                        S.op('dve', lambda e, dc=dc: e.tensor_scalar(out=nrep[:, dc, :], in0=onesb, scalar1=ncol[:, dc:dc + 1], scalar2=None, op0=ALU.mult), reads=['ncol', 'onesb'], writes=['nrep'])
                    for dc in range(2):
                        for vc in range(4):
                            pb, pk = po[vc]
                            S.op('pe', lambda e, pb=pb, vc=vc, dc=dc: e.matmul(pb[:, 0:128], lhsT=CTbf[:, dc, vc * 128:(vc + 1) * 128], rhs=qtl[:, dc, :], start=False, stop=(dc == 1)), reads=['CTbf', X('qtl')], writes=[pk], signal=(dc == 1))
                        S.op('pe', lambda e, pd=pd, dc=dc: e.matmul(pd[:, 0:128], lhsT=nrep[:, dc, :], rhs=qtl[:, dc, :], start=False, stop=(dc == 1)), reads=['nrep', X('qtl')], writes=[pdk], signal=(dc == 1))
                    deccol = Wm[:, 127:128]
                    for dc in range(2):
                        pS, pSk = self.bank()
                        S.op('pe', lambda e, pS=pS, dc=dc, c=c: e.matmul(pS[:, 0:512], lhsT=kdtok[:, dc * 128:(dc + 1) * 128], rhs=vtok[:, c, :], start=True, stop=True), reads=[X('kdtok'), 'vtok'], writes=[pSk])
                        S.op('dve', lambda e, pS=pS, dc=dc: e.scalar_tensor_tensor(out=CT[:, dc, :], in0=CT[:, dc, :], scalar=deccol, in1=pS[:, 0:512], op0=ALU.mult, op1=ALU.add), reads=[pSk, X('Wm'), 'CT', 'CTbf'], writes=['CT'])
                        pn, pnk = self.bank()
                        S.op('pe', lambda e, pn=pn, dc=dc: e.matmul(pn[:, 0:2], lhsT=kdtok[:, dc * 128:(dc + 1) * 128], rhs=onesb[:, 0:2], start=True, stop=True), reads=[X('kdtok'), 'onesb'], writes=[pnk])
                        S.op('dve', lambda e, pn=pn, dc=dc: e.scalar_tensor_tensor(out=ncol[:, dc:dc + 1], in0=ncol[:, dc:dc + 1], scalar=deccol, in1=pn[:, 0:1], op0=ALU.mult, op1=ALU.add), reads=[pnk, X('Wm'), 'ncol', 'nrep'], writes=['ncol'])
                else:
                    for j in range(16):
                        for dc in range(2):
                            S.dma('sp', Cin, self.sC0T[j, h, dc * 128:(dc + 1) * 128, :], writes=['Cin'])
                            S.op('act', lambda e, j=j: e.activation(out=Cbf, in_=Cin, func=AF.Copy, scale=colS[:, 16 + j:17 + j]), reads=['Cin', 'colS'], writes=['Cbf'])
                            S.op('dve', lambda e, j=j, dc=dc: e.tensor_scalar(out=nrep[:, dc, :], in0=onesb, scalar1=N0[:, dc, j:j + 1], scalar2=colS[:, 16 + j:17 + j], op0=ALU.mult, op1=ALU.mult), reads=['N0', 'colS', 'onesb'], writes=['nrep'])
                            fin = (j == 15 and dc == 1)
                            for vc in range(4):
                                pb, pk = po[vc]
                                S.op('pe', lambda e, pb=pb, vc=vc, dc=dc, j=j, fin=fin: e.matmul(pb[:, 8 * j:8 * j + 8], lhsT=Cbf[:, vc * 128:(vc + 1) * 128], rhs=qtl[:, dc, 8 * j:8 * j + 8], start=False, stop=fin), reads=['Cbf', X('qtl')], writes=[pk], signal=(vc == 3))
                            S.op('pe', lambda e, pd=pd, dc=dc, j=j, fin=fin: e.matmul(pd[:, 8 * j:8 * j + 8], lhsT=nrep[:, dc, :], rhs=qtl[:, dc, 8 * j:8 * j + 8], start=False, stop=fin), reads=['nrep', X('qtl')], writes=[pdk])
                            S.op('dve', lambda e, j=j, dc=dc: e.tensor_scalar(out=km, in0=kdtok[:, dc * 128:(dc + 1) * 128], scalar1=selS[:, j:j + 1], scalar2=None, op0=ALU.mult), reads=[X('kdtok'), 'cst'], writes=['km'])
                            pS, pSk = self.bank()
                            S.op('pe', lambda e, pS=pS, c=c: e.matmul(pS[:, 0:512], lhsT=km, rhs=vtok[:, c, :], start=True, stop=True), reads=['km', 'vtok'], writes=[pSk])
                            S.op('dve', lambda e, pS=pS, j=j: e.scalar_tensor_tensor(out=Cout, in0=Cin, scalar=colS[:, j:j + 1], in1=pS[:, 0:512], op0=ALU.mult, op1=ALU.add), reads=[pSk, 'colS', 'Cin', 'Cbf'], writes=['Cout'])
                            S.dma('sp', self.o_ml_C_s[j, h, dc * 128:(dc + 1) * 128, :], Cout, reads=['Cout'], writes=['o_ml_C_s'])
                            pn, pnk = self.bank()
                            S.op('pe', lambda e, pn=pn: e.matmul(pn[:, 0:2], lhsT=km, rhs=onesb[:, 0:2], start=True, stop=True), reads=['km', 'onesb'], writes=[pnk])
                            S.op('dve', lambda e, pn=pn, dc=dc, j=j: e.scalar_tensor_tensor(out=NOUT[:, dc, j:j + 1], in0=N0[:, dc, j:j + 1], scalar=colS[:, j:j + 1], in1=pn[:, 0:1], op0=ALU.mult, op1=ALU.add), reads=[pnk, 'colS', 'N0'], writes=['NOUT'])
                    S.dma('sp', self.o_ml_n_s[h], NOUT, reads=['NOUT'], writes=['o_ml_n_s'])
                S.op('act', lambda e, pd=pd: e.activation(out=rden, in_=pd[:, 0:128], func=AF.Abs), reads=[pdk], writes=[X('rden')])
                S.op('dve', lambda e: e.tensor_scalar(out=rden, in0=rden, scalar1=1.0, scalar2=None, op0=ALU.max), reads=[X('rden')], writes=[X('rden')])
                S.op('dve', lambda e: e.reciprocal(out=rden, in_=rden), reads=[X('rden')], writes=[X('rden')])
                pss, pssk = self.bank()
                for vc in range(4):
                    pb, pk = po[vc]
                    S.op('act', lambda e, pb=pb, vc=vc: e.activation(out=sq[:, vc, :], in_=pb[:, 0:128], func=AF.Square), reads=[pk], writes=['sq%d' % vc])
                    S.op('pe', lambda e, pss=pss, vc=vc: e.matmul(pss[:, 0:128], lhsT=onesb, rhs=sq[:, vc, :], start=(vc == 0), stop=(vc == 3)), reads=['sq%d' % vc, 'onesb'], writes=[pssk], signal=(vc == 3))
                S.op('dve', lambda e, pss=pss: e.tensor_tensor(out=rs_, in0=pss[:, 0:128], in1=rden, op=ALU.mult), reads=[pssk, X('rden')], writes=['rs_'])
                S.op('dve', lambda e: e.scalar_tensor_tensor(out=rs_, in0=rs_, scalar=1.0 / 512, in1=rden, op0=ALU.mult, op1=ALU.mult), reads=['rs_', X('rden')], writes=['rs_'])
                S.op('dve', lambda e: e.tensor_scalar(out=rs_, in0=rs_, scalar1=EPS, scalar2=None, op0=ALU.add), reads=['rs_'], writes=['rs_'])
                S.op('act', lambda e: e.activation(out=rs_, in_=rs_, func=AF.Sqrt), reads=['rs_'], writes=['rs_'])
                S.op('dve', lambda e: e.reciprocal(out=rs_, in_=rs_), reads=['rs_'], writes=['rs_'])
                S.op('dve', lambda e: e.tensor_tensor(out=rs_, in0=rs_, in1=rden, op=ALU.mult), reads=['rs_', X('rden')], writes=['rs_'])
                for vc in range(4):
                    pb, pk = po[vc]
                    S.op('dve', lambda e, pb=pb, vc=vc: e.scalar_tensor_tensor(out=t1, in0=pb[:, 0:128], scalar=self.col("mlng", 4 * h + vc), in1=rs_, op0=ALU.mult, op1=ALU.mult), reads=[pk, 'rs_', 'cols'], writes=['t1'])
                    S.op('dve', lambda e, vc=vc, sl=sl: e.tensor_tensor(out=osT[:, vc, sl], in0=t1, in1=osT[:, vc, sl], op=ALU.mult), reads=['t1', 'srcT'], writes=['srcT'])
            if last:
                pb, pk = self.bank()
                S.op('pe', lambda e, pb=pb: e.matmul(pb[:, 0:2], lhsT=selh, rhs=self.sm3[0:4, 96:98], start=True, stop=True), reads=['EMF', 'cst'], writes=[pk])
                S.op('act', lambda e, pb=pb: e.copy(out=emfcol, in_=pb[:, 1:2]), reads=[pk], writes=['emfcol'])
                S.op('dve', lambda e: e.tensor_scalar(out=CT, in0=CT, scalar1=emfcol, scalar2=None, op0=ALU.mult), reads=['CT', 'emfcol'], writes=['CT'])
                S.op('dve', lambda e: e.tensor_scalar(out=ncol, in0=ncol, scalar1=emfcol, scalar2=None, op0=ALU.mult), reads=['ncol', 'emfcol'], writes=['ncol'])
                S.dma('sp', self.o_ml_C_p[h].rearrange("(c p) v -> p c v", p=128), CT, reads=['CT'], writes=['o_ml_C_p'])
                S.dma('sp', self.o_ml_n_p[h], ncol, reads=['ncol'], writes=['o_ml_n_p'])
            else:
                S.dma('sp', self.scr_mlC[h].rearrange("(c p) v -> p c v", p=128), CT, reads=['CT'], writes=['scr_mlC'])
                S.dma('sp', self.scr_mln[h], ncol, reads=['ncol'], writes=['scr_mln'])
            self.out_proj(self.w_ml_out, h * 512, 4, osT)
            S.barrier(pool=False)


_NC_CACHE = {}


def _consts():
    c = np.zeros((128, 1040), np.float32)
    c[:, 0:128] = np.eye(128, dtype=np.float32)
    s = np.arange(128)[:, None]
    t = np.arange(128)[None, :]
    c[:, 128:256] = (s <= t).astype(np.float32)
    c[:, 256:384] = ((s <= t) & (s // 8 == t // 8)).astype(np.float32)
    c[:, 384:400] = (s // 8 == np.arange(16)[None, :]).astype(np.float32)
    c[:, 400:528] = 1.0
    for k in range(4):
        c[k, 528 + k * 128:528 + (k + 1) * 128] = 1.0
    return c


def make_in_maps(inp, with_mixers=True):
    f = lambda a: np.ascontiguousarray(np.asarray(a, np.float32))
    cols = np.zeros((128, NCOLS), np.float32)

    def put(name, arr):
        o, k = COLOFF[name]
        assert arr.shape == (128, k), (name, arr.shape, k)
        cols[:, o:o + k] = arr
    adab = [inp["ab_ada_b"][0], inp["ffn_ada_b"][0], inp["ml_ada_b"][0], inp["ffn_ada_b"][1]]
    ng = [inp["ab_norm_g"][0], inp["ffn_norm_g"][0], inp["ml_norm_g"][0], inp["ffn_norm_g"][1]]
    adaw = [inp["ab_ada_w"][0], inp["ffn_ada_w"][0], inp["ml_ada_w"][0], inp["ffn_ada_w"][1]]
    for s in range(4):
        put("adab%d" % s, _fm(adab[s]))
        put("normg%d" % s, _fm(ng[s]))
    put("glabg", _fm(inp["gla_b_gate"][0]))
    put("glang", _fm(inp["gla_norm_g"][0]))
    put("s5d", _fm(inp["s5_d"][0]))
    put("s5bglu", _fm(inp["s5_b_glu"][0]))
    put("mlng", _fm(inp["ml_out_norm_g"][0]))
    bi = np.zeros((128, 1), np.float32)
    bi[0:4, 0] = np.asarray(inp["ml_b_i"][0], np.float32)
    bf = np.zeros((128, 1), np.float32)
    bf[0:4, 0] = np.asarray(inp["ml_b_f"][0], np.float32)
    put("mlbi", bi)
    put("mlbf", bf)
    r = lambda a: np.ascontiguousarray(np.asarray(a, np.float32).reshape(32, 2, 64).transpose(1, 2, 0).reshape(128, 32))
    put("lamre", r(inp["s5_lam_re"][0]))
    put("lamim", r(inp["s5_lam_im"][0]))
    put("logdt", r(np.broadcast_to(np.asarray(inp["s5_log_dt"][0], np.float32)[:, None], (64, 64))))
    rst = np.ones((128, 1152), np.float32)
    rst[:, 1024::8] = 0.0
    shared = {"cols": cols, "cst": _consts(), "rst": rst,
              "w_ab_in": f(inp["ab_w_in"][0]), "w_gate": f(inp["gla_w_gate"][0]), "w_ab_out": f(inp["ab_w_out"][0]), "w_glu": f(inp["s5_w_glu"][0]),
              "w_ml_in": f(inp["ml_w_in"][0]), "w_ml_out": f(inp["ml_w_out"][0]),
              "gfin": np.ascontiguousarray(np.broadcast_to(np.asarray(inp["final_norm_g"], np.float32)[None, :], (128, D)))}
    for s in range(4):
        shared["ada_w%d" % s] = f(adaw[s])
    for l in range(2):
        shared["ffn_w1_%d" % l] = f(inp["ffn_w1"][l])
        shared["ffn_w3_%d" % l] = f(inp["ffn_w3"][l])
        shared["ffn_w2_%d" % l] = f(inp["ffn_w2"][l])
    bpad = np.zeros((32, 2, 128, 128), np.float32)
    cpad = np.zeros((32, 2, 128, 128), np.float32)
    for q in range(32):
        for g2 in range(2):
            g = 2 * q + g2
            c0 = (2 * (q % 4) + g2) * 16
            bpad[q, 0, g2 * 64:(g2 + 1) * 64, c0:c0 + 16] = inp["s5_b_re"][0][g]
            bpad[q, 1, g2 * 64:(g2 + 1) * 64, c0:c0 + 16] = inp["s5_b_im"][0][g]
            cpad[q, 0, g2 * 64:(g2 + 1) * 64, c0:c0 + 16] = np.asarray(inp["s5_c_re"][0][g]).T
            cpad[q, 1, g2 * 64:(g2 + 1) * 64, c0:c0 + 16] = np.asarray(inp["s5_c_im"][0][g]).T
    shared["bpad"] = bpad
    shared["cpad"] = cpad
    maps = []
    for c in range(8):
        b = c % 4
        m = dict(shared)
        m["xp"] = f(inp["x_prompt"][b])
        m["xs"] = f(np.asarray(inp["x_sample"][16 * c:16 * c + 16]).reshape(128, D))
        crep = np.concatenate([np.repeat(np.asarray(inp["c_prompt"][b:b + 1], np.float32), 128, axis=0),
                               np.repeat(np.asarray(inp["c_sample"][16 * c:16 * c + 16], np.float32), 8, axis=0)], axis=0)
        m["crep"] = np.ascontiguousarray(crep)
        m["sgla"] = f(inp["state_gla"][0, 16 * c:16 * c + 16])
        sl = slice(16 * c, 16 * c + 16)
        m["sC0T"] = f(np.asarray(inp["state_mlstm_C"][0, sl]).transpose(0, 1, 3, 2))
        m["sn_in"] = f(np.asarray(inp["state_mlstm_n"][0, sl]).reshape(16, 4, 2, 128).transpose(1, 3, 2, 0))
        m["sm_in"] = f(np.asarray(inp["state_mlstm_m"][0, sl]).T)
        sre = np.asarray(inp["state_s5_re"][0, sl], np.float32).reshape(16, 32, 128)
        sim_ = np.asarray(inp["state_s5_im"][0, sl], np.float32).reshape(16, 32, 128)
        m["s5in"] = f(np.stack([sre, sim_], axis=0).transpose(2, 0, 3, 1))
        maps.append(m)
    return maps


def kernel(**inp):
    key = "full"
    if key not in _NC_CACHE:
        _NC_CACHE[key] = K(True).build()
    nc = _NC_CACHE[key]
    maps = make_in_maps(inp)
    res = run_bass_kernel_spmd(nc, maps, core_ids=list(range(8)))
    R = res.results
    f32 = lambda a: np.ascontiguousarray(np.asarray(a, np.float32))
    y_prompt = f32(np.stack([R[b]["yp"] for b in range(4)], axis=0))
    y_sample = f32(np.concatenate([R[c]["ys"].reshape(16, 8, D) for c in range(8)], axis=0))
    unp = lambda arr: np.asarray(arr).reshape(2, 64, 32).transpose(2, 0, 1).reshape(64, 64)
    gla_p = f32(np.stack([R[b]["o_gla_p"] for b in range(4)], axis=0)[None])
    s5re_p = f32(np.stack([unp(R[b]["o_s5_p"][0]) for b in range(4)], axis=0)[None])
    s5im_p = f32(np.stack([unp(R[b]["o_s5_p"][1]) for b in range(4)], axis=0)[None])
    C_p = f32(np.stack([np.asarray(R[b]["o_ml_C_p"]).transpose(0, 2, 1) for b in range(4)], axis=0)[None])
    n_p = f32(np.stack([np.asarray(R[b]["o_ml_n_p"]).transpose(0, 2, 1).reshape(4, 256) for b in range(4)], axis=0)[None])
    m_p = f32(np.stack([np.asarray(R[b]["o_ml_m_p"])[:, 0] for b in range(4)], axis=0)[None])
    gla_s = f32(np.concatenate([R[c]["o_gla_s"] for c in range(8)], axis=0)[None])
    uns = lambda arr, k: np.asarray(arr)[:, k].transpose(2, 0, 1).reshape(16, 64, 64)
    s5re_s = f32(np.concatenate([uns(R[c]["o_s5_s"], 0) for c in range(8)], axis=0)[None])
    s5im_s = f32(np.concatenate([uns(R[c]["o_s5_s"], 1) for c in range(8)], axis=0)[None])
    C_s = f32(np.concatenate([np.asarray(R[c]["o_ml_C_s"]).transpose(0, 1, 3, 2) for c in range(8)], axis=0)[None])
    n_s = f32(np.concatenate([np.asarray(R[c]["o_ml_n_s"]).transpose(3, 0, 2, 1).reshape(16, 4, 256) for c in range(8)], axis=0)[None])
    m_s = f32(np.concatenate([np.asarray(R[c]["o_ml_m_s"]).T for c in range(8)], axis=0)[None])
    return (y_prompt, y_sample, gla_p, s5re_p, s5im_p, C_p, n_p, m_p, gla_s, s5re_s, s5im_s, C_s, n_s, m_s)
```

```python
import types
import numpy as np
from contextlib import ExitStack
import concourse.bass as bass
import concourse.mybir as mybir
from concourse.bass_utils import run_bass_kernel_spmd

F32 = mybir.dt.float32
BF16 = mybir.dt.bfloat16
AF = mybir.ActivationFunctionType
ALU = mybir.AluOpType

D = 2048
DFF = 5632
EPS = 1e-6
EPOCH = 4000
N_DMA_SEMS = 24
SAME_ENGINE_SYNC = True
NBLK = 2
TP = 1024


def _freeze(fn):
    if fn.__closure__ is None:
        return fn
    cells = []
    for c in fn.__closure__:
        try:
            cells.append(types.CellType(c.cell_contents))
        except ValueError:
            cells.append(c)
    return types.FunctionType(fn.__code__, fn.__globals__, fn.__name__, fn.__defaults__, tuple(cells))


class Sched:
    ENGS = ("pe", "act", "dve", "pool", "sp")

    def __init__(self, nc, es):
        self.nc = nc
        self.es = es
        self.prog = {e: [] for e in self.ENGS}
        self.cnt = {e: 0 for e in self.ENGS}
        self.sems = {e: [] for e in self.ENGS}
        self.waited = {e: {} for e in self.ENGS}
        self.last_write = {}
        self.reads = {}
        self.dma_sems = [es.enter_context(nc.semaphore("dma%d" % i)) for i in range(N_DMA_SEMS)]
        self.dma_cnt = [0] * N_DMA_SEMS
        self.dma_rr = 0
        self.dma_rr_pool = 0
        self.n_inst = {e: 0 for e in self.ENGS}

    def _eng_sem(self, eng, epoch):
        lst = self.sems[eng]
        while len(lst) <= epoch:
            lst.append(self.es.enter_context(self.nc.semaphore("s_%s_%d" % (eng, len(lst)))))
        return lst[epoch]

    def _wait(self, eng, dep):
        if dep[0] == 'e':
            _, peng, c = dep
            if peng == eng and (eng == 'pe' or not SAME_ENGINE_SYNC):
                return
            epoch = (c - 1) // EPOCH
            val = (c - 1) % EPOCH + 1
            key = ('e', peng, epoch)
            sem = self._eng_sem(peng, epoch)
        else:
            _, si, val = dep
            key = ('d', si)
            sem = self.dma_sems[si]
        w = self.waited[eng]
        if w.get(key, 0) >= val:
            return
        w[key] = val
        self.prog[eng].append(lambda e, sem=sem, val=val: e.wait_ge(sem, val))

    def _deps(self, eng, reads, writes):
        deps = []
        for r in reads:
            d = self.last_write.get(r)
            if d is not None:
                deps.append(d)
        for w in writes:
            d = self.last_write.get(w)
            if d is not None:
                deps.append(d)
            deps.extend(self.reads.get(w, {}).values())
        for d in deps:
            self._wait(eng, d)

    def _commit(self, myid, reads, writes):
        for r in reads:
            self.reads.setdefault(r, {})[myid[:2]] = myid
        for w in writes:
            self.last_write[w] = myid
            self.reads[w] = {}

    def op(self, eng, fn, reads=(), writes=(), signal=True):
        fn = _freeze(fn)
        self._deps(eng, reads, writes)
        c = self.cnt[eng] + 1
        myid = ('e', eng, c)
        self.n_inst[eng] += 1
        if signal:
            self.cnt[eng] = c
            sem = self._eng_sem(eng, (c - 1) // EPOCH)
            self.prog[eng].append(lambda e, fn=fn, sem=sem: fn(e).then_inc(sem, 1))
        else:
            self.prog[eng].append(lambda e, fn=fn: fn(e))
        self._commit(myid, reads, writes)
        return myid

    def dma(self, q, out, in_, reads=(), writes=(), **kw):
        half = N_DMA_SEMS // 2
        if q == 'pool':
            si = half + self.dma_rr_pool
            self.dma_rr_pool = (self.dma_rr_pool + 1) % half
        else:
            si = self.dma_rr
            self.dma_rr = (self.dma_rr + 1) % half
        if self.dma_cnt[si] > 0:
            self._wait(q, ('d', si, 16 * self.dma_cnt[si]))
        self._deps(q, reads, writes)
        self.dma_cnt[si] += 1
        val = 16 * self.dma_cnt[si]
        sem = self.dma_sems[si]
        myid = ('d', si, val)
        self.n_inst[q] += 1
        self.prog[q].append(lambda e, out=out, in_=in_, sem=sem, kw=kw: e.dma_start(out=out, in_=in_, **kw).then_inc(sem, 16))
        self._commit(myid, reads, writes)
        return myid

    def barrier(self, pool=True):
        for eng in (("pe", "act", "dve", "pool", "sp") if pool else ("pe", "act", "dve", "sp")):
            for peng in ("pe", "act", "dve", "pool"):
                if self.cnt[peng] > 0:
                    self._wait(eng, ('e', peng, self.cnt[peng]))
            for si in range(N_DMA_SEMS):
                if self.dma_cnt[si] > 0:
                    self._wait(eng, ('d', si, 16 * self.dma_cnt[si]))

    def finish(self):
        for si in range(N_DMA_SEMS):
            if self.dma_cnt[si] > 0:
                self._wait('sp', ('d', si, 16 * self.dma_cnt[si]))
        nc = self.nc
        with nc.Block() as block:
            @block.tensor
            def _(e):
                for f in self.prog['pe']:
                    f(e)

            @block.scalar
            def _(e):
                for f in self.prog['act']:
                    f(e)

            @block.vector
            def _(e):
                for f in self.prog['dve']:
                    f(e)

            @block.gpsimd
            def _(e):
                for f in self.prog['pool']:
                    f(e)

            @block.sync
            def _(e):
                for f in self.prog['sp']:
                    f(e)


def _col_layout():
    off = {}
    n = 0

    def add(name, k):
        nonlocal n
        off[name] = (n, k)
        n += k
    for s in range(4):
        add("adab%d" % s, 48)
        add("normg%d" % s, 16)
    add("glabg", 4)
    add("glang", 8)
    add("s5d", 8)
    add("s5bglu", 8)
    add("mlng", 16)
    add("mlbi", 1)
    add("mlbf", 1)
    add("lamre", 32)
    add("lamim", 32)
    add("logdt", 32)
    return off, n


COLOFF, NCOLS = _col_layout()


def K3(x):
    return [x, x + 'r', x + 'i']


def _fm(v):
    v = np.asarray(v, np.float32).reshape(-1, 128)
    return np.ascontiguousarray(v.T)


class K:
    def __init__(self, with_mixers=True, dbg=None):
        self.with_mixers = with_mixers
        self.dbg = dbg or {}
        self.nc = bass.Bass("TRN2", target_bir_lowering=False)
        self.es = ExitStack()

    def dram_in(self, name, shape, dt=F32):
        return self.nc.dram_tensor(name, list(shape), dt, kind="ExternalInput").ap()

    def dram_out(self, name, shape, dt=F32):
        return self.nc.dram_tensor(name, list(shape), dt, kind="ExternalOutput").ap()

    def sb(self, name, shape, dt):
        return self.es.enter_context(self.nc.sbuf_tensor(name, list(shape), dt))

    def build(self):
        nc = self.nc
        with self.es:
            self._decl()
            self.S = Sched(nc, self.es)
            self._setup()
            for blk in range(self.dbg.get('nblk', NBLK)):
                self._block(blk)
            self.S.finish()
        return nc

    def _decl(self):
        di, do = self.dram_in, self.dram_out
        self.xp = di("xp", [NBLK * TP, D])
        self.xs = di("xs", [128, D])
        self.crep = di("crep", [256, D])
        self.cols_d = di("cols", [128, NCOLS])
        self.gfin_d = di("gfin", [128, D])
        self.cst_d = di("cst", [128, 1040])
        self.ada_w = [di("ada_w%d" % s, [D, 3 * D]) for s in range(4)]
        self.ffn_w1 = [di("ffn_w1_%d" % l, [D, DFF]) for l in range(2)]
        self.ffn_w3 = [di("ffn_w3_%d" % l, [D, DFF]) for l in range(2)]
        self.ffn_w2 = [di("ffn_w2_%d" % l, [DFF, D]) for l in range(2)]
        self.w_ab_in = di("w_ab_in", [D, 4112])
        self.w_gate = di("w_gate", [16, 512])
        self.w_ab_out = di("w_ab_out", [D, D])
        self.w_glu = di("w_glu", [1024, 1024])
        self.rst_d = di("rst", [128, 1152])
        self.sgla = di("sgla", [16, 4, 128, 256])
        self.o_gla_p = do("o_gla_p", [4, 128, 256])
        self.o_gla_s = do("o_gla_s", [16, 4, 128, 256])
        self.bpad = di("bpad", [32, 2, 128, 128])
        self.cpad = di("cpad", [32, 2, 128, 128])
        self.s5in = di("s5in", [32, 2, 128, 16])
        self.o_s5_p = do("o_s5_p", [2, 128, 32])
        self.o_s5_s = do("o_s5_s", [32, 2, 128, 16])
        self.scr_bbT = self.nc.dram_tensor("scr_bbT", [32, 2, 128, 128], BF16, kind="Internal").ap()
        self.w_ml_in = di("w_ml_in", [D, 6152])
        self.w_ml_out = di("w_ml_out", [D, D])
        self.sC0T = di("sC0T", [16, 4, 256, 512])
        self.sn_in = di("sn_in", [4, 128, 2, 16])
        self.sm_in = di("sm_in", [4, 16])
        self.o_ml_C_p = do("o_ml_C_p", [4, 256, 512])
        self.o_ml_n_p = do("o_ml_n_p", [4, 128, 2])
        self.o_ml_m_p = do("o_ml_m_p", [4, 1])
        self.o_ml_C_s = do("o_ml_C_s", [16, 4, 256, 512])
        self.o_ml_n_s = do("o_ml_n_s", [4, 128, 2, 16])
        self.o_ml_m_s = do("o_ml_m_s", [4, 16])
        self.scr_mlC = self.nc.dram_tensor("scr_mlC", [4, 256, 512], F32, kind="Internal").ap()
        self.scr_mln = self.nc.dram_tensor("scr_mln", [4, 128, 2], F32, kind="Internal").ap()
        self.scr_mod = self.nc.dram_tensor("scr_mod", [4, 128, 8448], F32, kind="Internal").ap()
        self.scr_gla = self.nc.dram_tensor("scr_gla", [4, 128, 256], F32, kind="Internal").ap()
        self.yp = do("yp", [NBLK * TP, D])
        self.ys = do("ys", [128, D])
        self.x = self.sb("x", [128, 9, D], F32)
        self.wb = [self.sb("wb%d" % i, [128, 4096], BF16) for i in range(2)]
        self.wb_rr = 0
        self.wb_cur = [w[:] for w in self.wb]
        self.GT = self.sb("GT", [128, 2, D], F32)
        self.cols = self.sb("colsb", [128, NCOLS], F32)
        self.cst = self.sb("cstb", [128, 1040], F32)
        self.identb = self.sb("identb", [128, 128], BF16)
        self.onesb = self.sb("onesb", [128, 128], BF16)
        self.small = self.sb("small", [128, 64], F32)
        self.rst = self.sb("rstb", [128, 1152], BF16)
        self.sm2 = self.sb("sm2", [128, 64], F32)
        self.sm3 = self.sb("sm3", [128, 128], F32)
        self.PW = self.sb("PW", [128, 3, 15, 32], F32)
        self.S5H = self.sb("S5H", [128, 2, 32], F32)
        self.ARW = 22350
        self.AR = self.sb("arena", [128, self.ARW], F32)
        self.ps = [self.es.enter_context(self.nc.psum_tensor("ps%d" % i, [128, 512], F32)) for i in range(8)]
        self.ps_rr = 0
        self.gen_banks = list(range(8))

    def af(self, off, n):
        return self.AR[:, off:off + n]

    def ab(self, off, n_bf):
        assert n_bf % 2 == 0
        return self.AR[:, off:off + n_bf // 2].bitcast(BF16)

    def bank(self):
        self.ps_rr = (self.ps_rr + 1) % len(self.gen_banks)
        i = self.gen_banks[self.ps_rr]
        return self.ps[i], "ps%d" % i

    def lbank(self, i):
        return self.ps[3 + i], "ps%d" % (3 + i)

    def wbuf(self):
        bufs = self.wb_cur
        self.wb_rr = (self.wb_rr + 1) % len(bufs)
        i = self.wb_rr
        return bufs[i], "wb%d" % i

    def col(self, name, j=0, n=1):
        o, k = COLOFF[name]
        return self.cols[:, o + j:o + j + n]

    def load_w(self, W, r0, kc, c0, ncols):
        buf, key = self.wbuf()
        v = buf[:, 0:kc * ncols].rearrange("p (k n) -> p k n", k=kc)
        self.S.dma('pool', v, W[r0:r0 + kc * 128, c0:c0 + ncols].rearrange("(k p) n -> p k n", p=128), writes=[key])
        return v, key

    def _setup(self):
        S = self.S
        S.dma('sp', self.cols[:], self.cols_d, writes=['cols'])
        S.dma('sp', self.cst[:], self.cst_d, writes=['cst'])
        S.dma('pool', self.rst[:], self.rst_d, writes=['rst'])
        S.op('dve', lambda e: e.tensor_copy(out=self.onesb[:], in_=self.cst[:, 400:528]), reads=['cst'], writes=['onesb'])
        S.op('dve', lambda e: e.tensor_copy(out=self.identb[:], in_=self.cst[:, 0:128]), reads=['cst'], writes=['identb'])
        S.barrier()
        if self.with_mixers and 0 in self.dbg.get('layers', range(2)) and self.dbg.get('s5', True):
            self.s5_setup()

    def _block(self, blk):
        S = self.S
        self.NT = 9 if blk == 0 else 8
        NT = self.NT
        self.T = NT * 128
        for i in range(8):
            S.dma('sp', self.x[:, i, :], self.xp[blk * TP + i * 128: blk * TP + (i + 1) * 128, :], writes=[('x', i)])
        if NT == 9:
            S.dma('sp', self.x[:, 8, :], self.xs, writes=[('x', 8)])
        for layer in self.dbg.get('layers', range(2)):
            if self.with_mixers:
                self.prenorm(2 * layer, blk)
                if layer == 0:
                    self.mixer_ab(blk)
                else:
                    self.mixer_ml(blk)
            if self.dbg.get('ffn', True):
                self.prenorm(2 * layer + 1, blk)
                self.ffn(layer)
        self.final_norm(blk)
        S.barrier(pool=False)

    def groups(self):
        g = [(0, 512), (512, 512)]
        if self.NT == 9:
            g.append((1024, 128))
        return g

    def make_cT(self):
        S = self.S
        o2 = 9216 + 4352
        self.cT = self.ab(18688, 16 * 256).rearrange("p (k t) -> p k t", k=16)
        tmpf = self.af(o2, D)
        tmpb = self.ab(o2 + D, D)
        for t in range(2):
            S.dma('sp', tmpf, self.crep[t * 128:(t + 1) * 128, :], writes=['tmpf'])
            S.op('act', lambda e: e.activation(out=tmpb, in_=tmpf, func=AF.Silu), reads=['tmpf'], writes=['tmpb'])
            for half in range(2):
                pb, pk = self.bank()
                pv = pb[:].bitcast(BF16).rearrange("p (k n) -> p k n", k=8)
                for k in range(8):
                    kk = half * 8 + k
                    S.op('pe', lambda e, k=k, kk=kk, pv=pv: e.transpose(out=pv[:, k, :], in_=tmpb[:, kk * 128:(kk + 1) * 128], identity=self.identb[:]),
                         reads=['tmpb', 'identb'], writes=[pk], signal=(k == 7))
                S.op('dve', lambda e, pv=pv, half=half, t=t: e.tensor_copy(out=self.cT[:, half * 8:(half + 1) * 8, t * 128:(t + 1) * 128], in_=pv),
                     reads=[pk], writes=['cT'])

        S.barrier()

    def prenorm(self, site, blk=0):
        S = self.S
        NT = self.NT
        S.barrier(pool=False)
        self.gen_banks = list(range(8))
        W = self.ada_w[site]
        cached = blk > 0
        if not cached:
            self.make_cT()
        self.hT = self.ab(0, 16 * 1152).rearrange("p (k t) -> p k t", k=16)
        o1 = 9216
        GXg = self.af(o1, 4096).rearrange("p (k t) -> p k t", k=16)
        GX = self.af(o1, 2176).rearrange("p (k t) -> p k t", k=16)
        SHx = self.af(o1 + 2176, 2176).rearrange("p (k t) -> p k t", k=16)
        o2 = o1 + 4352
        xn = [self.ab(o2 + i * 1024, D) for i in range(2)]
        junk = self.ab(o2 + 2048, D)
        adab = lambda j: self.col("adab%d" % site, j)
        ng = lambda j: self.col("normg%d" % site, j)

        if not cached:
            def mod_chunks(jlist, evac, c_lo, ncol):
                for j0 in range(jlist[0], jlist[-1] + 1, 2):
                    wv, wk = self.load_w(W, 0, 16, j0 * 128, 256)
                    for jj in range(2):
                        j = j0 + jj
                        pb, pk = self.bank()
                        for k in range(16):
                            S.op('pe', lambda e, k=k, jj=jj, wv=wv, pb=pb: e.matmul(pb[:, 0:ncol], lhsT=wv[:, k, jj * 128:(jj + 1) * 128], rhs=self.cT[:, k, c_lo:c_lo + ncol], start=(k == 0), stop=(k == 15)),
                                 reads=[wk, 'cT'], writes=[pk], signal=(k == 15))
                        evac(j, pb, pk)

            def ev_gate(j, pb, pk):
                S.op('act', lambda e: e.activation(out=GXg[:, j - 32, :], in_=pb[:, 0:256], func=AF.Identity, bias=adab(j), scale=1.0),
                     reads=[pk, 'cols'], writes=[('GX', j - 32)])
            mod_chunks(list(range(32, 48)), ev_gate, 0, 256)
            identf = self.cst[:, 0:128]
            for g in range(2):
                for q in range(4):
                    pb, pk = self.bank()
                    for kk in range(4):
                        k = q * 4 + kk
                        S.op('pe', lambda e, k=k, kk=kk, pb=pb, g=g: e.transpose(out=pb[:, kk * 128:(kk + 1) * 128], in_=GXg[:, k, g * 128:(g + 1) * 128], identity=identf),
                             reads=[('GX', k), 'cst'], writes=[pk], signal=(kk == 3))
                    S.op('act', lambda e, pb=pb, g=g, q=q: e.copy(out=self.GT[:, g, q * 512:(q + 1) * 512], in_=pb[:]), reads=[pk], writes=[('GT', g)])
            S.barrier(pool=False)

            def ev_shift(j, pb, pk):
                S.op('act', lambda e: e.activation(out=SHx[:, j, :], in_=pb[:, 0:136], func=AF.Identity, bias=adab(j), scale=1.0),
                     reads=[pk, 'cols'], writes=[('SHx', j)])
            mod_chunks(list(range(0, 16)), ev_shift, 120, 136)

            def ev_scale(j, pb, pk):
                jj = j - 16
                S.op('dve', lambda e: e.tensor_scalar(out=GX[:, jj, :], in0=pb[:, 0:136], scalar1=self.small[:, jj:jj + 1], scalar2=ng(jj), op0=ALU.add, op1=ALU.mult),
                     reads=[pk, 'cols', 'small'], writes=[('GX', jj)])
            S.op('dve', lambda e: e.tensor_scalar(out=self.small[:, 0:16], in0=self.col("adab%d" % site, 16, 16), scalar1=1.0, scalar2=None, op0=ALU.add),
                 reads=['cols'], writes=['small'])
            mod_chunks(list(range(16, 32)), ev_scale, 120, 136)
            S.dma('sp', self.scr_mod[site, :, 0:2176], self.af(o1, 2176), reads=[('GX', k) for k in range(16)], writes=['scr_mod'])
            S.dma('sp', self.scr_mod[site, :, 2176:4352], self.af(o1 + 2176, 2176), reads=[('SHx', k) for k in range(16)], writes=['scr_mod'])
            S.dma('sp', self.scr_mod[site, :, 4352:8448], self.GT[:].rearrange("p g d -> p (g d)"), reads=[('GT', 0), ('GT', 1)], writes=['scr_mod'])
        else:
            S.dma('sp', self.af(o1, 2176), self.scr_mod[site, :, 0:2176], reads=['scr_mod'], writes=[('GX', k) for k in range(16)])
            S.dma('sp', self.af(o1 + 2176, 2176), self.scr_mod[site, :, 2176:4352], reads=['scr_mod'], writes=[('SHx', k) for k in range(16)])
            S.dma('sp', self.GT[:].rearrange("p g d -> p (g d)"), self.scr_mod[site, :, 4352:8448], reads=['scr_mod'], writes=[('GT', 0), ('GT', 1)])
        ss = self.small[:, 16:16 + NT]
        rstd = self.small[:, 32:32 + NT]
        for i in range(NT):
            S.op('act', lambda e, i=i: e.activation(out=junk, in_=self.x[:, i, :], func=AF.Square, accum_out=self.small[:, 16 + i:17 + i]),
                 reads=[('x', i)], writes=['mtmp0', 'mtmp1', 'small'])
        S.op('dve', lambda e: e.tensor_scalar(out=rstd, in0=ss, scalar1=1.0 / D, scalar2=EPS, op0=ALU.mult, op1=ALU.add), reads=['small'], writes=['small'])
        S.op('act', lambda e: e.activation(out=rstd, in_=rstd, func=AF.Sqrt), reads=['small'], writes=['small'])
        S.op('dve', lambda e: e.reciprocal(out=rstd, in_=rstd), reads=['small'], writes=['small'])
        for i in range(NT):
            xb = xn[i % 2]
            xk = 'xn%d' % (i % 2)
            S.op('act', lambda e, i=i, xb=xb: e.activation(out=xb, in_=self.x[:, i, :], func=AF.Copy, scale=self.small[:, 32 + i:33 + i]),
                 reads=[('x', i), 'small'], writes=[xk])
            for half in range(2):
                pb, pk = self.bank()
                pv = pb[:].bitcast(BF16).rearrange("p (k n) -> p k n", k=8)
                for k in range(8):
                    kk = half * 8 + k
                    S.op('pe', lambda e, k=k, kk=kk, pv=pv, xb=xb: e.transpose(out=pv[:, k, :], in_=xb[:, kk * 128:(kk + 1) * 128], identity=self.identb[:]),
                         reads=[xk, 'identb'], writes=[pk], signal=(k == 7))
                dst = self.hT[:, half * 8:(half + 1) * 8, i * 128:(i + 1) * 128]
                tmp = self.af(o2 + 3072 + 1024 * half, 1024).rearrange("p (k n) -> p k n", k=8)
                tk = 'mtmp%d' % half
                S.op('dve', lambda e, pv=pv, tmp=tmp, half=half, i=i: e.tensor_tensor(out=tmp, in0=pv, in1=(GX[:, half * 8:(half + 1) * 8, 0:1].to_broadcast([128, 8, 128]) if i < 8 else GX[:, half * 8:(half + 1) * 8, 8:136]), op=ALU.mult),
                     reads=[pk] + [('GX', half * 8 + k) for k in range(8)], writes=[tk])
                S.op('dve', lambda e, dst=dst, tmp=tmp, half=half, i=i: e.tensor_tensor(out=dst, in0=tmp, in1=(SHx[:, half * 8:(half + 1) * 8, 0:1].to_broadcast([128, 8, 128]) if i < 8 else SHx[:, half * 8:(half + 1) * 8, 8:136]), op=ALU.add),
                     reads=[tk] + [('SHx', half * 8 + k) for k in range(8)], writes=[('hT', i)])
        S.barrier()

    def resid(self, i, c0, n, pb, pk):
        S = self.S
        g = 0 if i < 8 else 1
        tmp = self.rtmp[self.rt_rr % 2][:, 0:n]
        tk = 'rtmp%d' % (self.rt_rr % 2)
        self.rt_rr += 1
        S.op('dve', lambda e: e.tensor_tensor(out=tmp, in0=pb[:, 0:n], in1=self.GT[:, g, c0:c0 + n], op=ALU.mult), reads=[pk, ('GT', g)], writes=[tk])
        S.op('dve', lambda e: e.tensor_tensor(out=self.x[:, i, c0:c0 + n], in0=self.x[:, i, c0:c0 + n], in1=tmp, op=ALU.add), reads=[tk, ('x', i)], writes=[('x', i)])

    def ffn(self, layer):
        S = self.S
        NT = self.NT
        T = self.T
        o1 = 9216
        W1, W3, W2 = self.ffn_w1[layer], self.ffn_w3[layer], self.ffn_w2[layer]
        self.rtmp = [self.af(o1 + 4608 + i * 512, 512) for i in range(2)]
        self.rt_rr = 0
        sil = [self.af(o1 + 5632 + i * 512, 512) for i in range(2)]
        self.wb_cur = [w[:] for w in self.wb] + [self.ab(o1 + 6656 + i * 2048, 4096) for i in range(2)]
        pieces = [(0, 8), (8, 8), (16, 8), (24, 8), (32, 8), (40, 4)]
        for (j0, nj) in pieces:
            gT = self.ab(o1, 8 * 1152).rearrange("p (k t) -> p k t", k=8)
            for jp in range(j0, j0 + nj, 2):
                w1v, w1k = self.load_w(W1, 0, 16, jp * 128, 256)
                w3v, w3k = self.load_w(W3, 0, 16, jp * 128, 256)
                for jj in range(2):
                    jl = jp + jj - j0
                    for gi, (t0, tn) in enumerate(self.groups()):
                        pa, pak = self.bank()
                        pc, pck = self.bank()
                        for k in range(16):
                            S.op('pe', lambda e, k=k, jj=jj, w1v=w1v, pa=pa, t0=t0, tn=tn: e.matmul(pa[:, 0:tn], lhsT=w1v[:, k, jj * 128:(jj + 1) * 128], rhs=self.hT[:, k, t0:t0 + tn], start=(k == 0), stop=(k == 15)),
                                 reads=[w1k, 'hTall'], writes=[pak], signal=(k == 15))
                        for k in range(16):
                            S.op('pe', lambda e, k=k, jj=jj, w3v=w3v, pc=pc, t0=t0, tn=tn: e.matmul(pc[:, 0:tn], lhsT=w3v[:, k, jj * 128:(jj + 1) * 128], rhs=self.hT[:, k, t0:t0 + tn], start=(k == 0), stop=(k == 15)),
                                 reads=[w3k, 'hTall'], writes=[pck], signal=(k == 15))
                        sl = sil[self.rt_rr % 2][:, 0:tn]
                        sk = 'sil%d' % (self.rt_rr % 2)
                        self.rt_rr += 1
                        S.op('act', lambda e, sl=sl, pa=pa, tn=tn: e.activation(out=sl, in_=pa[:, 0:tn], func=AF.Silu), reads=[pak], writes=[sk])
                        S.op('dve', lambda e, sl=sl, pc=pc, tn=tn, jl=jl, t0=t0: e.tensor_tensor(out=gT[:, jl, t0:t0 + tn], in0=sl, in1=pc[:, 0:tn], op=ALU.mult),
                             reads=[sk, pck], writes=[('gT', jl)])
            for cp in range(4):
                w2v, w2k = self.load_w(W2, j0 * 128, nj, cp * 512, 512)
                for i in range(NT):
                    pb, pk = self.bank()
                    for k in range(nj):
                        S.op('pe', lambda e, k=k, w2v=w2v, pb=pb, i=i: e.matmul(pb[:, 0:512], lhsT=gT[:, k, i * 128:(i + 1) * 128], rhs=w2v[:, k, :], start=(k == 0), stop=(k == nj - 1)),
                             reads=[w2k] + [('gT', kk) for kk in range(nj)], writes=[pk], signal=(k == nj - 1))
                    self.resid(i, cp * 512, 512, pb, pk)
        S.barrier(pool=False)
        self.wb_cur = [w[:] for w in self.wb]
        self.wb_rr = 0

    def final_norm(self, blk):
        S = self.S
        NT = self.NT
        S.barrier(pool=False)
        gB = self.af(0, D)
        junk = self.ab(D, D)
        yt = [self.af(D + 1024 + i * D, D) for i in range(2)]
        S.dma('sp', gB, self.gfin_d, writes=['gB'])
        ss = self.small[:, 16:16 + NT]
        rstd = self.small[:, 32:32 + NT]
        for i in range(NT):
            S.op('act', lambda e, i=i: e.activation(out=junk, in_=self.x[:, i, :], func=AF.Square, accum_out=self.small[:, 16 + i:17 + i]),
                 reads=[('x', i)], writes=['mtmp0', 'mtmp1', 'small'])
        S.op('dve', lambda e: e.tensor_scalar(out=rstd, in0=ss, scalar1=1.0 / D, scalar2=EPS, op0=ALU.mult, op1=ALU.add), reads=['small'], writes=['small'])
        S.op('act', lambda e: e.activation(out=rstd, in_=rstd, func=AF.Sqrt), reads=['small'], writes=['small'])
        S.op('dve', lambda e: e.reciprocal(out=rstd, in_=rstd), reads=['small'], writes=['small'])
        for i in range(NT):
            y = yt[i % 2]
            yk = 'yt%d' % (i % 2)
            S.op('dve', lambda e, i=i, y=y: e.scalar_tensor_tensor(out=y, in0=self.x[:, i, :], scalar=self.small[:, 32 + i:33 + i], in1=gB, op0=ALU.mult, op1=ALU.mult),
                 reads=[('x', i), 'small', 'gB'], writes=[yk])
            if i < 8:
                S.dma('sp', self.yp[blk * TP + i * 128: blk * TP + (i + 1) * 128, :], y, reads=[yk], writes=['yp'])
            else:
                S.dma('sp', self.ys, y, reads=[yk], writes=['ys'])

    def proj_feat(self, W, c0, M, evac):
        S = self.S
        wv, wk = self.load_w(W, 0, 16, c0, M)
        for (t0, tn) in self.groups():
            pb, pk = self.bank()
            for k in range(16):
                S.op('pe', lambda e, k=k, pb=pb, t0=t0, tn=tn: e.matmul(pb[0:M, 0:tn], lhsT=wv[:, k, 0:M], rhs=self.hT[:, k, t0:t0 + tn], start=(k == 0), stop=(k == 15)),
                     reads=[wk], writes=[pk], signal=(k == 15))
            evac(t0, tn, pb, pk)

    def proj_feat_multi(self, W, c0, evacs):
        S = self.S
        n = len(evacs)
        wv, wk = self.load_w(W, 0, 16, c0, 128 * n)
        for jj in range(n):
            for (t0, tn) in self.groups():
                pb, pk = self.bank()
                for k in range(16):
                    S.op('pe', lambda e, k=k, pb=pb, t0=t0, tn=tn, jj=jj: e.matmul(pb[:, 0:tn], lhsT=wv[:, k, jj * 128:(jj + 1) * 128], rhs=self.hT[:, k, t0:t0 + tn], start=(k == 0), stop=(k == 15)),
                         reads=[wk], writes=[pk], signal=(k == 15))
                evacs[jj](t0, tn, pb, pk)

    def proj_tok(self, W, c0, n, evac):
        S = self.S
        wv, wk = self.load_w(W, 0, 16, c0, n)
        for i in range(self.NT):
            pb, pk = self.bank()
            for k in range(16):
                S.op('pe', lambda e, k=k, pb=pb, i=i: e.matmul(pb[:, 0:n], lhsT=self.hT[:, k, i * 128:(i + 1) * 128], rhs=wv[:, k, :], start=(k == 0), stop=(k == 15)),
                     reads=[wk], writes=[pk], signal=(k == 15))
            evac(i, pb, pk)

    def out_proj(self, W, r0, kc, srcT):
        S = self.S
        for cp in range(8):
            wv, wk = self.load_w(W, r0, kc, cp * 256, 256)
            for i in range(self.NT):
                pb, pk = self.bank()
                for k in range(kc):
                    S.op('pe', lambda e, k=k, pb=pb, i=i, wv=wv: e.matmul(pb[:, 0:256], lhsT=srcT[:, k, i * 128:(i + 1) * 128], rhs=wv[:, k, :], start=(k == 0), stop=(k == kc - 1)),
                         reads=[wk, 'srcT'], writes=[pk], signal=(k == kc - 1))
                self.resid(i, cp * 256, 256, pb, pk)

    def mixer_ab(self, blk):
        S = self.S
        self.gen_banks = [0, 1, 2]
        NT = self.NT
        T = self.T
        Wi = self.w_ab_in
        o = [9216]

        def alloc(n):
            r = o[0]
            o[0] += n
            assert o[0] <= self.ARW, o[0]
            return r
        glrT = self.ab(alloc(576), 1152)
        wgb = self.ab(alloc(256), 512)
        LB = self.af(alloc(1152), 1152)
        NB = self.af(alloc(1152), 1152)
        qk = self.af(alloc(1152), 1152)
        qtil = self.ab(alloc(576), 1152)
        ktil = self.ab(alloc(576), 1152)
        kdT = self.ab(alloc(576), 1152)
        E = [self.af(alloc(128), 128) for _ in range(3)]
        Sbf = [self.ab(alloc(128), 256) for _ in range(4)]
        attm_ = [self.ab(alloc(64), 128) for _ in range(2)]
        kdtok_ = [self.ab(alloc(64), 128) for _ in range(2)]
        km = [self.ab(alloc(64), 128) for _ in range(4)]
        oT_ = [self.af(alloc(256), 256).rearrange("p (c t) -> p c t", c=2) for _ in range(2)]
        sq_ = [self.ab(alloc(128), 256).rearrange("p (c t) -> p c t", c=2) for _ in range(2)]
        rs__ = [self.af(alloc(128), 128) for _ in range(2)]
        t1_ = [self.af(alloc(128), 128) for _ in range(2)]
        self.rtmp = [self.af(alloc(256), 256) for i in range(2)]
        self.rt_rr = 0
        Sin = [self.af(alloc(256), 256) for _ in range(4)]
        Sout = [self.af(alloc(256), 256) for _ in range(4)]
        Sg = self.af(alloc(256), 256)
        vtok = LB.bitcast(BF16)[:, 0:NT * 256].rearrange("p (i v) -> p i v", i=NT)
        rsT = NB.bitcast(BF16).rearrange("p (c t) -> p c t", c=2)
        ogT = qk.bitcast(BF16).rearrange("p (c t) -> p c t", c=2)
        maskP = self.cst[:, 128:256]
        maskS = self.cst[:, 256:384]
        selS = self.cst[:, 384:400]
        onesb = self.onesb[:]
        negbg = self.sm2[:, 32:36]
        ebl = self.sm2[:, 0:8]
        ebls = self.sm2[:, 8:24]
        S.barrier(pool=False)
        S.op('dve', lambda e: e.tensor_scalar(out=negbg, in0=self.col("glabg", 0, 4), scalar1=-1.0, scalar2=None, op0=ALU.mult), reads=['cols'], writes=['negbg'])
        wgf = self.af(alloc(512), 512)
        S.dma('sp', wgf[0:16, :], self.w_gate, writes=['wgf'])
        S.op('dve', lambda e: e.tensor_copy(out=wgb[0:16, :], in_=wgf[0:16, :]), reads=['wgf'], writes=['wgb'])

        def ev_glr(t0, tn, pb, pk):
            S.op('act', lambda e: e.copy(out=glrT[0:16, t0:t0 + tn], in_=pb[0:16, 0:tn]), reads=[pk], writes=['glrT'])
        self.proj_feat(Wi, 3072, 16, ev_glr)
        for h in self.dbg.get('heads', range(4)):
            for (t0, tn) in self.groups():
                pb, pk = self.bank()
                S.op('pe', lambda e, pb=pb, t0=t0, tn=tn: e.matmul(pb[:, 0:tn], lhsT=wgb[0:16, h * 128:(h + 1) * 128], rhs=glrT[0:16, t0:t0 + tn], start=True, stop=True),
                     reads=['wgb', 'glrT'], writes=[pk])
                S.op('act', lambda e, pb=pb, t0=t0, tn=tn: e.activation(out=LB[:, t0:t0 + tn], in_=pb[:, 0:tn], func=AF.Exp, bias=negbg[:, h:h + 1], scale=-1.0),
                     reads=[pk, 'negbg'], writes=['LB'])
            S.op('act', lambda e: e.activation(out=LB[:, 0:T], in_=LB[:, 0:T], func=AF.Ln, bias=1.0, scale=1.0), reads=['LB'], writes=['LB'])
            S.op('dve', lambda e: e.tensor_tensor_scan(out=NB[:, 0:T], data0=self.rst[:, 0:T], data1=LB[:, 0:T], initial=0.0, op0=ALU.mult, op1=ALU.add),
                 reads=['LB', 'rst'], writes=['NB'])
            S.op('dve', lambda e: e.tensor_scalar(out=LB[:, 0:T], in0=NB[:, 0:T], scalar1=1.0 / 16, scalar2=None, op0=ALU.mult), reads=['NB'], writes=['LB'])
            S.op('dve', lambda e: e.tensor_scalar(out=NB[:, 0:T], in0=NB[:, 0:T], scalar1=-1.0 / 16, scalar2=None, op0=ALU.mult), reads=['NB'], writes=['NB'])

            def etab(which, c, Eb, ek):
                sl = slice(c * 128, (c + 1) * 128)
                if c < 8:
                    if which == 'q':
                        src, bias = NB, (LB[:, c * 128 - 1:c * 128] if c > 0 else 0.0)
                    elif which == 'k':
                        src, bias = LB, (NB[:, c * 128 - 1:c * 128] if c > 0 else 0.0)
                    else:
                        src, bias = LB, NB[:, c * 128 + 127:c * 128 + 128]
                    S.op('act', lambda e: e.activation(out=Eb, in_=src[:, sl], func=AF.Exp, bias=bias, scale=1.0), reads=['LB', 'NB'], writes=[ek])
                else:
                    if which == 'q':
                        S.op('act', lambda e: e.activation(out=Eb, in_=NB[:, sl], func=AF.Exp), reads=['NB'], writes=[ek])
                    elif which == 'k':
                        S.op('act', lambda e: e.activation(out=Eb, in_=LB[:, sl], func=AF.Exp), reads=['LB'], writes=[ek])
                    else:
                        v3 = lambda a: a.rearrange("p (j i) -> p j i", i=8)
                        S.op('dve', lambda e: e.tensor_tensor(out=v3(Eb), in0=v3(LB[:, sl]), in1=v3(NB[:, sl])[:, :, 7:8].to_broadcast([128, 16, 8]), op=ALU.add),
                             reads=['LB', 'NB'], writes=[ek])
                        S.op('act', lambda e: e.activation(out=Eb, in_=Eb, func=AF.Exp), reads=[ek], writes=[ek])

            def ev_q(t0, tn, pb, pk):
                S.op('act', lambda e: e.activation(out=qk[:, t0:t0 + tn], in_=pb[:, 0:tn], func=AF.Copy, scale=128 ** -0.5), reads=[pk], writes=['qk'])
            self.proj_feat(Wi, h * 128, 128, ev_q)
            for c in range(NT):
                sl = slice(c * 128, (c + 1) * 128)
                Eb, ek = E[c % 3], 'E%d' % (c % 3)
                etab('q', c, Eb, ek)
                S.op('dve', lambda e, sl=sl, Eb=Eb: e.tensor_tensor(out=qtil[:, sl], in0=qk[:, sl], in1=Eb, op=ALU.mult), reads=['qk', ek], writes=['qtil'])
                if c < 8:
                    S.op('dve', lambda e, c=c, Eb=Eb: e.tensor_copy(out=ebl[:, c:c + 1], in_=Eb[:, 127:128]), reads=[ek], writes=['ebl'])
                else:
                    S.op('dve', lambda e, Eb=Eb: e.tensor_copy(out=ebls, in_=Eb.rearrange("p (j i) -> p j i", i=8)[:, :, 7]), reads=[ek], writes=['ebl'])

            def ev_k(t0, tn, pb, pk):
                S.op('act', lambda e: e.copy(out=qk[:, t0:t0 + tn], in_=pb[:, 0:tn]), reads=[pk, 'qtil'], writes=['qk'])
            self.proj_feat(Wi, 512 + h * 128, 128, ev_k)
            for c in range(NT):
                sl = slice(c * 128, (c + 1) * 128)
                Eb, ek = E[c % 3], 'E%d' % (c % 3)
                etab('k', c, Eb, ek)
                S.op('dve', lambda e, sl=sl, Eb=Eb: e.tensor_tensor(out=ktil[:, sl], in0=qk[:, sl], in1=Eb, op=ALU.mult), reads=['qk', ek], writes=['ktil'])
                Eb, ek = E[(c + 1) % 3], 'E%d' % ((c + 1) % 3)
                etab('d', c, Eb, ek)
                S.op('dve', lambda e, sl=sl, Eb=Eb: e.tensor_tensor(out=kdT[:, sl], in0=qk[:, sl], in1=Eb, op=ALU.mult), reads=['qk', ek], writes=['kdT'])
            S.barrier(pool=False)

            def ev_v(i, pb, pk):
                S.op('act', lambda e: e.copy(out=vtok[:, i, :], in_=pb[:, 0:256]), reads=[pk], writes=['vtok'])
            self.proj_tok(Wi, 1024 + h * 256, 256, ev_v)
            evs = []
            for vc in range(2):
                def ev_r(t0, tn, pb, pk, vc=vc):
                    S.op('act', lambda e: e.activation(out=rsT[:, vc, t0:t0 + tn], in_=pb[:, 0:tn], func=AF.Silu), reads=[pk], writes=['rsT'])
                evs.append(ev_r)
            self.proj_feat_multi(Wi, 2048 + h * 256, evs)
            S.barrier(pool=False)
            if blk == 0:
                S.op('dve', lambda e: e.memset(Sg, 0.0), writes=['Sg'])
            else:
                S.dma('sp', Sg, self.scr_gla[h], reads=['scr_gla'], writes=['Sg'])
            for c in range(NT):
                sl = slice(c * 128, (c + 1) * 128)
                cp_ = c % 2
                attm, kdtok, oT, sq, rs_, t1 = attm_[cp_], kdtok_[cp_], oT_[cp_], sq_[cp_], rs__[cp_], t1_[cp_]
                X = lambda nm: nm + str(cp_)
                pa, pak = self.bank()
                S.op('pe', lambda e, pa=pa, sl=sl: e.matmul(pa[:, 0:128], lhsT=ktil[:, sl], rhs=qtil[:, sl], start=True, stop=True), reads=['ktil', 'qtil'], writes=[pak])
                S.op('dve', lambda e, pa=pa, c=c: e.tensor_tensor(out=attm, in0=pa[:, 0:128], in1=(maskP if c < 8 else maskS), op=ALU.mult), reads=[pak, 'cst'], writes=[X('attm')])
                pt, ptk = self.bank()
                ptv = pt[:].bitcast(BF16)
                S.op('pe', lambda e, ptv=ptv, sl=sl: e.transpose(out=ptv[:, 0:128], in_=kdT[:, sl], identity=self.identb[:]), reads=['kdT', 'identb'], writes=[ptk])
                S.op('act', lambda e, ptv=ptv: e.copy(out=kdtok, in_=ptv[:, 0:128]), reads=[ptk], writes=[X('kdtok')])
                po = [self.lbank(2 * cp_), self.lbank(2 * cp_ + 1)]
                if c < 8:
                    S.op('act', lambda e: e.copy(out=Sbf[0], in_=Sg), reads=['Sg'], writes=['Sbf0'])
                    for vc in range(2):
                        pb, pk = po[vc]
                        S.op('pe', lambda e, pb=pb, vc=vc, c=c: e.matmul(pb[:, 0:128], lhsT=vtok[:, c, vc * 128:(vc + 1) * 128], rhs=attm, start=True, stop=False),
                             reads=['vtok', X('attm')], writes=[pk], signal=False)
                        S.op('pe', lambda e, pb=pb, vc=vc, sl=sl: e.matmul(pb[:, 0:128], lhsT=Sbf[0][:, vc * 128:(vc + 1) * 128], rhs=qtil[:, sl], start=False, stop=True),
                             reads=['Sbf0', 'qtil'], writes=[pk])
                    pS, pSk = self.bank()
                    S.op('pe', lambda e, pS=pS, c=c: e.matmul(pS[:, 0:256], lhsT=kdtok, rhs=vtok[:, c, :], start=True, stop=True), reads=[X('kdtok'), 'vtok'], writes=[pSk])
                    S.op('dve', lambda e, pS=pS, c=c: e.scalar_tensor_tensor(out=Sg, in0=Sg, scalar=ebl[:, c:c + 1], in1=pS[:, 0:256], op0=ALU.mult, op1=ALU.add),
                         reads=[pSk, 'ebl', 'Sg', 'Sbf0'], writes=['Sg'])
                else:
                    for vc in range(2):
                        pb, pk = po[vc]
                        S.op('pe', lambda e, pb=pb, vc=vc, c=c: e.matmul(pb[:, 0:128], lhsT=vtok[:, c, vc * 128:(vc + 1) * 128], rhs=attm, start=True, stop=False),
                             reads=['vtok', X('attm')], writes=[pk], signal=False)
                    for j0_ in range(3):
                        S.dma('sp', Sin[j0_ % 4], self.sgla[j0_, h], writes=['Sin%d' % (j0_ % 4)])
                    for j in range(16):
                        r = j % 4
                        if j + 3 < 16:
                            S.dma('sp', Sin[(j + 3) % 4], self.sgla[j + 3, h], writes=['Sin%d' % ((j + 3) % 4)])
                        S.op('act', lambda e, r=r: e.copy(out=Sbf[r], in_=Sin[r]), reads=['Sin%d' % r], writes=['Sbf%d' % r])
                        for vc in range(2):
                            pb, pk = po[vc]
                            S.op('pe', lambda e, pb=pb, vc=vc, j=j, r=r: e.matmul(pb[:, 8 * j:8 * j + 8], lhsT=Sbf[r][:, vc * 128:(vc + 1) * 128], rhs=qtil[:, 1024 + 8 * j:1032 + 8 * j], start=False, stop=(j == 15)),
                                 reads=['Sbf%d' % r, 'qtil'], writes=[pk], signal=True)
                        S.op('dve', lambda e, j=j, r=r: e.tensor_scalar(out=km[r], in0=kdtok, scalar1=selS[:, j:j + 1], scalar2=None, op0=ALU.mult), reads=[X('kdtok'), 'cst'], writes=['km%d' % r])
                        pS, pSk = self.bank()
                        S.op('pe', lambda e, pS=pS, r=r, c=c: e.matmul(pS[:, 0:256], lhsT=km[r], rhs=vtok[:, c, :], start=True, stop=True), reads=['km%d' % r, 'vtok'], writes=[pSk])
                        S.op('dve', lambda e, pS=pS, j=j, r=r: e.scalar_tensor_tensor(out=Sout[r], in0=Sin[r], scalar=ebls[:, j:j + 1], in1=pS[:, 0:256], op0=ALU.mult, op1=ALU.add),
                             reads=[pSk, 'ebl', 'Sin%d' % r], writes=['Sout%d' % r])
                        S.dma('sp', self.o_gla_s[j, h], Sout[r], reads=['Sout%d' % r], writes=['o_gla_s'])
                pss, pssk = self.bank()
                for vc in range(2):
                    pb, pk = po[vc]
                    S.op('act', lambda e, pb=pb, vc=vc: e.copy(out=oT[:, vc, :], in_=pb[:, 0:128]), reads=[pk], writes=[X('oT%d' % vc)])
                    S.op('act', lambda e, vc=vc: e.activation(out=sq[:, vc, :], in_=oT[:, vc, :], func=AF.Square), reads=[X('oT%d' % vc)], writes=[X('sq%d' % vc)])
                    S.op('pe', lambda e, pss=pss, vc=vc: e.matmul(pss[:, 0:128], lhsT=onesb, rhs=sq[:, vc, :], start=(vc == 0), stop=(vc == 1)), reads=[X('sq%d' % vc), 'onesb'], writes=[pssk], signal=(vc == 1))
                S.op('dve', lambda e, pss=pss: e.tensor_scalar(out=rs_, in0=pss[:, 0:128], scalar1=1.0 / 256, scalar2=EPS, op0=ALU.mult, op1=ALU.add), reads=[pssk], writes=[X('rs_')])
                S.op('act', lambda e: e.activation(out=rs_, in_=rs_, func=AF.Sqrt), reads=[X('rs_')], writes=[X('rs_')])
                S.op('dve', lambda e: e.reciprocal(out=rs_, in_=rs_), reads=[X('rs_')], writes=[X('rs_')])
                for vc in range(2):
                    S.op('dve', lambda e, vc=vc: e.scalar_tensor_tensor(out=t1, in0=oT[:, vc, :], scalar=self.col("glang", 2 * h + vc), in1=rs_, op0=ALU.mult, op1=ALU.mult),
                         reads=[X('oT%d' % vc), X('rs_'), 'cols'], writes=[X('t1')])
                    S.op('dve', lambda e, vc=vc, sl=sl: e.tensor_tensor(out=ogT[:, vc, sl], in0=t1, in1=rsT[:, vc, sl], op=ALU.mult), reads=[X('t1'), 'rsT'], writes=['srcT'])
            if blk == self.dbg.get('nblk', NBLK) - 1:
                S.dma('sp', self.o_gla_p[h], Sg, reads=['Sg'], writes=['o_gla_p'])
            else:
                S.dma('sp', self.scr_gla[h], Sg, reads=['Sg'], writes=['scr_gla'])
            self.out_proj(self.w_ab_out, h * 256, 2, ogT)
            S.barrier(pool=False)
        if self.dbg.get('s5', True):
            self.s5(blk)

    def s5_setup(self):
        S = self.S
        PW = self.PW
        o = [0]

        def alloc(n):
            r = o[0]
            o[0] += n
            return r
        T_ = lambda: self.af(alloc(32), 32)
        dt, mag, th, tw, sn, cs, t0, t1, t2, fre, fim = [T_() for _ in range(11)]
        lr = self.col("lamre", 0, 32)
        li = self.col("lamim", 0, 32)
        PI = float(np.pi)
        n = [0]

        def dv(fn, r, w):
            S.op('dve', fn, reads=r, writes=w)

        def ac(fn, r, w):
            S.op('act', fn, reads=r, writes=w)
        ac(lambda e: e.activation(out=dt, in_=self.col("logdt", 0, 32), func=AF.Exp), ['cols'], ['dt'])
        dv(lambda e: e.tensor_tensor(out=t0, in0=lr, in1=dt, op=ALU.mult), ['dt', 'cols'], ['t0'])
        ac(lambda e: e.activation(out=mag, in_=t0, func=AF.Exp), ['t0'], ['mag'])
        dv(lambda e: e.tensor_tensor(out=th, in0=li, in1=dt, op=ALU.mult), ['dt', 'cols'], ['th'])

        def wrapped_sin(dst, shift, key):
            dv(lambda e: e.tensor_scalar(out=tw, in0=th, scalar1=shift, scalar2=None, op0=ALU.add), ['th'], ['tw'])
            for _ in range(4):
                dv(lambda e: e.tensor_scalar(out=t1, in0=tw, scalar1=PI, scalar2=-2 * PI, op0=ALU.is_gt, op1=ALU.mult), ['tw'], ['t1'])
                dv(lambda e: e.tensor_tensor(out=tw, in0=tw, in1=t1, op=ALU.add), ['t1', 'tw'], ['tw'])
            ac(lambda e: e.activation(out=dst, in_=tw, func=AF.Sin), ['tw'], [key])
        wrapped_sin(sn, 0.0, 'sn')
        wrapped_sin(cs, PI / 2, 'cs')
        are, aim, nim = PW[:, 0, 0, :], PW[:, 1, 0, :], PW[:, 2, 0, :]
        dv(lambda e: e.tensor_tensor(out=are, in0=mag, in1=cs, op=ALU.mult), ['mag', 'cs'], ['PW'])
        dv(lambda e: e.tensor_tensor(out=aim, in0=mag, in1=sn, op=ALU.mult), ['mag', 'sn'], ['PW'])
        dv(lambda e: e.tensor_tensor(out=t0, in0=lr, in1=lr, op=ALU.mult), ['cols'], ['t0'])
        dv(lambda e: e.tensor_tensor(out=t1, in0=li, in1=li, op=ALU.mult), ['cols'], ['t1'])
        dv(lambda e: e.tensor_tensor(out=t0, in0=t0, in1=t1, op=ALU.add), ['t0', 't1'], ['t0'])
        dv(lambda e: e.reciprocal(out=t0, in_=t0), ['t0'], ['t0'])
        dv(lambda e: e.tensor_scalar(out=t1, in0=are, scalar1=-1.0, scalar2=None, op0=ALU.add), ['PW'], ['t1'])
        dv(lambda e: e.tensor_tensor(out=fre, in0=t1, in1=lr, op=ALU.mult), ['t1', 'cols'], ['fre'])
        dv(lambda e: e.tensor_tensor(out=t2, in0=aim, in1=li, op=ALU.mult), ['PW', 'cols'], ['t2'])
        dv(lambda e: e.tensor_tensor(out=fre, in0=fre, in1=t2, op=ALU.add), ['fre', 't2'], ['fre'])
        dv(lambda e: e.tensor_tensor(out=fre, in0=fre, in1=t0, op=ALU.mult), ['fre', 't0'], ['fre'])
        dv(lambda e: e.tensor_tensor(out=fim, in0=aim, in1=lr, op=ALU.mult), ['PW', 'cols'], ['fim'])
        dv(lambda e: e.tensor_tensor(out=t2, in0=t1, in1=li, op=ALU.mult), ['t1', 'cols'], ['t2'])
        dv(lambda e: e.tensor_tensor(out=fim, in0=fim, in1=t2, op=ALU.subtract), ['fim', 't2'], ['fim'])
        dv(lambda e: e.tensor_tensor(out=fim, in0=fim, in1=t0, op=ALU.mult), ['fim', 't0'], ['fim'])

        def cmul(dst, x, y):
            xr, xi = PW[:, 0, x, :], PW[:, 1, x, :]
            yr, yi = PW[:, 0, y, :], PW[:, 1, y, :]
            dr, di_ = PW[:, 0, dst, :], PW[:, 1, dst, :]
            dv(lambda e: e.tensor_tensor(out=t0, in0=xr, in1=yr, op=ALU.mult), ['PW'], ['t0'])
            dv(lambda e: e.tensor_tensor(out=t1, in0=xi, in1=yi, op=ALU.mult), ['PW'], ['t1'])
            dv(lambda e: e.tensor_tensor(out=t2, in0=xr, in1=yi, op=ALU.mult), ['PW'], ['t2'])
            dv(lambda e: e.tensor_tensor(out=tw, in0=xi, in1=yr, op=ALU.mult), ['PW'], ['tw'])
            dv(lambda e: e.tensor_tensor(out=dr, in0=t0, in1=t1, op=ALU.subtract), ['t0', 't1'], ['PW'])
            dv(lambda e: e.tensor_tensor(out=di_, in0=t2, in1=tw, op=ALU.add), ['t2', 'tw'], ['PW'])
        for k in range(1, 8):
            cmul(k, k - 1, 0)
        for k in range(8, 15):
            cmul(k, k - 1 if k > 8 else 7, k - 1 if k > 8 else 7)
        dv(lambda e: e.tensor_scalar(out=PW[:, 2, :, :], in0=PW[:, 1, :, :], scalar1=-1.0, scalar2=None, op0=ALU.mult), ['PW'], ['PW'])
        dv(lambda e: e.memset(self.S5H[:], 0.0), [], K3('S5H'))
        bre = [self.af(alloc(128), 128) for _ in range(2)]
        bim = [self.af(alloc(128), 128) for _ in range(2)]
        o1_ = [self.af(alloc(128), 128) for _ in range(2)]
        ob = [self.ab(alloc(128), 256) for _ in range(2)]
        obT = [self.ab(alloc(128), 256) for _ in range(2)]
        for q in range(32):
            r = q % 2
            S.dma('sp', bre[r], self.bpad[q, 0], writes=['bre%d' % r])
            S.dma('sp', bim[r], self.bpad[q, 1], writes=['bim%d' % r])
            fr, fi = fre[:, q:q + 1], fim[:, q:q + 1]
            dv(lambda e: e.tensor_scalar(out=o1_[r], in0=bim[r], scalar1=fi, scalar2=-1.0, op0=ALU.mult, op1=ALU.mult), ['bim%d' % r, 'fim'], ['o1_%d' % r])
            dv(lambda e: e.scalar_tensor_tensor(out=ob[r][:, 0:128], in0=bre[r], scalar=fr, in1=o1_[r], op0=ALU.mult, op1=ALU.add), ['bre%d' % r, 'o1_%d' % r, 'fre'], ['ob%d' % r])
            dv(lambda e: e.tensor_scalar(out=o1_[r], in0=bre[r], scalar1=fi, scalar2=None, op0=ALU.mult), ['bre%d' % r, 'fim', 'ob%d' % r], ['o1_%d' % r])
            dv(lambda e: e.scalar_tensor_tensor(out=ob[r][:, 128:256], in0=bim[r], scalar=fr, in1=o1_[r], op0=ALU.mult, op1=ALU.add), ['bim%d' % r, 'o1_%d' % r, 'fre'], ['ob%d' % r])
            pt, ptk = self.bank()
            ptv = pt[:].bitcast(BF16)
            for c in range(2):
                S.op('pe', lambda e: e.transpose(out=ptv[:, c * 128:(c + 1) * 128], in_=ob[r][:, c * 128:(c + 1) * 128], identity=self.identb[:]), reads=['ob%d' % r, 'identb'], writes=[ptk], signal=(c == 1))
            S.op('act', lambda e: e.copy(out=obT[r], in_=ptv[:, 0:256]), reads=[ptk], writes=['obT%d' % r])
            S.dma('sp', self.scr_bbT[q].rearrange("c p n -> p c n"), obT[r].rearrange("p (c n) -> p c n", c=2), reads=['obT%d' % r], writes=['scr_bbT'])
        S.barrier()

    def s5(self, blk):
        S = self.S
        NT = self.NT
        T = self.T
        NC = T // 8
        PW = self.PW
        last = (blk == self.dbg.get('nblk', NBLK) - 1)
        S.barrier()
        uT = self.ab(9216, 8 * 1152).rearrange("p (c t) -> p c t", c=8)
        ygT = self.ab(9216 + 4608, 8 * 1152).rearrange("p (c t) -> p c t", c=8)
        yy = [self.af(9216 + 9216 + i * 512, 512) for i in range(2)]
        y2 = [self.af(9216 + 9216 + 1024 + i * 512, 512) for i in range(2)]
        self.rtmp = [self.af(9216 + 9216 + 2048 + i * 256, 256) for i in range(2)]
        self.rt_rr = 0
        for b2 in range(0, 8, 2):
            evs = []
            for b in (b2, b2 + 1):
                def ev_u(t0, tn, pb, pk, b=b):
                    S.op('act', lambda e: e.copy(out=uT[:, b, t0:t0 + tn], in_=pb[:, 0:tn]), reads=[pk], writes=['uT'])
                evs.append(ev_u)
            self.proj_feat_multi(self.w_ab_in, 3088 + b2 * 128, evs)
        S.barrier()
        o = [0]

        def alloc(n):
            r = o[0]
            o[0] += n
            assert o[0] <= 9216, o[0]
            return r
        groups = self.groups()
        H3 = lambda a: a[:, 0:T].rearrange("p (c i) -> p c i", i=8)
        B = []
        for sl_ in range(2):
            d = {}
            d['Hre'] = self.af(alloc(1152), 1152)
            d['Him'] = self.af(alloc(1152), 1152)
            d['Hb'] = [self.ab(alloc(576), 1152) for _ in range(2)]
            d['G'] = [[self.af(alloc(128), 128) for _ in range(2)] for _ in range(2)]
            d['hin'] = [self.af(alloc(144), 144) for _ in range(2)]
            d['bbT'] = self.ab(alloc(128), 256).rearrange("p (c n) -> p c n", c=2)
            d['cp'] = self.ab(alloc(128), 256).rearrange("p (c n) -> p c n", c=2)
            d['FS'] = self.af(alloc(32), 32).rearrange("p (c j) -> p c j", c=2)
            d['hr'], d['hi'] = H3(d['Hre']), H3(d['Him'])
            d['s'] = str(sl_)
            B.append(d)

        def cmac2(items):
            ops = [[], [], [], []]
            for (dre, dim, sre, sim, k, key_s, key_d, q) in items:
                a_r, a_i, n_i = PW[:, 0, k, q:q + 1], PW[:, 1, k, q:q + 1], PW[:, 2, k, q:q + 1]
                kr, ki = key_d + 'r', key_d + 'i'
                ops[0].append((lambda e, dre=dre, sre=sre, a_r=a_r: e.scalar_tensor_tensor(out=dre, in0=sre, scalar=a_r, in1=dre, op0=ALU.mult, op1=ALU.add), K3(key_s) + [key_d, kr, 'PW'], [kr]))
                ops[1].append((lambda e, dim=dim, sim=sim, a_r=a_r: e.scalar_tensor_tensor(out=dim, in0=sim, scalar=a_r, in1=dim, op0=ALU.mult, op1=ALU.add), K3(key_s) + [key_d, ki, 'PW'], [ki]))
                ops[2].append((lambda e, dre=dre, sim=sim, n_i=n_i: e.scalar_tensor_tensor(out=dre, in0=sim, scalar=n_i, in1=dre, op0=ALU.mult, op1=ALU.add), K3(key_s) + [key_d, kr, 'PW'], [kr]))
                ops[3].append((lambda e, dim=dim, sre=sre, a_i=a_i: e.scalar_tensor_tensor(out=dim, in0=sre, scalar=a_i, in1=dim, op0=ALU.mult, op1=ALU.add), K3(key_s) + [key_d, ki, 'PW'], [ki]))
            for grp in ops:
                for (fn, r_, w_) in grp:
                    S.op('dve', fn, reads=r_, writes=w_)

        for qq in range(0, 32, 2):
            P = [(qq + j, B[j]) for j in range(2)]
            b = qq // 4
            for q, d in P:
                sk = d['s']
                S.dma('sp', d['bbT'], self.scr_bbT[q].rearrange("c p n -> p c n"), reads=['scr_bbT'], writes=['bbT' + sk])
                S.dma('pool', d['cp'], self.cpad[q].rearrange("c p n -> p c n"), writes=['cp' + sk])
                if NT == 9:
                    S.dma('sp', d['hin'][0][:, 128:144], self.s5in[q, 0], writes=K3('hin' + sk))
                    S.dma('sp', d['hin'][1][:, 128:144], self.s5in[q, 1], writes=K3('hin' + sk))
            for q, d in P:
                sk = d['s']
                for (t0, tn) in groups:
                    for c, Hx in ((0, d['Hre']), (1, d['Him'])):
                        pb, pk = self.bank()
                        S.op('pe', lambda e: e.matmul(pb[:, 0:tn], lhsT=d['bbT'][:, c, :], rhs=uT[:, b, t0:t0 + tn], start=True, stop=True), reads=['bbT' + sk, 'uT'], writes=[pk])
                        S.op('act', lambda e: e.copy(out=Hx[:, t0:t0 + tn], in_=pb[:, 0:tn]), reads=[pk], writes=K3('H' + sk))
            for i in range(1, 8):
                cmac2([(d['hr'][:, :, i], d['hi'][:, :, i], d['hr'][:, :, i - 1], d['hi'][:, :, i - 1], 0, 'H' + d['s'], 'H' + d['s'], q) for q, d in P])
            for q, d in P:
                sk = d['s']
                g0 = d['G'][0]
                S.op('act', lambda e: e.copy(out=g0[0], in_=d['hr'][:, 0:128, 7]), reads=K3('H' + sk), writes=K3('G0' + sk))
                S.op('act', lambda e: e.copy(out=g0[1], in_=d['hi'][:, 0:128, 7]), reads=K3('H' + sk), writes=K3('G0' + sk))
            if blk > 0:
                cmac2([(d['G'][0][0][:, 0:1], d['G'][0][1][:, 0:1], self.S5H[:, 0, q:q + 1], self.S5H[:, 1, q:q + 1], 7, 'S5H', 'G0' + d['s'], q) for q, d in P])
            cur = 0
            for k in range(7):
                s_ = 1 << k
                for q, d in P:
                    sk = d['s']
                    src, dst = d['G'][cur], d['G'][1 - cur]
                    S.op('act', lambda e: e.copy(out=dst[0], in_=src[0]), reads=K3('G%d' % cur + sk), writes=K3('G%d' % (1 - cur) + sk))
                    S.op('act', lambda e: e.copy(out=dst[1], in_=src[1]), reads=K3('G%d' % cur + sk), writes=K3('G%d' % (1 - cur) + sk))
                cmac2([(d['G'][1 - cur][0][:, s_:128], d['G'][1 - cur][1][:, s_:128], d['G'][cur][0][:, 0:128 - s_], d['G'][cur][1][:, 0:128 - s_], 7 + k, 'G%d' % cur + d['s'], 'G%d' % (1 - cur) + d['s'], q) for q, d in P])
                cur = 1 - cur
            for q, d in P:
                sk = d['s']
                Sf = d['G'][cur]
                kf = 'G%d' % cur + sk
                hin = d['hin']
                hp_re, hp_im = self.S5H[:, 0, q:q + 1], self.S5H[:, 1, q:q + 1]
                S.op('act', lambda e: e.copy(out=hin[0][:, 0:1], in_=hp_re), reads=K3('S5H'), writes=K3('hin' + sk))
                S.op('act', lambda e: e.copy(out=hin[1][:, 0:1], in_=hp_im), reads=K3('S5H'), writes=K3('hin' + sk))
                S.op('act', lambda e: e.copy(out=hin[0][:, 1:128], in_=Sf[0][:, 0:127]), reads=K3(kf), writes=K3('hin' + sk))
                S.op('act', lambda e: e.copy(out=hin[1][:, 1:128], in_=Sf[1][:, 0:127]), reads=K3(kf), writes=K3('hin' + sk))
                S.op('act', lambda e: e.copy(out=hp_re, in_=Sf[0][:, 127:128]), reads=K3(kf) + K3('hin' + sk), writes=K3('S5H'))
                S.op('act', lambda e: e.copy(out=hp_im, in_=Sf[1][:, 127:128]), reads=K3(kf) + K3('hin' + sk), writes=K3('S5H'))
            for i in range(8):
                cmac2([(d['hr'][:, :, i], d['hi'][:, :, i], d['hin'][0][:, 0:NC], d['hin'][1][:, 0:NC], i, 'hin' + d['s'], 'H' + d['s'], q) for q, d in P])
            for q, d in P:
                sk = d['s']
                pb4 = q % 4
                if NT == 9:
                    S.op('act', lambda e: e.copy(out=d['FS'][:, 0, :], in_=d['hr'][:, 128:144, 7]), reads=K3('H' + sk), writes=['FS' + sk])
                    S.op('act', lambda e: e.copy(out=d['FS'][:, 1, :], in_=d['hi'][:, 128:144, 7]), reads=K3('H' + sk), writes=['FS' + sk])
                    S.dma('sp', self.o_s5_s[q].rearrange("c p j -> p c j"), d['FS'], reads=['FS' + sk], writes=['o_s5_s'])
                S.op('act', lambda e: e.copy(out=d['Hb'][0][:, 0:T], in_=d['Hre'][:, 0:T]), reads=K3('H' + sk), writes=['Hb' + sk])
                S.op('act', lambda e: e.activation(out=d['Hb'][1][:, 0:T], in_=d['Him'][:, 0:T], func=AF.Copy, scale=-1.0), reads=K3('H' + sk), writes=['Hb' + sk])
                for gi, (t0, tn) in enumerate(groups):
                    lb, lk = self.lbank(gi)
                    S.op('pe', lambda e: e.matmul(lb[:, 0:tn], lhsT=d['cp'][:, 0, :], rhs=d['Hb'][0][:, t0:t0 + tn], start=(pb4 == 0), stop=False), reads=['cp' + sk, 'Hb' + sk], writes=[lk], signal=False)
                    S.op('pe', lambda e: e.matmul(lb[:, 0:tn], lhsT=d['cp'][:, 1, :], rhs=d['Hb'][1][:, t0:t0 + tn], start=False, stop=(pb4 == 3)), reads=['cp' + sk, 'Hb' + sk], writes=[lk])
            if P[1][0] % 4 == 3:
                for gi, (t0, tn) in enumerate(groups):
                    lb, lk = self.lbank(gi)
                    y_, y2_ = yy[gi % 2][:, 0:tn], y2[gi % 2][:, 0:tn]
                    ky, ky2 = 'yy%d' % (gi % 2), 'y2%d' % (gi % 2)
                    S.op('dve', lambda e: e.scalar_tensor_tensor(out=y_, in0=uT[:, b, t0:t0 + tn], scalar=self.col("s5d", b), in1=lb[:, 0:tn], op0=ALU.mult, op1=ALU.add), reads=[lk, 'uT', 'cols'], writes=[ky])
                    S.op('dve', lambda e: e.tensor_tensor(out=y2_, in0=y_, in1=y_, op=ALU.mult), reads=[ky], writes=[ky2])
                    S.op('dve', lambda e: e.tensor_scalar(out=y2_, in0=y2_, scalar1=0.044715, scalar2=1.0, op0=ALU.mult, op1=ALU.add), reads=[ky2], writes=[ky2])
                    S.op('dve', lambda e: e.tensor_tensor(out=y2_, in0=y2_, in1=y_, op=ALU.mult), reads=[ky, ky2], writes=[ky2])
                    S.op('act', lambda e: e.activation(out=y2_, in_=y2_, func=AF.Sigmoid, scale=1.5957691216057308), reads=[ky2], writes=[ky2])
                    S.op('dve', lambda e: e.tensor_tensor(out=ygT[:, b, t0:t0 + tn], in0=y_, in1=y2_, op=ALU.mult), reads=[ky, ky2], writes=['ygT'])
        if last:
            S.dma('sp', self.o_s5_p.rearrange("c p q -> p c q"), self.S5H[:], reads=K3('S5H'), writes=['o_s5_p'])
        S.barrier()
        for oc in range(8):
            wv, wk = self.load_w(self.w_glu, 0, 8, oc * 128, 128)
            for gi, (t0, tn) in enumerate(groups):
                pb, pk = self.bank()
                for k in range(8):
                    S.op('pe', lambda e: e.matmul(pb[:, 0:tn], lhsT=wv[:, k, :], rhs=ygT[:, k, t0:t0 + tn], start=(k == 0), stop=(k == 7)), reads=[wk, 'ygT'], writes=[pk], signal=(k == 7))
                y2_ = y2[gi % 2][:, 0:tn]
                ky2 = 'y2%d' % (gi % 2)
                S.op('act', lambda e: e.activation(out=y2_, in_=pb[:, 0:tn], func=AF.Sigmoid, bias=self.col("s5bglu", oc), scale=1.0), reads=[pk, 'cols'], writes=[ky2])
                S.op('dve', lambda e: e.tensor_tensor(out=uT[:, oc, t0:t0 + tn], in0=ygT[:, oc, t0:t0 + tn], in1=y2_, op=ALU.mult), reads=[ky2, 'ygT'], writes=['srcT'])
        self.out_proj(self.w_ab_out, 1024, 8, uT)
        S.barrier()

    def mixer_ml(self, blk):
        S = self.S
        self.gen_banks = [0, 1, 2]
        NT = self.NT
        T = self.T
        W = self.w_ml_in
        last = (blk == self.dbg.get('nblk', NBLK) - 1)
        o = [9216]

        def alloc(n):
            r = o[0]
            o[0] += n
            assert o[0] <= self.ARW, o[0]
            return r
        E1 = self.af(alloc(1152), 1152)
        A0 = alloc(2304)
        ipT = self.af(A0, 1152)
        cum = self.af(A0 + 1152, 1152)
        qT = self.ab(A0, 2 * 1152).rearrange("p (c t) -> p c t", c=2)
        kT = self.ab(A0 + 1152, 2 * 1152).rearrange("p (c t) -> p c t", c=2)
        vtok = self.ab(alloc(NT * 256), NT * 512).rearrange("p (i v) -> p i v", i=NT)
        osT = self.ab(alloc(2304), 4 * 1152).rearrange("p (c t) -> p c t", c=4)
        CT = self.af(alloc(1024), 1024).rearrange("p (c v) -> p c v", c=2)
        CTbf = self.ab(alloc(512), 1024).rearrange("p (c v) -> p c v", c=2)
        E2C = self.af(alloc(36), 36).rearrange("p (c h) -> p c h", h=4)
        E3C = self.af(alloc(36), 36).rearrange("p (c h) -> p c h", h=4)
        U = [self.af(alloc(128), 128) for _ in range(6)]
        Wm = self.af(alloc(128), 128)
        attm = self.ab(alloc(64), 128)
        qtl = self.ab(alloc(128), 256).rearrange("p (c t) -> p c t", c=2)
        kdtok = self.ab(alloc(128), 256)
        nrep = self.ab(alloc(128), 256).rearrange("p (c t) -> p c t", c=2)
        rden = self.af(alloc(128), 128)
        sq = self.ab(alloc(256), 512).rearrange("p (c t) -> p c t", c=4)
        rs_ = self.af(alloc(128), 128)
        t1 = self.af(alloc(128), 128)
        Cin = self.af(alloc(512), 512)
        Cout = self.af(alloc(512), 512)
        Cbf = self.ab(alloc(256), 512)
        km = self.ab(alloc(64), 128)
        N0 = self.af(alloc(32), 32).rearrange("p (c j) -> p c j", c=2)
        NOUT = self.af(alloc(32), 32).rearrange("p (c j) -> p c j", c=2)
        colS = self.af(alloc(32), 32)
        self.rtmp = [Cin[:, 0:256], Cin[:, 256:512]]
        self.rt_rr = 0
        maskP = self.cst[:, 128:256]
        maskS = self.cst[:, 256:384]
        selS = self.cst[:, 384:400]
        onesb = self.onesb[:]
        id4 = self.cst[0:4, 0:4]
        R = lambda a: a[0:4, :]
        mrun = self.sm3[0:4, 0:1]
        MS = self.sm3[0:4, 1:9]
        BLS = self.sm3[0:4, 9:17]
        M0 = self.sm3[0:4, 17:33]
        MFS = self.sm3[0:4, 33:49]
        DEC = self.sm3[0:4, 49:65]
        EM0 = self.sm3[0:4, 65:81]
        tt = self.sm3[0:4, 81:97]
        EMF = self.sm3[0:4, 97:98]
        ncol = self.sm3[:, 98:100]
        emfcol = self.sm3[:, 100:101]
        negbf = self.sm3[0:4, 101:102]
        S.barrier(pool=False)
        S.op('dve', lambda e: e.tensor_scalar(out=negbf, in0=self.col("mlbf")[0:4, :], scalar1=-1.0, scalar2=None, op0=ALU.mult), reads=['cols'], writes=['negbf'])
        if blk == 0:
            S.op('dve', lambda e: e.memset(mrun, 0.0), writes=['mrun'])

        def ev_i(t0, tn, pb, pk):
            S.op('act', lambda e: e.activation(out=ipT[0:4, t0:t0 + tn], in_=pb[0:4, 0:tn], func=AF.Identity, bias=self.col("mlbi")[0:4, :], scale=1.0), reads=[pk, 'cols'], writes=['ipT'])
        self.proj_feat(W, 6144, 4, ev_i)

        def ev_f(t0, tn, pb, pk):
            S.op('act', lambda e: e.activation(out=E1[0:4, t0:t0 + tn], in_=pb[0:4, 0:tn], func=AF.Exp, bias=negbf, scale=-1.0), reads=[pk, 'negbf'], writes=['E1'])
        self.proj_feat(W, 6148, 4, ev_f)
        S.op('act', lambda e: e.activation(out=E1[0:4, 0:T], in_=E1[0:4, 0:T], func=AF.Ln, bias=1.0, scale=1.0), reads=['E1'], writes=['E1'])
        S.op('dve', lambda e: e.tensor_tensor_scan(out=cum[0:4, 0:T], data0=self.rst[0:4, 0:T], data1=E1[0:4, 0:T], initial=0.0, op0=ALU.mult, op1=ALU.add),
             reads=['E1', 'rst'], writes=['cum'])
        if NT == 9:
            S.dma('sp', M0, self.sm_in, writes=['M0'])
        for c in range(NT):
            sl = slice(c * 128, (c + 1) * 128)
            u1, u2, u3, e2, e3, ux = [R(a) for a in U]
            v3 = lambda a: a.rearrange("p (j i) -> p j i", i=8)
            if c < 8:
                if c == 0:
                    S.op('dve', lambda e: e.tensor_copy(out=u1, in_=cum[0:4, sl]), reads=['cum'], writes=['u1'])
                else:
                    S.op('dve', lambda e, sl=sl, c=c: e.tensor_scalar(out=u1, in0=cum[0:4, sl], scalar1=cum[0:4, c * 128 - 1:c * 128], scalar2=None, op0=ALU.subtract), reads=['cum'], writes=['u1'])
                S.op('dve', lambda e, sl=sl: e.tensor_tensor(out=u2, in0=ipT[0:4, sl], in1=u1, op=ALU.add), reads=['ipT', 'u1'], writes=['u2'])
                S.op('dve', lambda e: e.tensor_scalar(out=u3, in0=u2, scalar1=u1[:, 127:128], scalar2=None, op0=ALU.subtract), reads=['u2', 'u1'], writes=['u3'])
                S.op('dve', lambda e, c=c: e.tensor_reduce(out=MS[:, c:c + 1], in_=u3, axis=mybir.AxisListType.X, op=ALU.max), reads=['u3'], writes=['MS'])
                S.op('dve', lambda e, c=c: e.tensor_copy(out=BLS[:, c:c + 1], in_=u1[:, 127:128]), reads=['u1'], writes=['BLS'])
                S.op('dve', lambda e, c=c: e.tensor_tensor(out=tt[:, 0:1], in0=mrun, in1=BLS[:, c:c + 1], op=ALU.subtract), reads=['mrun', 'BLS'], writes=['tt'])
                S.op('dve', lambda e, c=c: e.tensor_tensor(out=mrun, in0=tt[:, 0:1], in1=MS[:, c:c + 1], op=ALU.max), reads=['tt', 'MS'], writes=['mrun'])
            else:
                S.op('dve', lambda e, sl=sl: e.tensor_copy(out=u1, in_=cum[0:4, sl]), reads=['cum'], writes=['u1'])
                S.op('dve', lambda e, sl=sl: e.tensor_tensor(out=u2, in0=ipT[0:4, sl], in1=u1, op=ALU.add), reads=['ipT', 'u1'], writes=['u2'])
                S.op('dve', lambda e: e.tensor_tensor(out=v3(u3), in0=v3(u2), in1=v3(u1)[:, :, 7:8].to_broadcast([4, 16, 8]), op=ALU.subtract), reads=['u2', 'u1'], writes=['u3'])
                S.op('dve', lambda e: e.tensor_reduce(out=MFS, in_=v3(u3), axis=mybir.AxisListType.X, op=ALU.max), reads=['u3'], writes=['MFS'])
                S.op('dve', lambda e: e.tensor_tensor(out=tt, in0=M0, in1=v3(u1)[:, :, 7], op=ALU.subtract), reads=['M0', 'u1'], writes=['tt'])
                S.op('dve', lambda e: e.tensor_tensor(out=MFS, in0=MFS, in1=tt, op=ALU.max), reads=['tt', 'MFS'], writes=['MFS'])
                S.op('dve', lambda e: e.tensor_tensor(out=tt, in0=tt, in1=MFS, op=ALU.subtract), reads=['tt', 'MFS'], writes=['tt'])
                S.op('act', lambda e: e.activation(out=DEC, in_=tt, func=AF.Exp), reads=['tt'], writes=['DEC'])
                S.op('act', lambda e: e.activation(out=EM0, in_=M0, func=AF.Exp), reads=['M0'], writes=['EM0'])
                S.op('dve', lambda e: e.tensor_tensor(out=v3(u3), in0=v3(u3), in1=MFS.rearrange("p (j i) -> p j i", i=1).to_broadcast([4, 16, 8]), op=ALU.subtract), reads=['u3', 'MFS'], writes=['u3'])
                S.dma('sp', self.o_ml_m_s, MFS, reads=['MFS'], writes=['o_ml_m_s'])
            S.op('act', lambda e, sl=sl: e.activation(out=E1[0:4, sl], in_=u1, func=AF.Exp, scale=-1.0), reads=['u1'], writes=['E1'])
            S.op('act', lambda e: e.activation(out=e2, in_=u2, func=AF.Exp), reads=['u2'], writes=['e2'])
            S.op('act', lambda e: e.activation(out=e3, in_=u3, func=AF.Exp), reads=['u3'], writes=['e3'])
            for (src, dst, nm) in ((e2, E2C, 'e2'), (e3, E3C, 'e3')):
                pb, pk = self.bank()
                S.op('pe', lambda e, pb=pb, src=src: e.matmul(pb[:, 0:4], lhsT=src, rhs=id4, start=True, stop=True), reads=[nm, 'cst'], writes=[pk])
                S.op('act', lambda e, pb=pb, dst=dst, c=c: e.copy(out=dst[:, c, :], in_=pb[:, 0:4]), reads=[pk], writes=['EC'])
        if last:
            S.op('act', lambda e: e.activation(out=EMF, in_=mrun, func=AF.Exp, scale=-1.0), reads=['mrun'], writes=['EMF'])
            S.dma('sp', self.o_ml_m_p, mrun, reads=['mrun'], writes=['o_ml_m_p'])
        S.barrier(pool=False)

        for h in self.dbg.get('heads', range(4)):
            selh = self.cst[0:4, 528 + h * 128:528 + (h + 1) * 128]
            evq, evk = [], []
            for dc in range(2):
                def ev_q(t0, tn, pb, pk, dc=dc):
                    S.op('act', lambda e: e.copy(out=qT[:, dc, t0:t0 + tn], in_=pb[:, 0:tn]), reads=[pk], writes=['qT'])
                evq.append(ev_q)

                def ev_k(t0, tn, pb, pk, dc=dc):
                    S.op('act', lambda e: e.activation(out=kT[:, dc, t0:t0 + tn], in_=pb[:, 0:tn], func=AF.Copy, scale=1.0 / 16), reads=[pk], writes=['kT'])
                evk.append(ev_k)
            self.proj_feat_multi(W, h * 256, evq)
            self.proj_feat_multi(W, 1024 + h * 256, evk)
            for half in range(2):
                def ev_v(i, pb, pk, half=half):
                    S.op('act', lambda e: e.copy(out=vtok[:, i, half * 256:(half + 1) * 256], in_=pb[:, 0:256]), reads=[pk], writes=['vtok'])
                self.proj_tok(W, 2048 + h * 512 + half * 256, 256, ev_v)
            for v2 in range(0, 4, 2):
                evs = []
                for vc in (v2, v2 + 1):
                    def ev_o(t0, tn, pb, pk, vc=vc):
                        S.op('act', lambda e: e.activation(out=osT[:, vc, t0:t0 + tn], in_=pb[:, 0:tn], func=AF.Sigmoid), reads=[pk], writes=['srcT'])
                    evs.append(ev_o)
                self.proj_feat_multi(W, 4096 + h * 512 + v2 * 128, evs)
            if blk == 0:
                S.op('dve', lambda e: e.memset(CT[:, 0, :], 0.0), writes=['CT'])
                S.op('dve', lambda e: e.memset(CT[:, 1, :], 0.0), writes=['CT'])
                S.op('dve', lambda e: e.memset(ncol, 0.0), writes=['ncol'])
            else:
                S.dma('sp', CT, self.scr_mlC[h].rearrange("(c p) v -> p c v", p=128), reads=['scr_mlC'], writes=['CT'])
                S.dma('sp', ncol, self.scr_mln[h], reads=['scr_mln'], writes=['ncol'])
            if NT == 9:
                S.dma('sp', N0, self.sn_in[h], writes=['N0'])
                for (rows, c0) in ((DEC, 0), (EM0, 16)):
                    pb, pk = self.bank()
                    S.op('pe', lambda e, pb=pb, rows=rows: e.matmul(pb[:, 0:16], lhsT=selh, rhs=rows, start=True, stop=True), reads=['DEC', 'EM0', 'cst'], writes=[pk])
                    S.op('act', lambda e, pb=pb, c0=c0: e.copy(out=colS[:, c0:c0 + 16], in_=pb[:, 0:16]), reads=[pk], writes=['colS'])
            Wm_, attm_, qtl_, kdtok_, rden_ = Wm, attm, qtl, kdtok, rden
            alt = (U[0], U[4].bitcast(BF16)[:, 0:128], U[1].bitcast(BF16).rearrange("p (c t) -> p c t", c=2), U[2].bitcast(BF16), U[3])
            for c in range(NT):
                sl = slice(c * 128, (c + 1) * 128)
                cp_ = c % 2
                Wm, attm, qtl, kdtok, rden = (Wm_, attm_, qtl_, kdtok_, rden_) if cp_ == 0 else alt
                X = lambda nm: nm + str(cp_)
                per, perk = self.bank()
                S.op('pe', lambda e, per=per, sl=sl: e.matmul(per[:, 0:128], lhsT=selh, rhs=E1[0:4, sl], start=True, stop=True), reads=['E1', 'cst'], writes=[perk])
                S.op('dve', lambda e, per=per, c=c: e.tensor_tensor(out=Wm, in0=per[:, 0:128], in1=(maskP if c < 8 else maskS), op=ALU.mult), reads=[perk, 'cst'], writes=[X('Wm')])
                for dc in range(2):
                    S.op('dve', lambda e, per=per, dc=dc, sl=sl: e.tensor_tensor(out=qtl[:, dc, :], in0=per[:, 0:128], in1=qT[:, dc, sl], op=ALU.mult), reads=[perk, 'qT'], writes=[X('qtl')])
                pa, pak = self.bank()
                for dc in range(2):
                    S.op('pe', lambda e, pa=pa, dc=dc, sl=sl: e.matmul(pa[:, 0:128], lhsT=kT[:, dc, sl], rhs=qT[:, dc, sl], start=(dc == 0), stop=(dc == 1)), reads=['kT', 'qT'], writes=[pak], signal=(dc == 1))
                S.op('dve', lambda e, pa=pa, c=c: e.scalar_tensor_tensor(out=attm, in0=pa[:, 0:128], scalar=E2C[:, c, h:h + 1], in1=Wm, op0=ALU.mult, op1=ALU.mult), reads=[pak, 'EC', X('Wm')], writes=[X('attm')])
                pt, ptk = self.bank()
                ptv = pt[:].bitcast(BF16)
                for dc in range(2):
                    S.op('pe', lambda e, ptv=ptv, dc=dc, sl=sl: e.transpose(out=ptv[:, dc * 128:(dc + 1) * 128], in_=kT[:, dc, sl], identity=self.identb[:]), reads=['kT', 'identb'], writes=[ptk], signal=(dc == 1))
                S.op('act', lambda e, ptv=ptv, c=c: e.activation(out=kdtok, in_=ptv[:, 0:256], func=AF.Copy, scale=E3C[:, c, h:h + 1]), reads=[ptk, 'EC'], writes=[X('kdtok')])
                po = [self.lbank(i) for i in range(4)]
                pd, pdk = self.lbank(4)
                for vc in range(4):
                    pb, pk = po[vc]
                    S.op('pe', lambda e, pb=pb, vc=vc, c=c: e.matmul(pb[:, 0:128], lhsT=vtok[:, c, vc * 128:(vc + 1) * 128], rhs=attm, start=True, stop=False), reads=['vtok', X('attm')], writes=[pk], signal=False)
                S.op('pe', lambda e, pd=pd: e.matmul(pd[:, 0:128], lhsT=onesb, rhs=attm, start=True, stop=False), reads=[X('attm'), 'onesb'], writes=[pdk], signal=False)
                if c < 8:
                    S.op('act', lambda e: e.copy(out=CTbf, in_=CT), reads=['CT'], writes=['CTbf'])
                    for dc in range(2):
                        S.op('dve', lambda e, dc=dc: e.tensor_scalar(out=nrep[:, dc, :], in0=onesb, scalar1=ncol[:, dc:dc + 1], scalar2=None, op0=ALU.mult), reads=['ncol', 'onesb'], writes=['nrep'])
                    for dc in range(2):
                        for vc in range(4):
                            pb, pk = po[vc]
                            S.op('pe', lambda e, pb=pb, vc=vc, dc=dc: e.matmul(pb[:, 0:128], lhsT=CTbf[:, dc, vc * 128:(vc + 1) * 128], rhs=qtl[:, dc, :], start=False, stop=(dc == 1)), reads=['CTbf', X('qtl')], writes=[pk], signal=(dc == 1))
                        S.op('pe', lambda e, pd=pd, dc=dc: e.matmul(pd[:, 0:128], lhsT=nrep[:, dc, :], rhs=qtl[:, dc, :], start=False, stop=(dc == 1)), reads=['nrep', X('qtl')], writes=[pdk], signal=(dc == 1))
                    deccol = Wm[:, 127:128]
                    for dc in range(2):
                        pS, pSk = self.bank()
                        S.op('pe', lambda e, pS=pS, dc=dc, c=c: e.matmul(pS[:, 0:512], lhsT=kdtok[:, dc * 128:(dc + 1) * 128], rhs=vtok[:, c, :], start=True, stop=True), reads=[X('kdtok'), 'vtok'], writes=[pSk])
                        S.op('dve', lambda e, pS=pS, dc=dc: e.scalar_tensor_tensor(out=CT[:, dc, :], in0=CT[:, dc, :], scalar=deccol, in1=pS[:, 0:512], op0=ALU.mult, op1=ALU.add), reads=[pSk, X('Wm'), 'CT', 'CTbf'], writes=['CT'])
                        pn, pnk = self.bank()
                        S.op('pe', lambda e, pn=pn, dc=dc: e.matmul(pn[:, 0:2], lhsT=kdtok[:, dc * 128:(dc + 1) * 128], rhs=onesb[:, 0:2], start=True, stop=True), reads=[X('kdtok'), 'onesb'], writes=[pnk])
                        S.op('dve', lambda e, pn=pn, dc=dc: e.scalar_tensor_tensor(out=ncol[:, dc:dc + 1], in0=ncol[:, dc:dc + 1], scalar=deccol, in1=pn[:, 0:1], op0=ALU.mult, op1=ALU.add), reads=[pnk, X('Wm'), 'ncol', 'nrep'], writes=['ncol'])
                else:
                    for j in range(16):
                        for dc in range(2):
                            S.dma('sp', Cin, self.sC0T[j, h, dc * 128:(dc + 1) * 128, :], writes=['Cin'])
                            S.op('act', lambda e, j=j: e.activation(out=Cbf, in_=Cin, func=AF.Copy, scale=colS[:, 16 + j:17 + j]), reads=['Cin', 'colS'], writes=['Cbf'])
                            S.op('dve', lambda e, j=j, dc=dc: e.tensor_scalar(out=nrep[:, dc, :], in0=onesb, scalar1=N0[:, dc, j:j + 1], scalar2=colS[:, 16 + j:17 + j], op0=ALU.mult, op1=ALU.mult), reads=['N0', 'colS', 'onesb'], writes=['nrep'])
                            fin = (j == 15 and dc == 1)
                            for vc in range(4):
                                pb, pk = po[vc]
                                S.op('pe', lambda e, pb=pb, vc=vc, dc=dc, j=j, fin=fin: e.matmul(pb[:, 8 * j:8 * j + 8], lhsT=Cbf[:, vc * 128:(vc + 1) * 128], rhs=qtl[:, dc, 8 * j:8 * j + 8], start=False, stop=fin), reads=['Cbf', X('qtl')], writes=[pk], signal=(vc == 3))
                            S.op('pe', lambda e, pd=pd, dc=dc, j=j, fin=fin: e.matmul(pd[:, 8 * j:8 * j + 8], lhsT=nrep[:, dc, :], rhs=qtl[:, dc, 8 * j:8 * j + 8], start=False, stop=fin), reads=['nrep', X('qtl')], writes=[pdk])
                            S.op('dve', lambda e, j=j, dc=dc: e.tensor_scalar(out=km, in0=kdtok[:, dc * 128:(dc + 1) * 128], scalar1=selS[:, j:j + 1], scalar2=None, op0=ALU.mult), reads=[X('kdtok'), 'cst'], writes=['km'])
                            pS, pSk = self.bank()
                            S.op('pe', lambda e, pS=pS, c=c: e.matmul(pS[:, 0:512], lhsT=km, rhs=vtok[:, c, :], start=True, stop=True), reads=['km', 'vtok'], writes=[pSk])
                            S.op('dve', lambda e, pS=pS, j=j: e.scalar_tensor_tensor(out=Cout, in0=Cin, scalar=colS[:, j:j + 1], in1=pS[:, 0:512], op0=ALU.mult, op1=ALU.add), reads=[pSk, 'colS', 'Cin', 'Cbf'], writes=['Cout'])
                            S.dma('sp', self.o_ml_C_s[j, h, dc * 128:(dc + 1) * 128, :], Cout, reads=['Cout'], writes=['o_ml_C_s'])
                            pn, pnk = self.bank()
                            S.op('pe', lambda e, pn=pn: e.matmul(pn[:, 0:2], lhsT=km, rhs=onesb[:, 0:2], start=True, stop=True), reads=['km', 'onesb'], writes=[pnk])
                            S.op('dve', lambda e, pn=pn, dc=dc, j=j: e.scalar_tensor_tensor(out=NOUT[:, dc, j:j + 1], in0=N0[:, dc, j:j + 1], scalar=colS[:, j:j + 1], in1=pn[:, 0:1], op0=ALU.mult, op1=ALU.add), reads=[pnk, 'colS', 'N0'], writes=['NOUT'])
                    S.dma('sp', self.o_ml_n_s[h], NOUT, reads=['NOUT'], writes=['o_ml_n_s'])
                S.op('act', lambda e, pd=pd: e.activation(out=rden, in_=pd[:, 0:128], func=AF.Abs), reads=[pdk], writes=[X('rden')])
                S.op('dve', lambda e: e.tensor_scalar(out=rden, in0=rden, scalar1=1.0, scalar2=None, op0=ALU.max), reads=[X('rden')], writes=[X('rden')])
                S.op('dve', lambda e: e.reciprocal(out=rden, in_=rden), reads=[X('rden')], writes=[X('rden')])
                pss, pssk = self.bank()
                for vc in range(4):
                    pb, pk = po[vc]
                    S.op('act', lambda e, pb=pb, vc=vc: e.activation(out=sq[:, vc, :], in_=pb[:, 0:128], func=AF.Square), reads=[pk], writes=['sq%d' % vc])
                    S.op('pe', lambda e, pss=pss, vc=vc: e.matmul(pss[:, 0:128], lhsT=onesb, rhs=sq[:, vc, :], start=(vc == 0), stop=(vc == 3)), reads=['sq%d' % vc, 'onesb'], writes=[pssk], signal=(vc == 3))
                S.op('dve', lambda e, pss=pss: e.tensor_tensor(out=rs_, in0=pss[:, 0:128], in1=rden, op=ALU.mult), reads=[pssk, X('rden')], writes=['rs_'])
                S.op('dve', lambda e: e.scalar_tensor_tensor(out=rs_, in0=rs_, scalar=1.0 / 512, in1=rden, op0=ALU.mult, op1=ALU.mult), reads=['rs_', X('rden')], writes=['rs_'])
                S.op('dve', lambda e: e.tensor_scalar(out=rs_, in0=rs_, scalar1=EPS, scalar2=None, op0=ALU.add), reads=['rs_'], writes=['rs_'])
                S.op('act', lambda e: e.activation(out=rs_, in_=rs_, func=AF.Sqrt), reads=['rs_'], writes=['rs_'])
                S.op('dve', lambda e: e.reciprocal(out=rs_, in_=rs_), reads=['rs_'], writes=['rs_'])
                S.op('dve', lambda e: e.tensor_tensor(out=rs_, in0=rs_, in1=rden, op=ALU.mult), reads=['rs_', X('rden')], writes=['rs_'])
                for vc in range(4):
                    pb, pk = po[vc]
                    S.op('dve', lambda e, pb=pb, vc=vc: e.scalar_tensor_tensor(out=t1, in0=pb[:, 0:128], scalar=self.col("mlng", 4 * h + vc), in1=rs_, op0=ALU.mult, op1=ALU.mult), reads=[pk, 'rs_', 'cols'], writes=['t1'])
                    S.op('dve', lambda e, vc=vc, sl=sl: e.tensor_tensor(out=osT[:, vc, sl], in0=t1, in1=osT[:, vc, sl], op=ALU.mult), reads=['t1', 'srcT'], writes=['srcT'])
            if last:
                pb, pk = self.bank()
                S.op('pe', lambda e, pb=pb: e.matmul(pb[:, 0:2], lhsT=selh, rhs=self.sm3[0:4, 96:98], start=True, stop=True), reads=['EMF', 'cst'], writes=[pk])
                S.op('act', lambda e, pb=pb: e.copy(out=emfcol, in_=pb[:, 1:2]), reads=[pk], writes=['emfcol'])
                S.op('dve', lambda e: e.tensor_scalar(out=CT, in0=CT, scalar1=emfcol, scalar2=None, op0=ALU.mult), reads=['CT', 'emfcol'], writes=['CT'])
                S.op('dve', lambda e: e.tensor_scalar(out=ncol, in0=ncol, scalar1=emfcol, scalar2=None, op0=ALU.mult), reads=['ncol', 'emfcol'], writes=['ncol'])
                S.dma('sp', self.o_ml_C_p[h].rearrange("(c p) v -> p c v", p=128), CT, reads=['CT'], writes=['o_ml_C_p'])
                S.dma('sp', self.o_ml_n_p[h], ncol, reads=['ncol'], writes=['o_ml_n_p'])
            else:
                S.dma('sp', self.scr_mlC[h].rearrange("(c p) v -> p c v", p=128), CT, reads=['CT'], writes=['scr_mlC'])
                S.dma('sp', self.scr_mln[h], ncol, reads=['ncol'], writes=['scr_mln'])
            self.out_proj(self.w_ml_out, h * 512, 4, osT)
            S.barrier(pool=False)


_NC_CACHE = {}


def _consts():
    c = np.zeros((128, 1040), np.float32)
    c[:, 0:128] = np.eye(128, dtype=np.float32)
    s = np.arange(128)[:, None]
    t = np.arange(128)[None, :]
    c[:, 128:256] = (s <= t).astype(np.float32)
    c[:, 256:384] = ((s <= t) & (s // 8 == t // 8)).astype(np.float32)
    c[:, 384:400] = (s // 8 == np.arange(16)[None, :]).astype(np.float32)
    c[:, 400:528] = 1.0
    for k in range(4):
        c[k, 528 + k * 128:528 + (k + 1) * 128] = 1.0
    return c


def make_in_maps(inp, with_mixers=True):
    f = lambda a: np.ascontiguousarray(np.asarray(a, np.float32))
    cols = np.zeros((128, NCOLS), np.float32)

    def put(name, arr):
        o, k = COLOFF[name]
        assert arr.shape == (128, k), (name, arr.shape, k)
        cols[:, o:o + k] = arr
    adab = [inp["ab_ada_b"][0], inp["ffn_ada_b"][0], inp["ml_ada_b"][0], inp["ffn_ada_b"][1]]
    ng = [inp["ab_norm_g"][0], inp["ffn_norm_g"][0], inp["ml_norm_g"][0], inp["ffn_norm_g"][1]]
    adaw = [inp["ab_ada_w"][0], inp["ffn_ada_w"][0], inp["ml_ada_w"][0], inp["ffn_ada_w"][1]]
    for s in range(4):
        put("adab%d" % s, _fm(adab[s]))
        put("normg%d" % s, _fm(ng[s]))
    put("glabg", _fm(inp["gla_b_gate"][0]))
    put("glang", _fm(inp["gla_norm_g"][0]))
    put("s5d", _fm(inp["s5_d"][0]))
    put("s5bglu", _fm(inp["s5_b_glu"][0]))
    put("mlng", _fm(inp["ml_out_norm_g"][0]))
    bi = np.zeros((128, 1), np.float32)
    bi[0:4, 0] = np.asarray(inp["ml_b_i"][0], np.float32)
    bf = np.zeros((128, 1), np.float32)
    bf[0:4, 0] = np.asarray(inp["ml_b_f"][0], np.float32)
    put("mlbi", bi)
    put("mlbf", bf)
    r = lambda a: np.ascontiguousarray(np.asarray(a, np.float32).reshape(32, 2, 64).transpose(1, 2, 0).reshape(128, 32))
    put("lamre", r(inp["s5_lam_re"][0]))
    put("lamim", r(inp["s5_lam_im"][0]))
    put("logdt", r(np.broadcast_to(np.asarray(inp["s5_log_dt"][0], np.float32)[:, None], (64, 64))))
    rst = np.ones((128, 1152), np.float32)
    rst[:, 1024::8] = 0.0
    shared = {"cols": cols, "cst": _consts(), "rst": rst,
              "w_ab_in": f(inp["ab_w_in"][0]), "w_gate": f(inp["gla_w_gate"][0]), "w_ab_out": f(inp["ab_w_out"][0]), "w_glu": f(inp["s5_w_glu"][0]),
              "w_ml_in": f(inp["ml_w_in"][0]), "w_ml_out": f(inp["ml_w_out"][0]),
              "gfin": np.ascontiguousarray(np.broadcast_to(np.asarray(inp["final_norm_g"], np.float32)[None, :], (128, D)))}
    for s in range(4):
        shared["ada_w%d" % s] = f(adaw[s])
    for l in range(2):
        shared["ffn_w1_%d" % l] = f(inp["ffn_w1"][l])
        shared["ffn_w3_%d" % l] = f(inp["ffn_w3"][l])
        shared["ffn_w2_%d" % l] = f(inp["ffn_w2"][l])
    bpad = np.zeros((32, 2, 128, 128), np.float32)
    cpad = np.zeros((32, 2, 128, 128), np.float32)
    for q in range(32):
        for g2 in range(2):
            g = 2 * q + g2
            c0 = (2 * (q % 4) + g2) * 16
            bpad[q, 0, g2 * 64:(g2 + 1) * 64, c0:c0 + 16] = inp["s5_b_re"][0][g]
            bpad[q, 1, g2 * 64:(g2 + 1) * 64, c0:c0 + 16] = inp["s5_b_im"][0][g]
            cpad[q, 0, g2 * 64:(g2 + 1) * 64, c0:c0 + 16] = np.asarray(inp["s5_c_re"][0][g]).T
            cpad[q, 1, g2 * 64:(g2 + 1) * 64, c0:c0 + 16] = np.asarray(inp["s5_c_im"][0][g]).T
    shared["bpad"] = bpad
    shared["cpad"] = cpad
    maps = []
    for c in range(8):
        b = c % 4
        m = dict(shared)
        m["xp"] = f(inp["x_prompt"][b])
        m["xs"] = f(np.asarray(inp["x_sample"][16 * c:16 * c + 16]).reshape(128, D))
        crep = np.concatenate([np.repeat(np.asarray(inp["c_prompt"][b:b + 1], np.float32), 128, axis=0),
                               np.repeat(np.asarray(inp["c_sample"][16 * c:16 * c + 16], np.float32), 8, axis=0)], axis=0)
        m["crep"] = np.ascontiguousarray(crep)
        m["sgla"] = f(inp["state_gla"][0, 16 * c:16 * c + 16])
        sl = slice(16 * c, 16 * c + 16)
        m["sC0T"] = f(np.asarray(inp["state_mlstm_C"][0, sl]).transpose(0, 1, 3, 2))
        m["sn_in"] = f(np.asarray(inp["state_mlstm_n"][0, sl]).reshape(16, 4, 2, 128).transpose(1, 3, 2, 0))
        m["sm_in"] = f(np.asarray(inp["state_mlstm_m"][0, sl]).T)
        sre = np.asarray(inp["state_s5_re"][0, sl], np.float32).reshape(16, 32, 128)
        sim_ = np.asarray(inp["state_s5_im"][0, sl], np.float32).reshape(16, 32, 128)
        m["s5in"] = f(np.stack([sre, sim_], axis=0).transpose(2, 0, 3, 1))
        maps.append(m)
    return maps


def kernel(**inp):
    key = "full"
    if key not in _NC_CACHE:
        _NC_CACHE[key] = K(True).build()
    nc = _NC_CACHE[key]
    maps = make_in_maps(inp)
    res = run_bass_kernel_spmd(nc, maps, core_ids=list(range(8)))
    R = res.results
    f32 = lambda a: np.ascontiguousarray(np.asarray(a, np.float32))
    y_prompt = f32(np.stack([R[b]["yp"] for b in range(4)], axis=0))
    y_sample = f32(np.concatenate([R[c]["ys"].reshape(16, 8, D) for c in range(8)], axis=0))
    unp = lambda arr: np.asarray(arr).reshape(2, 64, 32).transpose(2, 0, 1).reshape(64, 64)
    gla_p = f32(np.stack([R[b]["o_gla_p"] for b in range(4)], axis=0)[None])
    s5re_p = f32(np.stack([unp(R[b]["o_s5_p"][0]) for b in range(4)], axis=0)[None])
    s5im_p = f32(np.stack([unp(R[b]["o_s5_p"][1]) for b in range(4)], axis=0)[None])
    C_p = f32(np.stack([np.asarray(R[b]["o_ml_C_p"]).transpose(0, 2, 1) for b in range(4)], axis=0)[None])
    n_p = f32(np.stack([np.asarray(R[b]["o_ml_n_p"]).transpose(0, 2, 1).reshape(4, 256) for b in range(4)], axis=0)[None])
    m_p = f32(np.stack([np.asarray(R[b]["o_ml_m_p"])[:, 0] for b in range(4)], axis=0)[None])
    gla_s = f32(np.concatenate([R[c]["o_gla_s"] for c in range(8)], axis=0)[None])
    uns = lambda arr, k: np.asarray(arr)[:, k].transpose(2, 0, 1).reshape(16, 64, 64)
    s5re_s = f32(np.concatenate([uns(R[c]["o_s5_s"], 0) for c in range(8)], axis=0)[None])
    s5im_s = f32(np.concatenate([uns(R[c]["o_s5_s"], 1) for c in range(8)], axis=0)[None])
    C_s = f32(np.concatenate([np.asarray(R[c]["o_ml_C_s"]).transpose(0, 1, 3, 2) for c in range(8)], axis=0)[None])
    n_s = f32(np.concatenate([np.asarray(R[c]["o_ml_n_s"]).transpose(3, 0, 2, 1).reshape(16, 4, 256) for c in range(8)], axis=0)[None])
    m_s = f32(np.concatenate([np.asarray(R[c]["o_ml_m_s"]).T for c in range(8)], axis=0)[None])
    return (y_prompt, y_sample, gla_p, s5re_p, s5im_p, C_p, n_p, m_p, gla_s, s5re_s, s5im_s, C_s, n_s, m_s)
```

```python
import types
import numpy as np
from contextlib import ExitStack
import concourse.bass as bass
import concourse.mybir as mybir
from concourse.bass_utils import run_bass_kernel_spmd

F32 = mybir.dt.float32
BF16 = mybir.dt.bfloat16
AF = mybir.ActivationFunctionType
ALU = mybir.AluOpType

D = 2048
DFF = 5632
EPS = 1e-6
EPOCH = 4000
N_DMA_SEMS = 24
SAME_ENGINE_SYNC = True
NBLK = 2
TP = 1024


def _freeze(fn):
    if fn.__closure__ is None:
        return fn
    cells = []
    for c in fn.__closure__:
        try:
            cells.append(types.CellType(c.cell_contents))
        except ValueError:
            cells.append(c)
    return types.FunctionType(fn.__code__, fn.__globals__, fn.__name__, fn.__defaults__, tuple(cells))


class Sched:
    ENGS = ("pe", "act", "dve", "pool", "sp")

    def __init__(self, nc, es):
        self.nc = nc
        self.es = es
        self.prog = {e: [] for e in self.ENGS}
        self.cnt = {e: 0 for e in self.ENGS}
        self.sems = {e: [] for e in self.ENGS}
        self.waited = {e: {} for e in self.ENGS}
        self.last_write = {}
        self.reads = {}
        self.dma_sems = [es.enter_context(nc.semaphore("dma%d" % i)) for i in range(N_DMA_SEMS)]
        self.dma_cnt = [0] * N_DMA_SEMS
        self.dma_rr = 0
        self.dma_rr_pool = 0
        self.n_inst = {e: 0 for e in self.ENGS}

    def _eng_sem(self, eng, epoch):
        lst = self.sems[eng]
        while len(lst) <= epoch:
            lst.append(self.es.enter_context(self.nc.semaphore("s_%s_%d" % (eng, len(lst)))))
        return lst[epoch]

    def _wait(self, eng, dep):
        if dep[0] == 'e':
            _, peng, c = dep
            if peng == eng and (eng == 'pe' or not SAME_ENGINE_SYNC):
                return
            epoch = (c - 1) // EPOCH
            val = (c - 1) % EPOCH + 1
            key = ('e', peng, epoch)
            sem = self._eng_sem(peng, epoch)
        else:
            _, si, val = dep
            key = ('d', si)
            sem = self.dma_sems[si]
        w = self.waited[eng]
        if w.get(key, 0) >= val:
            return
        w[key] = val
        self.prog[eng].append(lambda e, sem=sem, val=val: e.wait_ge(sem, val))

    def _deps(self, eng, reads, writes):
        deps = []
        for r in reads:
            d = self.last_write.get(r)
            if d is not None:
                deps.append(d)
        for w in writes:
            d = self.last_write.get(w)
            if d is not None:
                deps.append(d)
            deps.extend(self.reads.get(w, {}).values())
        for d in deps:
            self._wait(eng, d)

    def _commit(self, myid, reads, writes):
        for r in reads:
            self.reads.setdefault(r, {})[myid[:2]] = myid
        for w in writes:
            self.last_write[w] = myid
            self.reads[w] = {}

    def op(self, eng, fn, reads=(), writes=(), signal=True):
        fn = _freeze(fn)
        self._deps(eng, reads, writes)
        c = self.cnt[eng] + 1
        myid = ('e', eng, c)
        self.n_inst[eng] += 1
        if signal:
            self.cnt[eng] = c
            sem = self._eng_sem(eng, (c - 1) // EPOCH)
            self.prog[eng].append(lambda e, fn=fn, sem=sem: fn(e).then_inc(sem, 1))
        else:
            self.prog[eng].append(lambda e, fn=fn: fn(e))
        self._commit(myid, reads, writes)
        return myid

    def dma(self, q, out, in_, reads=(), writes=(), **kw):
        half = N_DMA_SEMS // 2
        if q == 'pool':
            si = half + self.dma_rr_pool
            self.dma_rr_pool = (self.dma_rr_pool + 1) % half
        else:
            si = self.dma_rr
            self.dma_rr = (self.dma_rr + 1) % half
        if self.dma_cnt[si] > 0:
            self._wait(q, ('d', si, 16 * self.dma_cnt[si]))
        self._deps(q, reads, writes)
        self.dma_cnt[si] += 1
        val = 16 * self.dma_cnt[si]
        sem = self.dma_sems[si]
        myid = ('d', si, val)
        self.n_inst[q] += 1
        self.prog[q].append(lambda e, out=out, in_=in_, sem=sem, kw=kw: e.dma_start(out=out, in_=in_, **kw).then_inc(sem, 16))
        self._commit(myid, reads, writes)
        return myid

    def barrier(self, pool=True):
        for eng in (("pe", "act", "dve", "pool", "sp") if pool else ("pe", "act", "dve", "sp")):
            for peng in ("pe", "act", "dve", "pool"):
                if self.cnt[peng] > 0:
                    self._wait(eng, ('e', peng, self.cnt[peng]))
            for si in range(N_DMA_SEMS):
                if self.dma_cnt[si] > 0:
                    self._wait(eng, ('d', si, 16 * self.dma_cnt[si]))

    def finish(self):
        for si in range(N_DMA_SEMS):
            if self.dma_cnt[si] > 0:
                self._wait('sp', ('d', si, 16 * self.dma_cnt[si]))
        nc = self.nc
        with nc.Block() as block:
            @block.tensor
            def _(e):
                for f in self.prog['pe']:
                    f(e)

            @block.scalar
            def _(e):
                for f in self.prog['act']:
                    f(e)

            @block.vector
            def _(e):
                for f in self.prog['dve']:
                    f(e)

            @block.gpsimd
            def _(e):
                for f in self.prog['pool']:
                    f(e)

            @block.sync
            def _(e):
                for f in self.prog['sp']:
                    f(e)


def _col_layout():
    off = {}
    n = 0

    def add(name, k):
        nonlocal n
        off[name] = (n, k)
        n += k
    for s in range(4):
        add("adab%d" % s, 48)
        add("normg%d" % s, 16)
    add("glabg", 4)
    add("glang", 8)
    add("s5d", 8)
    add("s5bglu", 8)
    add("mlng", 16)
    add("mlbi", 1)
    add("mlbf", 1)
    add("lamre", 32)
    add("lamim", 32)
    add("logdt", 32)
    return off, n


COLOFF, NCOLS = _col_layout()


def K3(x):
    return [x, x + 'r', x + 'i']


def _fm(v):
    v = np.asarray(v, np.float32).reshape(-1, 128)
    return np.ascontiguousarray(v.T)


class K:
    def __init__(self, with_mixers=True, dbg=None):
        self.with_mixers = with_mixers
        self.dbg = dbg or {}
        self.nc = bass.Bass("TRN2", target_bir_lowering=False)
        self.es = ExitStack()

    def dram_in(self, name, shape, dt=F32):
        return self.nc.dram_tensor(name, list(shape), dt, kind="ExternalInput").ap()

    def dram_out(self, name, shape, dt=F32):
        return self.nc.dram_tensor(name, list(shape), dt, kind="ExternalOutput").ap()

    def sb(self, name, shape, dt):
        return self.es.enter_context(self.nc.sbuf_tensor(name, list(shape), dt))

    def build(self):
        nc = self.nc
        with self.es:
            self._decl()
            self.S = Sched(nc, self.es)
            self._setup()
            for blk in range(self.dbg.get('nblk', NBLK)):
                self._block(blk)
            self.S.finish()
        return nc

    def _decl(self):
        di, do = self.dram_in, self.dram_out
        self.xp = di("xp", [NBLK * TP, D])
        self.xs = di("xs", [128, D])
        self.crep = di("crep", [256, D])
        self.cols_d = di("cols", [128, NCOLS])
        self.gfin_d = di("gfin", [128, D])
        self.cst_d = di("cst", [128, 1040])
        self.ada_w = [di("ada_w%d" % s, [D, 3 * D]) for s in range(4)]
        self.ffn_w1 = [di("ffn_w1_%d" % l, [D, DFF]) for l in range(2)]
        self.ffn_w3 = [di("ffn_w3_%d" % l, [D, DFF]) for l in range(2)]
        self.ffn_w2 = [di("ffn_w2_%d" % l, [DFF, D]) for l in range(2)]
        self.w_ab_in = di("w_ab_in", [D, 4112])
        self.w_gate = di("w_gate", [16, 512])
        self.w_ab_out = di("w_ab_out", [D, D])
        self.w_glu = di("w_glu", [1024, 1024])
        self.rst_d = di("rst", [128, 1152])
        self.sgla = di("sgla", [16, 4, 128, 256])
        self.o_gla_p = do("o_gla_p", [4, 128, 256])
        self.o_gla_s = do("o_gla_s", [16, 4, 128, 256])
        self.bpad = di("bpad", [32, 2, 128, 128])
        self.cpad = di("cpad", [32, 2, 128, 128])
        self.s5in = di("s5in", [32, 2, 128, 16])
        self.o_s5_p = do("o_s5_p", [2, 128, 32])
        self.o_s5_s = do("o_s5_s", [32, 2, 128, 16])
        self.scr_bbT = self.nc.dram_tensor("scr_bbT", [32, 2, 128, 128], BF16, kind="Internal").ap()
        self.w_ml_in = di("w_ml_in", [D, 6152])
        self.w_ml_out = di("w_ml_out", [D, D])
        self.sC0T = di("sC0T", [16, 4, 256, 512])
        self.sn_in = di("sn_in", [4, 128, 2, 16])
        self.sm_in = di("sm_in", [4, 16])
        self.o_ml_C_p = do("o_ml_C_p", [4, 256, 512])
        self.o_ml_n_p = do("o_ml_n_p", [4, 128, 2])
        self.o_ml_m_p = do("o_ml_m_p", [4, 1])
        self.o_ml_C_s = do("o_ml_C_s", [16, 4, 256, 512])
        self.o_ml_n_s = do("o_ml_n_s", [4, 128, 2, 16])
        self.o_ml_m_s = do("o_ml_m_s", [4, 16])
        self.scr_mlC = self.nc.dram_tensor("scr_mlC", [4, 256, 512], F32, kind="Internal").ap()
        self.scr_mln = self.nc.dram_tensor("scr_mln", [4, 128, 2], F32, kind="Internal").ap()
        self.scr_mod = self.nc.dram_tensor("scr_mod", [4, 128, 8448], F32, kind="Internal").ap()
        self.scr_gla = self.nc.dram_tensor("scr_gla", [4, 128, 256], F32, kind="Internal").ap()
        self.yp = do("yp", [NBLK * TP, D])
        self.ys = do("ys", [128, D])
        self.x = self.sb("x", [128, 9, D], F32)
        self.wb = [self.sb("wb%d" % i, [128, 4096], BF16) for i in range(2)]
        self.wb_rr = 0
        self.wb_cur = [w[:] for w in self.wb]
        self.GT = self.sb("GT", [128, 2, D], F32)
        self.cols = self.sb("colsb", [128, NCOLS], F32)
        self.cst = self.sb("cstb", [128, 1040], F32)
        self.identb = self.sb("identb", [128, 128], BF16)
        self.onesb = self.sb("onesb", [128, 128], BF16)
        self.small = self.sb("small", [128, 64], F32)
        self.rst = self.sb("rstb", [128, 1152], BF16)
        self.sm2 = self.sb("sm2", [128, 64], F32)
        self.sm3 = self.sb("sm3", [128, 128], F32)
        self.PW = self.sb("PW", [128, 3, 15, 32], F32)
        self.S5H = self.sb("S5H", [128, 2, 32], F32)
        self.ARW = 22350
        self.AR = self.sb("arena", [128, self.ARW], F32)
        self.ps = [self.es.enter_context(self.nc.psum_tensor("ps%d" % i, [128, 512], F32)) for i in range(8)]
        self.ps_rr = 0
        self.gen_banks = list(range(8))

    def af(self, off, n):
        return self.AR[:, off:off + n]

    def ab(self, off, n_bf):
        assert n_bf % 2 == 0
        return self.AR[:, off:off + n_bf // 2].bitcast(BF16)

    def bank(self):
        self.ps_rr = (self.ps_rr + 1) % len(self.gen_banks)
        i = self.gen_banks[self.ps_rr]
        return self.ps[i], "ps%d" % i

    def lbank(self, i):
        return self.ps[3 + i], "ps%d" % (3 + i)

    def wbuf(self):
        bufs = self.wb_cur
        self.wb_rr = (self.wb_rr + 1) % len(bufs)
        i = self.wb_rr
        return bufs[i], "wb%d" % i

    def col(self, name, j=0, n=1):
        o, k = COLOFF[name]
        return self.cols[:, o + j:o + j + n]

    def load_w(self, W, r0, kc, c0, ncols):
        buf, key = self.wbuf()
        v = buf[:, 0:kc * ncols].rearrange("p (k n) -> p k n", k=kc)
        self.S.dma('pool', v, W[r0:r0 + kc * 128, c0:c0 + ncols].rearrange("(k p) n -> p k n", p=128), writes=[key])
        return v, key

    def _setup(self):
        S = self.S
        S.dma('sp', self.cols[:], self.cols_d, writes=['cols'])
        S.dma('sp', self.cst[:], self.cst_d, writes=['cst'])
        S.dma('pool', self.rst[:], self.rst_d, writes=['rst'])
        S.op('dve', lambda e: e.tensor_copy(out=self.onesb[:], in_=self.cst[:, 400:528]), reads=['cst'], writes=['onesb'])
        S.op('dve', lambda e: e.tensor_copy(out=self.identb[:], in_=self.cst[:, 0:128]), reads=['cst'], writes=['identb'])
        S.barrier()
        if self.with_mixers and 0 in self.dbg.get('layers', range(2)) and self.dbg.get('s5', True):
            self.s5_setup()

    def _block(self, blk):
        S = self.S
        self.NT = 9 if blk == 0 else 8
        NT = self.NT
        self.T = NT * 128
        for i in range(8):
            S.dma('sp', self.x[:, i, :], self.xp[blk * TP + i * 128: blk * TP + (i + 1) * 128, :], writes=[('x', i)])
        if NT == 9:
            S.dma('sp', self.x[:, 8, :], self.xs, writes=[('x', 8)])
        for layer in self.dbg.get('layers', range(2)):
            if self.with_mixers:
                self.prenorm(2 * layer, blk)
                if layer == 0:
                    self.mixer_ab(blk)
                else:
                    self.mixer_ml(blk)
            if self.dbg.get('ffn', True):
                self.prenorm(2 * layer + 1, blk)
                self.ffn(layer)
        self.final_norm(blk)
        S.barrier(pool=False)

    def groups(self):
        g = [(0, 512), (512, 512)]
        if self.NT == 9:
            g.append((1024, 128))
        return g

    def make_cT(self):
        S = self.S
        o2 = 9216 + 4352
        self.cT = self.ab(18688, 16 * 256).rearrange("p (k t) -> p k t", k=16)
        tmpf = self.af(o2, D)
        tmpb = self.ab(o2 + D, D)
        for t in range(2):
            S.dma('sp', tmpf, self.crep[t * 128:(t + 1) * 128, :], writes=['tmpf'])
            S.op('act', lambda e: e.activation(out=tmpb, in_=tmpf, func=AF.Silu), reads=['tmpf'], writes=['tmpb'])
            for half in range(2):
                pb, pk = self.bank()
                pv = pb[:].bitcast(BF16).rearrange("p (k n) -> p k n", k=8)
                for k in range(8):
                    kk = half * 8 + k
                    S.op('pe', lambda e, k=k, kk=kk, pv=pv: e.transpose(out=pv[:, k, :], in_=tmpb[:, kk * 128:(kk + 1) * 128], identity=self.identb[:]),
                         reads=['tmpb', 'identb'], writes=[pk], signal=(k == 7))
                S.op('dve', lambda e, pv=pv, half=half, t=t: e.tensor_copy(out=self.cT[:, half * 8:(half + 1) * 8, t * 128:(t + 1) * 128], in_=pv),
                     reads=[pk], writes=['cT'])

        S.barrier()

    def prenorm(self, site, blk=0):
        S = self.S
        NT = self.NT
        S.barrier(pool=False)
        self.gen_banks = list(range(8))
        W = self.ada_w[site]
        cached = blk > 0
        if not cached:
            self.make_cT()
        self.hT = self.ab(0, 16 * 1152).rearrange("p (k t) -> p k t", k=16)
        o1 = 9216
        GXg = self.af(o1, 4096).rearrange("p (k t) -> p k t", k=16)
        GX = self.af(o1, 2176).rearrange("p (k t) -> p k t", k=16)
        SHx = self.af(o1 + 2176, 2176).rearrange("p (k t) -> p k t", k=16)
        o2 = o1 + 4352
        xn = [self.ab(o2 + i * 1024, D) for i in range(2)]
        junk = self.ab(o2 + 2048, D)
        adab = lambda j: self.col("adab%d" % site, j)
        ng = lambda j: self.col("normg%d" % site, j)

        if not cached:
            def mod_chunks(jlist, evac, c_lo, ncol):
                for j0 in range(jlist[0], jlist[-1] + 1, 2):
                    wv, wk = self.load_w(W, 0, 16, j0 * 128, 256)
                    for jj in range(2):
                        j = j0 + jj
                        pb, pk = self.bank()
                        for k in range(16):
                            S.op('pe', lambda e, k=k, jj=jj, wv=wv, pb=pb: e.matmul(pb[:, 0:ncol], lhsT=wv[:, k, jj * 128:(jj + 1) * 128], rhs=self.cT[:, k, c_lo:c_lo + ncol], start=(k == 0), stop=(k == 15)),
                                 reads=[wk, 'cT'], writes=[pk], signal=(k == 15))
                        evac(j, pb, pk)

            def ev_gate(j, pb, pk):
                S.op('act', lambda e: e.activation(out=GXg[:, j - 32, :], in_=pb[:, 0:256], func=AF.Identity, bias=adab(j), scale=1.0),
                     reads=[pk, 'cols'], writes=[('GX', j - 32)])
            mod_chunks(list(range(32, 48)), ev_gate, 0, 256)
            identf = self.cst[:, 0:128]
            for g in range(2):
                for q in range(4):
                    pb, pk = self.bank()
                    for kk in range(4):
                        k = q * 4 + kk
                        S.op('pe', lambda e, k=k, kk=kk, pb=pb, g=g: e.transpose(out=pb[:, kk * 128:(kk + 1) * 128], in_=GXg[:, k, g * 128:(g + 1) * 128], identity=identf),
                             reads=[('GX', k), 'cst'], writes=[pk], signal=(kk == 3))
                    S.op('act', lambda e, pb=pb, g=g, q=q: e.copy(out=self.GT[:, g, q * 512:(q + 1) * 512], in_=pb[:]), reads=[pk], writes=[('GT', g)])
            S.barrier(pool=False)

            def ev_shift(j, pb, pk):
                S.op('act', lambda e: e.activation(out=SHx[:, j, :], in_=pb[:, 0:136], func=AF.Identity, bias=adab(j), scale=1.0),
                     reads=[pk, 'cols'], writes=[('SHx', j)])
            mod_chunks(list(range(0, 16)), ev_shift, 120, 136)

            def ev_scale(j, pb, pk):
                jj = j - 16
                S.op('dve', lambda e: e.tensor_scalar(out=GX[:, jj, :], in0=pb[:, 0:136], scalar1=self.small[:, jj:jj + 1], scalar2=ng(jj), op0=ALU.add, op1=ALU.mult),
                     reads=[pk, 'cols', 'small'], writes=[('GX', jj)])
            S.op('dve', lambda e: e.tensor_scalar(out=self.small[:, 0:16], in0=self.col("adab%d" % site, 16, 16), scalar1=1.0, scalar2=None, op0=ALU.add),
                 reads=['cols'], writes=['small'])
            mod_chunks(list(range(16, 32)), ev_scale, 120, 136)
            S.dma('sp', self.scr_mod[site, :, 0:2176], self.af(o1, 2176), reads=[('GX', k) for k in range(16)], writes=['scr_mod'])
            S.dma('sp', self.scr_mod[site, :, 2176:4352], self.af(o1 + 2176, 2176), reads=[('SHx', k) for k in range(16)], writes=['scr_mod'])
            S.dma('sp', self.scr_mod[site, :, 4352:8448], self.GT[:].rearrange("p g d -> p (g d)"), reads=[('GT', 0), ('GT', 1)], writes=['scr_mod'])
        else:
            S.dma('sp', self.af(o1, 2176), self.scr_mod[site, :, 0:2176], reads=['scr_mod'], writes=[('GX', k) for k in range(16)])
            S.dma('sp', self.af(o1 + 2176, 2176), self.scr_mod[site, :, 2176:4352], reads=['scr_mod'], writes=[('SHx', k) for k in range(16)])
            S.dma('sp', self.GT[:].rearrange("p g d -> p (g d)"), self.scr_mod[site, :, 4352:8448], reads=['scr_mod'], writes=[('GT', 0), ('GT', 1)])
        ss = self.small[:, 16:16 + NT]
        rstd = self.small[:, 32:32 + NT]
        for i in range(NT):
            S.op('act', lambda e, i=i: e.activation(out=junk, in_=self.x[:, i, :], func=AF.Square, accum_out=self.small[:, 16 + i:17 + i]),
                 reads=[('x', i)], writes=['mtmp0', 'mtmp1', 'small'])
        S.op('dve', lambda e: e.tensor_scalar(out=rstd, in0=ss, scalar1=1.0 / D, scalar2=EPS, op0=ALU.mult, op1=ALU.add), reads=['small'], writes=['small'])
        S.op('act', lambda e: e.activation(out=rstd, in_=rstd, func=AF.Sqrt), reads=['small'], writes=['small'])
        S.op('dve', lambda e: e.reciprocal(out=rstd, in_=rstd), reads=['small'], writes=['small'])
        for i in range(NT):
            xb = xn[i % 2]
            xk = 'xn%d' % (i % 2)
            S.op('act', lambda e, i=i, xb=xb: e.activation(out=xb, in_=self.x[:, i, :], func=AF.Copy, scale=self.small[:, 32 + i:33 + i]),
                 reads=[('x', i), 'small'], writes=[xk])
            for half in range(2):
                pb, pk = self.bank()
                pv = pb[:].bitcast(BF16).rearrange("p (k n) -> p k n", k=8)
                for k in range(8):
                    kk = half * 8 + k
                    S.op('pe', lambda e, k=k, kk=kk, pv=pv, xb=xb: e.transpose(out=pv[:, k, :], in_=xb[:, kk * 128:(kk + 1) * 128], identity=self.identb[:]),
                         reads=[xk, 'identb'], writes=[pk], signal=(k == 7))
                dst = self.hT[:, half * 8:(half + 1) * 8, i * 128:(i + 1) * 128]
                tmp = self.af(o2 + 3072 + 1024 * half, 1024).rearrange("p (k n) -> p k n", k=8)
                tk = 'mtmp%d' % half
                S.op('dve', lambda e, pv=pv, tmp=tmp, half=half, i=i: e.tensor_tensor(out=tmp, in0=pv, in1=(GX[:, half * 8:(half + 1) * 8, 0:1].to_broadcast([128, 8, 128]) if i < 8 else GX[:, half * 8:(half + 1) * 8, 8:136]), op=ALU.mult),
                     reads=[pk] + [('GX', half * 8 + k) for k in range(8)], writes=[tk])
                S.op('dve', lambda e, dst=dst, tmp=tmp, half=half, i=i: e.tensor_tensor(out=dst, in0=tmp, in1=(SHx[:, half * 8:(half + 1) * 8, 0:1].to_broadcast([128, 8, 128]) if i < 8 else SHx[:, half * 8:(half + 1) * 8, 8:136]), op=ALU.add),
                     reads=[tk] + [('SHx', half * 8 + k) for k in range(8)], writes=[('hT', i)])
        S.barrier()

    def resid(self, i, c0, n, pb, pk):
        S = self.S
        g = 0 if i < 8 else 1
        tmp = self.rtmp[self.rt_rr % 2][:, 0:n]
        tk = 'rtmp%d' % (self.rt_rr % 2)
        self.rt_rr += 1
        S.op('dve', lambda e: e.tensor_tensor(out=tmp, in0=pb[:, 0:n], in1=self.GT[:, g, c0:c0 + n], op=ALU.mult), reads=[pk, ('GT', g)], writes=[tk])
        S.op('dve', lambda e: e.tensor_tensor(out=self.x[:, i, c0:c0 + n], in0=self.x[:, i, c0:c0 + n], in1=tmp, op=ALU.add), reads=[tk, ('x', i)], writes=[('x', i)])

    def ffn(self, layer):
        S = self.S
        NT = self.NT
        T = self.T
        o1 = 9216
        W1, W3, W2 = self.ffn_w1[layer], self.ffn_w3[layer], self.ffn_w2[layer]
        self.rtmp = [self.af(o1 + 4608 + i * 512, 512) for i in range(2)]
        self.rt_rr = 0
        sil = [self.af(o1 + 5632 + i * 512, 512) for i in range(2)]
        self.wb_cur = [w[:] for w in self.wb] + [self.ab(o1 + 6656 + i * 2048, 4096) for i in range(2)]
        pieces = [(0, 8), (8, 8), (16, 8), (24, 8), (32, 8), (40, 4)]
        for (j0, nj) in pieces:
            gT = self.ab(o1, 8 * 1152).rearrange("p (k t) -> p k t", k=8)
            for jp in range(j0, j0 + nj, 2):
                w1v, w1k = self.load_w(W1, 0, 16, jp * 128, 256)
                w3v, w3k = self.load_w(W3, 0, 16, jp * 128, 256)
                for jj in range(2):
                    jl = jp + jj - j0
                    for gi, (t0, tn) in enumerate(self.groups()):
                        pa, pak = self.bank()
                        pc, pck = self.bank()
                        for k in range(16):
                            S.op('pe', lambda e, k=k, jj=jj, w1v=w1v, pa=pa, t0=t0, tn=tn: e.matmul(pa[:, 0:tn], lhsT=w1v[:, k, jj * 128:(jj + 1) * 128], rhs=self.hT[:, k, t0:t0 + tn], start=(k == 0), stop=(k == 15)),
                                 reads=[w1k, 'hTall'], writes=[pak], signal=(k == 15))
                        for k in range(16):
                            S.op('pe', lambda e, k=k, jj=jj, w3v=w3v, pc=pc, t0=t0, tn=tn: e.matmul(pc[:, 0:tn], lhsT=w3v[:, k, jj * 128:(jj + 1) * 128], rhs=self.hT[:, k, t0:t0 + tn], start=(k == 0), stop=(k == 15)),
                                 reads=[w3k, 'hTall'], writes=[pck], signal=(k == 15))
                        sl = sil[self.rt_rr % 2][:, 0:tn]
                        sk = 'sil%d' % (self.rt_rr % 2)
                        self.rt_rr += 1
                        S.op('act', lambda e, sl=sl, pa=pa, tn=tn: e.activation(out=sl, in_=pa[:, 0:tn], func=AF.Silu), reads=[pak], writes=[sk])
                        S.op('dve', lambda e, sl=sl, pc=pc, tn=tn, jl=jl, t0=t0: e.tensor_tensor(out=gT[:, jl, t0:t0 + tn], in0=sl, in1=pc[:, 0:tn], op=ALU.mult),
                             reads=[sk, pck], writes=[('gT', jl)])
            for cp in range(4):
                w2v, w2k = self.load_w(W2, j0 * 128, nj, cp * 512, 512)
                for i in range(NT):
                    pb, pk = self.bank()
                    for k in range(nj):
                        S.op('pe', lambda e, k=k, w2v=w2v, pb=pb, i=i: e.matmul(pb[:, 0:512], lhsT=gT[:, k, i * 128:(i + 1) * 128], rhs=w2v[:, k, :], start=(k == 0), stop=(k == nj - 1)),
                             reads=[w2k] + [('gT', kk) for kk in range(nj)], writes=[pk], signal=(k == nj - 1))
                    self.resid(i, cp * 512, 512, pb, pk)
        S.barrier(pool=False)
        self.wb_cur = [w[:] for w in self.wb]
        self.wb_rr = 0

    def final_norm(self, blk):
        S = self.S
        NT = self.NT
        S.barrier(pool=False)
        gB = self.af(0, D)
        junk = self.ab(D, D)
        yt = [self.af(D + 1024 + i * D, D) for i in range(2)]
        S.dma('sp', gB, self.gfin_d, writes=['gB'])
        ss = self.small[:, 16:16 + NT]
        rstd = self.small[:, 32:32 + NT]
        for i in range(NT):
            S.op('act', lambda e, i=i: e.activation(out=junk, in_=self.x[:, i, :], func=AF.Square, accum_out=self.small[:, 16 + i:17 + i]),
                 reads=[('x', i)], writes=['mtmp0', 'mtmp1', 'small'])
        S.op('dve', lambda e: e.tensor_scalar(out=rstd, in0=ss, scalar1=1.0 / D, scalar2=EPS, op0=ALU.mult, op1=ALU.add), reads=['small'], writes=['small'])
        S.op('act', lambda e: e.activation(out=rstd, in_=rstd, func=AF.Sqrt), reads=['small'], writes=['small'])
        S.op('dve', lambda e: e.reciprocal(out=rstd, in_=rstd), reads=['small'], writes=['small'])
        for i in range(NT):
            y = yt[i % 2]
            yk = 'yt%d' % (i % 2)
            S.op('dve', lambda e, i=i, y=y: e.scalar_tensor_tensor(out=y, in0=self.x[:, i, :], scalar=self.small[:, 32 + i:33 + i], in1=gB, op0=ALU.mult, op1=ALU.mult),
                 reads=[('x', i), 'small', 'gB'], writes=[yk])
            if i < 8:
                S.dma('sp', self.yp[blk * TP + i * 128: blk * TP + (i + 1) * 128, :], y, reads=[yk], writes=['yp'])
            else:
                S.dma('sp', self.ys, y, reads=[yk], writes=['ys'])

    def proj_feat(self, W, c0, M, evac):
        S = self.S
        wv, wk = self.load_w(W, 0, 16, c0, M)
        for (t0, tn) in self.groups():
            pb, pk = self.bank()
            for k in range(16):
                S.op('pe', lambda e, k=k, pb=pb, t0=t0, tn=tn: e.matmul(pb[0:M, 0:tn], lhsT=wv[:, k, 0:M], rhs=self.hT[:, k, t0:t0 + tn], start=(k == 0), stop=(k == 15)),
                     reads=[wk], writes=[pk], signal=(k == 15))
            evac(t0, tn, pb, pk)

    def proj_feat_multi(self, W, c0, evacs):
        S = self.S
        n = len(evacs)
        wv, wk = self.load_w(W, 0, 16, c0, 128 * n)
        for jj in range(n):
            for (t0, tn) in self.groups():
                pb, pk = self.bank()
                for k in range(16):
                    S.op('pe', lambda e, k=k, pb=pb, t0=t0, tn=tn, jj=jj: e.matmul(pb[:, 0:tn], lhsT=wv[:, k, jj * 128:(jj + 1) * 128], rhs=self.hT[:, k, t0:t0 + tn], start=(k == 0), stop=(k == 15)),
                         reads=[wk], writes=[pk], signal=(k == 15))
                evacs[jj](t0, tn, pb, pk)

    def proj_tok(self, W, c0, n, evac):
        S = self.S
        wv, wk = self.load_w(W, 0, 16, c0, n)
        for i in range(self.NT):
            pb, pk = self.bank()
            for k in range(16):
                S.op('pe', lambda e, k=k, pb=pb, i=i: e.matmul(pb[:, 0:n], lhsT=self.hT[:, k, i * 128:(i + 1) * 128], rhs=wv[:, k, :], start=(k == 0), stop=(k == 15)),
                     reads=[wk], writes=[pk], signal=(k == 15))
            evac(i, pb, pk)

    def out_proj(self, W, r0, kc, srcT):
        S = self.S
        for cp in range(8):
            wv, wk = self.load_w(W, r0, kc, cp * 256, 256)
            for i in range(self.NT):
                pb, pk = self.bank()
                for k in range(kc):
                    S.op('pe', lambda e, k=k, pb=pb, i=i, wv=wv: e.matmul(pb[:, 0:256], lhsT=srcT[:, k, i * 128:(i + 1) * 128], rhs=wv[:, k, :], start=(k == 0), stop=(k == kc - 1)),
                         reads=[wk, 'srcT'], writes=[pk], signal=(k == kc - 1))
                self.resid(i, cp * 256, 256, pb, pk)

    def mixer_ab(self, blk):
        S = self.S
        self.gen_banks = [0, 1, 2]
        NT = self.NT
        T = self.T
        Wi = self.w_ab_in
        o = [9216]

        def alloc(n):
            r = o[0]
            o[0] += n
            assert o[0] <= self.ARW, o[0]
            return r
        glrT = self.ab(alloc(576), 1152)
        wgb = self.ab(alloc(256), 512)
        LB = self.af(alloc(1152), 1152)
        NB = self.af(alloc(1152), 1152)
        qk = self.af(alloc(1152), 1152)
        qtil = self.ab(alloc(576), 1152)
        ktil = self.ab(alloc(576), 1152)
        kdT = self.ab(alloc(576), 1152)
        E = [self.af(alloc(128), 128) for _ in range(3)]
        Sbf = [self.ab(alloc(128), 256) for _ in range(4)]
        attm_ = [self.ab(alloc(64), 128) for _ in range(2)]
        kdtok_ = [self.ab(alloc(64), 128) for _ in range(2)]
        km = [self.ab(alloc(64), 128) for _ in range(4)]
        oT_ = [self.af(alloc(256), 256).rearrange("p (c t) -> p c t", c=2) for _ in range(2)]
        sq_ = [self.ab(alloc(128), 256).rearrange("p (c t) -> p c t", c=2) for _ in range(2)]
        rs__ = [self.af(alloc(128), 128) for _ in range(2)]
        t1_ = [self.af(alloc(128), 128) for _ in range(2)]
        self.rtmp = [self.af(alloc(256), 256) for i in range(2)]
        self.rt_rr = 0
        Sin = [self.af(alloc(256), 256) for _ in range(4)]
        Sout = [self.af(alloc(256), 256) for _ in range(4)]
        Sg = self.af(alloc(256), 256)
        vtok = LB.bitcast(BF16)[:, 0:NT * 256].rearrange("p (i v) -> p i v", i=NT)
        rsT = NB.bitcast(BF16).rearrange("p (c t) -> p c t", c=2)
        ogT = qk.bitcast(BF16).rearrange("p (c t) -> p c t", c=2)
        maskP = self.cst[:, 128:256]
        maskS = self.cst[:, 256:384]
        selS = self.cst[:, 384:400]
        onesb = self.onesb[:]
        negbg = self.sm2[:, 32:36]
        ebl = self.sm2[:, 0:8]
        ebls = self.sm2[:, 8:24]
        S.barrier(pool=False)
        S.op('dve', lambda e: e.tensor_scalar(out=negbg, in0=self.col("glabg", 0, 4), scalar1=-1.0, scalar2=None, op0=ALU.mult), reads=['cols'], writes=['negbg'])
        wgf = self.af(alloc(512), 512)
        S.dma('sp', wgf[0:16, :], self.w_gate, writes=['wgf'])
        S.op('dve', lambda e: e.tensor_copy(out=wgb[0:16, :], in_=wgf[0:16, :]), reads=['wgf'], writes=['wgb'])

        def ev_glr(t0, tn, pb, pk):
            S.op('act', lambda e: e.copy(out=glrT[0:16, t0:t0 + tn], in_=pb[0:16, 0:tn]), reads=[pk], writes=['glrT'])
        self.proj_feat(Wi, 3072, 16, ev_glr)
        for h in self.dbg.get('heads', range(4)):
            for (t0, tn) in self.groups():
                pb, pk = self.bank()
                S.op('pe', lambda e, pb=pb, t0=t0, tn=tn: e.matmul(pb[:, 0:tn], lhsT=wgb[0:16, h * 128:(h + 1) * 128], rhs=glrT[0:16, t0:t0 + tn], start=True, stop=True),
                     reads=['wgb', 'glrT'], writes=[pk])
                S.op('act', lambda e, pb=pb, t0=t0, tn=tn: e.activation(out=LB[:, t0:t0 + tn], in_=pb[:, 0:tn], func=AF.Exp, bias=negbg[:, h:h + 1], scale=-1.0),
                     reads=[pk, 'negbg'], writes=['LB'])
            S.op('act', lambda e: e.activation(out=LB[:, 0:T], in_=LB[:, 0:T], func=AF.Ln, bias=1.0, scale=1.0), reads=['LB'], writes=['LB'])
            S.op('dve', lambda e: e.tensor_tensor_scan(out=NB[:, 0:T], data0=self.rst[:, 0:T], data1=LB[:, 0:T], initial=0.0, op0=ALU.mult, op1=ALU.add),
                 reads=['LB', 'rst'], writes=['NB'])
            S.op('dve', lambda e: e.tensor_scalar(out=LB[:, 0:T], in0=NB[:, 0:T], scalar1=1.0 / 16, scalar2=None, op0=ALU.mult), reads=['NB'], writes=['LB'])
            S.op('dve', lambda e: e.tensor_scalar(out=NB[:, 0:T], in0=NB[:, 0:T], scalar1=-1.0 / 16, scalar2=None, op0=ALU.mult), reads=['NB'], writes=['NB'])

            def etab(which, c, Eb, ek):
                sl = slice(c * 128, (c + 1) * 128)
                if c < 8:
                    if which == 'q':
                        src, bias = NB, (LB[:, c * 128 - 1:c * 128] if c > 0 else 0.0)
                    elif which == 'k':
                        src, bias = LB, (NB[:, c * 128 - 1:c * 128] if c > 0 else 0.0)
                    else:
                        src, bias = LB, NB[:, c * 128 + 127:c * 128 + 128]
                    S.op('act', lambda e: e.activation(out=Eb, in_=src[:, sl], func=AF.Exp, bias=bias, scale=1.0), reads=['LB', 'NB'], writes=[ek])
                else:
                    if which == 'q':
                        S.op('act', lambda e: e.activation(out=Eb, in_=NB[:, sl], func=AF.Exp), reads=['NB'], writes=[ek])
                    elif which == 'k':
                        S.op('act', lambda e: e.activation(out=Eb, in_=LB[:, sl], func=AF.Exp), reads=['LB'], writes=[ek])
                    else:
                        v3 = lambda a: a.rearrange("p (j i) -> p j i", i=8)
                        S.op('dve', lambda e: e.tensor_tensor(out=v3(Eb), in0=v3(LB[:, sl]), in1=v3(NB[:, sl])[:, :, 7:8].to_broadcast([128, 16, 8]), op=ALU.add),
                             reads=['LB', 'NB'], writes=[ek])
                        S.op('act', lambda e: e.activation(out=Eb, in_=Eb, func=AF.Exp), reads=[ek], writes=[ek])

            def ev_q(t0, tn, pb, pk):
                S.op('act', lambda e: e.activation(out=qk[:, t0:t0 + tn], in_=pb[:, 0:tn], func=AF.Copy, scale=128 ** -0.5), reads=[pk], writes=['qk'])
            self.proj_feat(Wi, h * 128, 128, ev_q)
            for c in range(NT):
                sl = slice(c * 128, (c + 1) * 128)
                Eb, ek = E[c % 3], 'E%d' % (c % 3)
                etab('q', c, Eb, ek)
                S.op('dve', lambda e, sl=sl, Eb=Eb: e.tensor_tensor(out=qtil[:, sl], in0=qk[:, sl], in1=Eb, op=ALU.mult), reads=['qk', ek], writes=['qtil'])
                if c < 8:
                    S.op('dve', lambda e, c=c, Eb=Eb: e.tensor_copy(out=ebl[:, c:c + 1], in_=Eb[:, 127:128]), reads=[ek], writes=['ebl'])
                else:
                    S.op('dve', lambda e, Eb=Eb: e.tensor_copy(out=ebls, in_=Eb.rearrange("p (j i) -> p j i", i=8)[:, :, 7]), reads=[ek], writes=['ebl'])

            def ev_k(t0, tn, pb, pk):
                S.op('act', lambda e: e.copy(out=qk[:, t0:t0 + tn], in_=pb[:, 0:tn]), reads=[pk, 'qtil'], writes=['qk'])
            self.proj_feat(Wi, 512 + h * 128, 128, ev_k)
            for c in range(NT):
                sl = slice(c * 128, (c + 1) * 128)
                Eb, ek = E[c % 3], 'E%d' % (c % 3)
                etab('k', c, Eb, ek)
                S.op('dve', lambda e, sl=sl, Eb=Eb: e.tensor_tensor(out=ktil[:, sl], in0=qk[:, sl], in1=Eb, op=ALU.mult), reads=['qk', ek], writes=['ktil'])
                Eb, ek = E[(c + 1) % 3], 'E%d' % ((c + 1) % 3)
                etab('d', c, Eb, ek)
                S.op('dve', lambda e, sl=sl, Eb=Eb: e.tensor_tensor(out=kdT[:, sl], in0=qk[:, sl], in1=Eb, op=ALU.mult), reads=['qk', ek], writes=['kdT'])
            S.barrier(pool=False)

            def ev_v(i, pb, pk):
                S.op('act', lambda e: e.copy(out=vtok[:, i, :], in_=pb[:, 0:256]), reads=[pk], writes=['vtok'])
            self.proj_tok(Wi, 1024 + h * 256, 256, ev_v)
            evs = []
            for vc in range(2):
                def ev_r(t0, tn, pb, pk, vc=vc):
                    S.op('act', lambda e: e.activation(out=rsT[:, vc, t0:t0 + tn], in_=pb[:, 0:tn], func=AF.Silu), reads=[pk], writes=['rsT'])
                evs.append(ev_r)
            self.proj_feat_multi(Wi, 2048 + h * 256, evs)
            S.barrier(pool=False)
            if blk == 0:
                S.op('dve', lambda e: e.memset(Sg, 0.0), writes=['Sg'])
            else:
                S.dma('sp', Sg, self.scr_gla[h], reads=['scr_gla'], writes=['Sg'])
            for c in range(NT):
                sl = slice(c * 128, (c + 1) * 128)
                cp_ = c % 2
                attm, kdtok, oT, sq, rs_, t1 = attm_[cp_], kdtok_[cp_], oT_[cp_], sq_[cp_], rs__[cp_], t1_[cp_]
                X = lambda nm: nm + str(cp_)
                pa, pak = self.bank()
                S.op('pe', lambda e, pa=pa, sl=sl: e.matmul(pa[:, 0:128], lhsT=ktil[:, sl], rhs=qtil[:, sl], start=True, stop=True), reads=['ktil', 'qtil'], writes=[pak])
                S.op('dve', lambda e, pa=pa, c=c: e.tensor_tensor(out=attm, in0=pa[:, 0:128], in1=(maskP if c < 8 else maskS), op=ALU.mult), reads=[pak, 'cst'], writes=[X('attm')])
                pt, ptk = self.bank()
                ptv = pt[:].bitcast(BF16)
                S.op('pe', lambda e, ptv=ptv, sl=sl: e.transpose(out=ptv[:, 0:128], in_=kdT[:, sl], identity=self.identb[:]), reads=['kdT', 'identb'], writes=[ptk])
                S.op('act', lambda e, ptv=ptv: e.copy(out=kdtok, in_=ptv[:, 0:128]), reads=[ptk], writes=[X('kdtok')])
                po = [self.lbank(2 * cp_), self.lbank(2 * cp_ + 1)]
                if c < 8:
                    S.op('act', lambda e: e.copy(out=Sbf[0], in_=Sg), reads=['Sg'], writes=['Sbf0'])
                    for vc in range(2):
                        pb, pk = po[vc]
                        S.op('pe', lambda e, pb=pb, vc=vc, c=c: e.matmul(pb[:, 0:128], lhsT=vtok[:, c, vc * 128:(vc + 1) * 128], rhs=attm, start=True, stop=False),
                             reads=['vtok', X('attm')], writes=[pk], signal=False)
                        S.op('pe', lambda e, pb=pb, vc=vc, sl=sl: e.matmul(pb[:, 0:128], lhsT=Sbf[0][:, vc * 128:(vc + 1) * 128], rhs=qtil[:, sl], start=False, stop=True),
                             reads=['Sbf0', 'qtil'], writes=[pk])
                    pS, pSk = self.bank()
                    S.op('pe', lambda e, pS=pS, c=c: e.matmul(pS[:, 0:256], lhsT=kdtok, rhs=vtok[:, c, :], start=True, stop=True), reads=[X('kdtok'), 'vtok'], writes=[pSk])
                    S.op('dve', lambda e, pS=pS, c=c: e.scalar_tensor_tensor(out=Sg, in0=Sg, scalar=ebl[:, c:c + 1], in1=pS[:, 0:256], op0=ALU.mult, op1=ALU.add),
                         reads=[pSk, 'ebl', 'Sg', 'Sbf0'], writes=['Sg'])
                else:
                    for vc in range(2):
                        pb, pk = po[vc]
                        S.op('pe', lambda e, pb=pb, vc=vc, c=c: e.matmul(pb[:, 0:128], lhsT=vtok[:, c, vc * 128:(vc + 1) * 128], rhs=attm, start=True, stop=False),
                             reads=['vtok', X('attm')], writes=[pk], signal=False)
                    for j0_ in range(3):
                        S.dma('sp', Sin[j0_ % 4], self.sgla[j0_, h], writes=['Sin%d' % (j0_ % 4)])
                    for j in range(16):
                        r = j % 4
                        if j + 3 < 16:
                            S.dma('sp', Sin[(j + 3) % 4], self.sgla[j + 3, h], writes=['Sin%d' % ((j + 3) % 4)])
                        S.op('act', lambda e, r=r: e.copy(out=Sbf[r], in_=Sin[r]), reads=['Sin%d' % r], writes=['Sbf%d' % r])
                        for vc in range(2):
                            pb, pk = po[vc]
                            S.op('pe', lambda e, pb=pb, vc=vc, j=j, r=r: e.matmul(pb[:, 8 * j:8 * j + 8], lhsT=Sbf[r][:, vc * 128:(vc + 1) * 128], rhs=qtil[:, 1024 + 8 * j:1032 + 8 * j], start=False, stop=(j == 15)),
                                 reads=['Sbf%d' % r, 'qtil'], writes=[pk], signal=True)
                        S.op('dve', lambda e, j=j, r=r: e.tensor_scalar(out=km[r], in0=kdtok, scalar1=selS[:, j:j + 1], scalar2=None, op0=ALU.mult), reads=[X('kdtok'), 'cst'], writes=['km%d' % r])
                        pS, pSk = self.bank()
                        S.op('pe', lambda e, pS=pS, r=r, c=c: e.matmul(pS[:, 0:256], lhsT=km[r], rhs=vtok[:, c, :], start=True, stop=True), reads=['km%d' % r, 'vtok'], writes=[pSk])
                        S.op('dve', lambda e, pS=pS, j=j, r=r: e.scalar_tensor_tensor(out=Sout[r], in0=Sin[r], scalar=ebls[:, j:j + 1], in1=pS[:, 0:256], op0=ALU.mult, op1=ALU.add),
                             reads=[pSk, 'ebl', 'Sin%d' % r], writes=['Sout%d' % r])
                        S.dma('sp', self.o_gla_s[j, h], Sout[r], reads=['Sout%d' % r], writes=['o_gla_s'])
                pss, pssk = self.bank()
                for vc in range(2):
                    pb, pk = po[vc]
                    S.op('act', lambda e, pb=pb, vc=vc: e.copy(out=oT[:, vc, :], in_=pb[:, 0:128]), reads=[pk], writes=[X('oT%d' % vc)])
                    S.op('act', lambda e, vc=vc: e.activation(out=sq[:, vc, :], in_=oT[:, vc, :], func=AF.Square), reads=[X('oT%d' % vc)], writes=[X('sq%d' % vc)])
                    S.op('pe', lambda e, pss=pss, vc=vc: e.matmul(pss[:, 0:128], lhsT=onesb, rhs=sq[:, vc, :], start=(vc == 0), stop=(vc == 1)), reads=[X('sq%d' % vc), 'onesb'], writes=[pssk], signal=(vc == 1))
                S.op('dve', lambda e, pss=pss: e.tensor_scalar(out=rs_, in0=pss[:, 0:128], scalar1=1.0 / 256, scalar2=EPS, op0=ALU.mult, op1=ALU.add), reads=[pssk], writes=[X('rs_')])
                S.op('act', lambda e: e.activation(out=rs_, in_=rs_, func=AF.Sqrt), reads=[X('rs_')], writes=[X('rs_')])
                S.op('dve', lambda e: e.reciprocal(out=rs_, in_=rs_), reads=[X('rs_')], writes=[X('rs_')])
                for vc in range(2):
                    S.op('dve', lambda e, vc=vc: e.scalar_tensor_tensor(out=t1, in0=oT[:, vc, :], scalar=self.col("glang", 2 * h + vc), in1=rs_, op0=ALU.mult, op1=ALU.mult),
                         reads=[X('oT%d' % vc), X('rs_'), 'cols'], writes=[X('t1')])
                    S.op('dve', lambda e, vc=vc, sl=sl: e.tensor_tensor(out=ogT[:, vc, sl], in0=t1, in1=rsT[:, vc, sl], op=ALU.mult), reads=[X('t1'), 'rsT'], writes=['srcT'])
            if blk == self.dbg.get('nblk', NBLK) - 1:
                S.dma('sp', self.o_gla_p[h], Sg, reads=['Sg'], writes=['o_gla_p'])
            else:
                S.dma('sp', self.scr_gla[h], Sg, reads=['Sg'], writes=['scr_gla'])
            self.out_proj(self.w_ab_out, h * 256, 2, ogT)
            S.barrier(pool=False)
        if self.dbg.get('s5', True):
            self.s5(blk)

    def s5_setup(self):
        S = self.S
        PW = self.PW
        o = [0]

        def alloc(n):
            r = o[0]
            o[0] += n
            return r
        T_ = lambda: self.af(alloc(32), 32)
        dt, mag, th, tw, sn, cs, t0, t1, t2, fre, fim = [T_() for _ in range(11)]
        lr = self.col("lamre", 0, 32)
        li = self.col("lamim", 0, 32)
        PI = float(np.pi)
        n = [0]

        def dv(fn, r, w):
            S.op('dve', fn, reads=r, writes=w)

        def ac(fn, r, w):
            S.op('act', fn, reads=r, writes=w)
        ac(lambda e: e.activation(out=dt, in_=self.col("logdt", 0, 32), func=AF.Exp), ['cols'], ['dt'])
        dv(lambda e: e.tensor_tensor(out=t0, in0=lr, in1=dt, op=ALU.mult), ['dt', 'cols'], ['t0'])
        ac(lambda e: e.activation(out=mag, in_=t0, func=AF.Exp), ['t0'], ['mag'])
        dv(lambda e: e.tensor_tensor(out=th, in0=li, in1=dt, op=ALU.mult), ['dt', 'cols'], ['th'])

        def wrapped_sin(dst, shift, key):
            dv(lambda e: e.tensor_scalar(out=tw, in0=th, scalar1=shift, scalar2=None, op0=ALU.add), ['th'], ['tw'])
            for _ in range(4):
                dv(lambda e: e.tensor_scalar(out=t1, in0=tw, scalar1=PI, scalar2=-2 * PI, op0=ALU.is_gt, op1=ALU.mult), ['tw'], ['t1'])
                dv(lambda e: e.tensor_tensor(out=tw, in0=tw, in1=t1, op=ALU.add), ['t1', 'tw'], ['tw'])
            ac(lambda e: e.activation(out=dst, in_=tw, func=AF.Sin), ['tw'], [key])
        wrapped_sin(sn, 0.0, 'sn')
        wrapped_sin(cs, PI / 2, 'cs')
        are, aim, nim = PW[:, 0, 0, :], PW[:, 1, 0, :], PW[:, 2, 0, :]
        dv(lambda e: e.tensor_tensor(out=are, in0=mag, in1=cs, op=ALU.mult), ['mag', 'cs'], ['PW'])
        dv(lambda e: e.tensor_tensor(out=aim, in0=mag, in1=sn, op=ALU.mult), ['mag', 'sn'], ['PW'])
        dv(lambda e: e.tensor_tensor(out=t0, in0=lr, in1=lr, op=ALU.mult), ['cols'], ['t0'])
        dv(lambda e: e.tensor_tensor(out=t1, in0=li, in1=li, op=ALU.mult), ['cols'], ['t1'])
        dv(lambda e: e.tensor_tensor(out=t0, in0=t0, in1=t1, op=ALU.add), ['t0', 't1'], ['t0'])
        dv(lambda e: e.reciprocal(out=t0, in_=t0), ['t0'], ['t0'])
        dv(lambda e: e.tensor_scalar(out=t1, in0=are, scalar1=-1.0, scalar2=None, op0=ALU.add), ['PW'], ['t1'])
        dv(lambda e: e.tensor_tensor(out=fre, in0=t1, in1=lr, op=ALU.mult), ['t1', 'cols'], ['fre'])
        dv(lambda e: e.tensor_tensor(out=t2, in0=aim, in1=li, op=ALU.mult), ['PW', 'cols'], ['t2'])
        dv(lambda e: e.tensor_tensor(out=fre, in0=fre, in1=t2, op=ALU.add), ['fre', 't2'], ['fre'])
        dv(lambda e: e.tensor_tensor(out=fre, in0=fre, in1=t0, op=ALU.mult), ['fre', 't0'], ['fre'])
        dv(lambda e: e.tensor_tensor(out=fim, in0=aim, in1=lr, op=ALU.mult), ['PW', 'cols'], ['fim'])
        dv(lambda e: e.tensor_tensor(out=t2, in0=t1, in1=li, op=ALU.mult), ['t1', 'cols'], ['t2'])
        dv(lambda e: e.tensor_tensor(out=fim, in0=fim, in1=t2, op=ALU.subtract), ['fim', 't2'], ['fim'])
        dv(lambda e: e.tensor_tensor(out=fim, in0=fim, in1=t0, op=ALU.mult), ['fim', 't0'], ['fim'])

        def cmul(dst, x, y):
            xr, xi = PW[:, 0, x, :], PW[:, 1, x, :]
            yr, yi = PW[:, 0, y, :], PW[:, 1, y, :]
            dr, di_ = PW[:, 0, dst, :], PW[:, 1, dst, :]
            dv(lambda e: e.tensor_tensor(out=t0, in0=xr, in1=yr, op=ALU.mult), ['PW'], ['t0'])
            dv(lambda e: e.tensor_tensor(out=t1, in0=xi, in1=yi, op=ALU.mult), ['PW'], ['t1'])
            dv(lambda e: e.tensor_tensor(out=t2, in0=xr, in1=yi, op=ALU.mult), ['PW'], ['t2'])
            dv(lambda e: e.tensor_tensor(out=tw, in0=xi, in1=yr, op=ALU.mult), ['PW'], ['tw'])
            dv(lambda e: e.tensor_tensor(out=dr, in0=t0, in1=t1, op=ALU.subtract), ['t0', 't1'], ['PW'])
            dv(lambda e: e.tensor_tensor(out=di_, in0=t2, in1=tw, op=ALU.add), ['t2', 'tw'], ['PW'])
        for k in range(1, 8):
            cmul(k, k - 1, 0)
        for k in range(8, 15):
            cmul(k, k - 1 if k > 8 else 7, k - 1 if k > 8 else 7)
        dv(lambda e: e.tensor_scalar(out=PW[:, 2, :, :], in0=PW[:, 1, :, :], scalar1=-1.0, scalar2=None, op0=ALU.mult), ['PW'], ['PW'])
        dv(lambda e: e.memset(self.S5H[:], 0.0), [], K3('S5H'))
        bre = [self.af(alloc(128), 128) for _ in range(2)]
        bim = [self.af(alloc(128), 128) for _ in range(2)]
        o1_ = [self.af(alloc(128), 128) for _ in range(2)]
        ob = [self.ab(alloc(128), 256) for _ in range(2)]
        obT = [self.ab(alloc(128), 256) for _ in range(2)]
        for q in range(32):
            r = q % 2
            S.dma('sp', bre[r], self.bpad[q, 0], writes=['bre%d' % r])
            S.dma('sp', bim[r], self.bpad[q, 1], writes=['bim%d' % r])
            fr, fi = fre[:, q:q + 1], fim[:, q:q + 1]
            dv(lambda e: e.tensor_scalar(out=o1_[r], in0=bim[r], scalar1=fi, scalar2=-1.0, op0=ALU.mult, op1=ALU.mult), ['bim%d' % r, 'fim'], ['o1_%d' % r])
            dv(lambda e: e.scalar_tensor_tensor(out=ob[r][:, 0:128], in0=bre[r], scalar=fr, in1=o1_[r], op0=ALU.mult, op1=ALU.add), ['bre%d' % r, 'o1_%d' % r, 'fre'], ['ob%d' % r])
            dv(lambda e: e.tensor_scalar(out=o1_[r], in0=bre[r], scalar1=fi, scalar2=None, op0=ALU.mult), ['bre%d' % r, 'fim', 'ob%d' % r], ['o1_%d' % r])
            dv(lambda e: e.scalar_tensor_tensor(out=ob[r][:, 128:256], in0=bim[r], scalar=fr, in1=o1_[r], op0=ALU.mult, op1=ALU.add), ['bim%d' % r, 'o1_%d' % r, 'fre'], ['ob%d' % r])
            pt, ptk = self.bank()
            ptv = pt[:].bitcast(BF16)
            for c in range(2):
                S.op('pe', lambda e: e.transpose(out=ptv[:, c * 128:(c + 1) * 128], in_=ob[r][:, c * 128:(c + 1) * 128], identity=self.identb[:]), reads=['ob%d' % r, 'identb'], writes=[ptk], signal=(c == 1))
            S.op('act', lambda e: e.copy(out=obT[r], in_=ptv[:, 0:256]), reads=[ptk], writes=['obT%d' % r])
            S.dma('sp', self.scr_bbT[q].rearrange("c p n -> p c n"), obT[r].rearrange("p (c n) -> p c n", c=2), reads=['obT%d' % r], writes=['scr_bbT'])
        S.barrier()

    def s5(self, blk):
        S = self.S
        NT = self.NT
        T = self.T
        NC = T // 8
        PW = self.PW
        last = (blk == self.dbg.get('nblk', NBLK) - 1)
        S.barrier()
        uT = self.ab(9216, 8 * 1152).rearrange("p (c t) -> p c t", c=8)
        ygT = self.ab(9216 + 4608, 8 * 1152).rearrange("p (c t) -> p c t", c=8)
        yy = [self.af(9216 + 9216 + i * 512, 512) for i in range(2)]
        y2 = [self.af(9216 + 9216 + 1024 + i * 512, 512) for i in range(2)]
        self.rtmp = [self.af(9216 + 9216 + 2048 + i * 256, 256) for i in range(2)]
        self.rt_rr = 0
        for b2 in range(0, 8, 2):
            evs = []
            for b in (b2, b2 + 1):
                def ev_u(t0, tn, pb, pk, b=b):
                    S.op('act', lambda e: e.copy(out=uT[:, b, t0:t0 + tn], in_=pb[:, 0:tn]), reads=[pk], writes=['uT'])
                evs.append(ev_u)
            self.proj_feat_multi(self.w_ab_in, 3088 + b2 * 128, evs)
        S.barrier()
        o = [0]

        def alloc(n):
            r = o[0]
            o[0] += n
            assert o[0] <= 9216, o[0]
            return r
        groups = self.groups()
        H3 = lambda a: a[:, 0:T].rearrange("p (c i) -> p c i", i=8)
        B = []
        for sl_ in range(2):
            d = {}
            d['Hre'] = self.af(alloc(1152), 1152)
            d['Him'] = self.af(alloc(1152), 1152)
            d['Hb'] = [self.ab(alloc(576), 1152) for _ in range(2)]
            d['G'] = [[self.af(alloc(128), 128) for _ in range(2)] for _ in range(2)]
            d['hin'] = [self.af(alloc(144), 144) for _ in range(2)]
            d['bbT'] = self.ab(alloc(128), 256).rearrange("p (c n) -> p c n", c=2)
            d['cp'] = self.ab(alloc(128), 256).rearrange("p (c n) -> p c n", c=2)
            d['FS'] = self.af(alloc(32), 32).rearrange("p (c j) -> p c j", c=2)
            d['hr'], d['hi'] = H3(d['Hre']), H3(d['Him'])
            d['s'] = str(sl_)
            B.append(d)

        def cmac2(items):
            ops = [[], [], [], []]
            for (dre, dim, sre, sim, k, key_s, key_d, q) in items:
                a_r, a_i, n_i = PW[:, 0, k, q:q + 1], PW[:, 1, k, q:q + 1], PW[:, 2, k, q:q + 1]
                kr, ki = key_d + 'r', key_d + 'i'
                ops[0].append((lambda e, dre=dre, sre=sre, a_r=a_r: e.scalar_tensor_tensor(out=dre, in0=sre, scalar=a_r, in1=dre, op0=ALU.mult, op1=ALU.add), K3(key_s) + [key_d, kr, 'PW'], [kr]))
                ops[1].append((lambda e, dim=dim, sim=sim, a_r=a_r: e.scalar_tensor_tensor(out=dim, in0=sim, scalar=a_r, in1=dim, op0=ALU.mult, op1=ALU.add), K3(key_s) + [key_d, ki, 'PW'], [ki]))
                ops[2].append((lambda e, dre=dre, sim=sim, n_i=n_i: e.scalar_tensor_tensor(out=dre, in0=sim, scalar=n_i, in1=dre, op0=ALU.mult, op1=ALU.add), K3(key_s) + [key_d, kr, 'PW'], [kr]))
                ops[3].append((lambda e, dim=dim, sre=sre, a_i=a_i: e.scalar_tensor_tensor(out=dim, in0=sre, scalar=a_i, in1=dim, op0=ALU.mult, op1=ALU.add), K3(key_s) + [key_d, ki, 'PW'], [ki]))
            for grp in ops:
                for (fn, r_, w_) in grp:
                    S.op('dve', fn, reads=r_, writes=w_)

        pending = []

        def flush():
            for (pq, pfs, pkey) in pending:
                S.dma('sp', self.o_s5_s[pq].rearrange("c p j -> p c j"), pfs, reads=[pkey], writes=['o_s5_s'])
            del pending[:]
        for qq in range(0, 32, 2):
            P = [(qq + j, B[j]) for j in range(2)]
            b = qq // 4
            for q, d in P:
                sk = d['s']
                S.dma('sp', d['bbT'], self.scr_bbT[q].rearrange("c p n -> p c n"), reads=['scr_bbT'], writes=['bbT' + sk])
                S.dma('pool', d['cp'], self.cpad[q].rearrange("c p n -> p c n"), writes=['cp' + sk])
                if NT == 9:
                    S.dma('sp', d['hin'][0][:, 128:144], self.s5in[q, 0], writes=K3('hin' + sk))
                    S.dma('sp', d['hin'][1][:, 128:144], self.s5in[q, 1], writes=K3('hin' + sk))
            flush()
            for q, d in P:
                sk = d['s']
                for (t0, tn) in groups:
                    for c, Hx in ((0, d['Hre']), (1, d['Him'])):
                        pb, pk = self.bank()
                        S.op('pe', lambda e: e.matmul(pb[:, 0:tn], lhsT=d['bbT'][:, c, :], rhs=uT[:, b, t0:t0 + tn], start=True, stop=True), reads=['bbT' + sk, 'uT'], writes=[pk])
                        S.op('act', lambda e: e.copy(out=Hx[:, t0:t0 + tn], in_=pb[:, 0:tn]), reads=[pk], writes=K3('H' + sk))
            for i in range(1, 8):
                cmac2([(d['hr'][:, :, i], d['hi'][:, :, i], d['hr'][:, :, i - 1], d['hi'][:, :, i - 1], 0, 'H' + d['s'], 'H' + d['s'], q) for q, d in P])
            for q, d in P:
                sk = d['s']
                g0 = d['G'][0]
                S.op('act', lambda e: e.copy(out=g0[0], in_=d['hr'][:, 0:128, 7]), reads=K3('H' + sk), writes=K3('G0' + sk))
                S.op('act', lambda e: e.copy(out=g0[1], in_=d['hi'][:, 0:128, 7]), reads=K3('H' + sk), writes=K3('G0' + sk))
            if blk > 0:
                cmac2([(d['G'][0][0][:, 0:1], d['G'][0][1][:, 0:1], self.S5H[:, 0, q:q + 1], self.S5H[:, 1, q:q + 1], 7, 'S5H', 'G0' + d['s'], q) for q, d in P])
            cur = 0
            for k in range(7):
                s_ = 1 << k
                for q, d in P:
                    sk = d['s']
                    src, dst = d['G'][cur], d['G'][1 - cur]
                    S.op('act', lambda e: e.copy(out=dst[0], in_=src[0]), reads=K3('G%d' % cur + sk), writes=K3('G%d' % (1 - cur) + sk))
                    S.op('act', lambda e: e.copy(out=dst[1], in_=src[1]), reads=K3('G%d' % cur + sk), writes=K3('G%d' % (1 - cur) + sk))
                cmac2([(d['G'][1 - cur][0][:, s_:128], d['G'][1 - cur][1][:, s_:128], d['G'][cur][0][:, 0:128 - s_], d['G'][cur][1][:, 0:128 - s_], 7 + k, 'G%d' % cur + d['s'], 'G%d' % (1 - cur) + d['s'], q) for q, d in P])
                cur = 1 - cur
            for q, d in P:
                sk = d['s']
                Sf = d['G'][cur]
                kf = 'G%d' % cur + sk
                hin = d['hin']
                hp_re, hp_im = self.S5H[:, 0, q:q + 1], self.S5H[:, 1, q:q + 1]
                S.op('act', lambda e: e.copy(out=hin[0][:, 0:1], in_=hp_re), reads=K3('S5H'), writes=K3('hin' + sk))
                S.op('act', lambda e: e.copy(out=hin[1][:, 0:1], in_=hp_im), reads=K3('S5H'), writes=K3('hin' + sk))
                S.op('act', lambda e: e.copy(out=hin[0][:, 1:128], in_=Sf[0][:, 0:127]), reads=K3(kf), writes=K3('hin' + sk))
                S.op('act', lambda e: e.copy(out=hin[1][:, 1:128], in_=Sf[1][:, 0:127]), reads=K3(kf), writes=K3('hin' + sk))
                S.op('act', lambda e: e.copy(out=hp_re, in_=Sf[0][:, 127:128]), reads=K3(kf) + K3('hin' + sk), writes=K3('S5H'))
                S.op('act', lambda e: e.copy(out=hp_im, in_=Sf[1][:, 127:128]), reads=K3(kf) + K3('hin' + sk), writes=K3('S5H'))
            for i in range(8):
                cmac2([(d['hr'][:, :, i], d['hi'][:, :, i], d['hin'][0][:, 0:NC], d['hin'][1][:, 0:NC], i, 'hin' + d['s'], 'H' + d['s'], q) for q, d in P])
            for q, d in P:
                sk = d['s']
                pb4 = q % 4
                if NT == 9:
                    S.op('act', lambda e: e.copy(out=d['FS'][:, 0, :], in_=d['hr'][:, 128:144, 7]), reads=K3('H' + sk), writes=['FS' + sk])
                    S.op('act', lambda e: e.copy(out=d['FS'][:, 1, :], in_=d['hi'][:, 128:144, 7]), reads=K3('H' + sk), writes=['FS' + sk])
                    pending.append((q, d['FS'], 'FS' + sk))
                S.op('act', lambda e: e.copy(out=d['Hb'][0][:, 0:T], in_=d['Hre'][:, 0:T]), reads=K3('H' + sk), writes=['Hb' + sk])
                S.op('act', lambda e: e.activation(out=d['Hb'][1][:, 0:T], in_=d['Him'][:, 0:T], func=AF.Copy, scale=-1.0), reads=K3('H' + sk), writes=['Hb' + sk])
                for gi, (t0, tn) in enumerate(groups):
                    lb, lk = self.lbank(gi)
                    S.op('pe', lambda e: e.matmul(lb[:, 0:tn], lhsT=d['cp'][:, 0, :], rhs=d['Hb'][0][:, t0:t0 + tn], start=(pb4 == 0), stop=False), reads=['cp' + sk, 'Hb' + sk], writes=[lk], signal=False)
                    S.op('pe', lambda e: e.matmul(lb[:, 0:tn], lhsT=d['cp'][:, 1, :], rhs=d['Hb'][1][:, t0:t0 + tn], start=False, stop=(pb4 == 3)), reads=['cp' + sk, 'Hb' + sk], writes=[lk])
            if P[1][0] % 4 == 3:
                for gi, (t0, tn) in enumerate(groups):
                    lb, lk = self.lbank(gi)
                    y_, y2_ = yy[gi % 2][:, 0:tn], y2[gi % 2][:, 0:tn]
                    ky, ky2 = 'yy%d' % (gi % 2), 'y2%d' % (gi % 2)
                    S.op('dve', lambda e: e.scalar_tensor_tensor(out=y_, in0=uT[:, b, t0:t0 + tn], scalar=self.col("s5d", b), in1=lb[:, 0:tn], op0=ALU.mult, op1=ALU.add), reads=[lk, 'uT', 'cols'], writes=[ky])
                    S.op('dve', lambda e: e.tensor_tensor(out=y2_, in0=y_, in1=y_, op=ALU.mult), reads=[ky], writes=[ky2])
                    S.op('dve', lambda e: e.tensor_scalar(out=y2_, in0=y2_, scalar1=0.044715, scalar2=1.0, op0=ALU.mult, op1=ALU.add), reads=[ky2], writes=[ky2])
                    S.op('dve', lambda e: e.tensor_tensor(out=y2_, in0=y2_, in1=y_, op=ALU.mult), reads=[ky, ky2], writes=[ky2])
                    S.op('act', lambda e: e.activation(out=y2_, in_=y2_, func=AF.Sigmoid, scale=1.5957691216057308), reads=[ky2], writes=[ky2])
                    S.op('dve', lambda e: e.tensor_tensor(out=ygT[:, b, t0:t0 + tn], in0=y_, in1=y2_, op=ALU.mult), reads=[ky, ky2], writes=['ygT'])
        flush()
        if last:
            S.dma('sp', self.o_s5_p.rearrange("c p q -> p c q"), self.S5H[:], reads=K3('S5H'), writes=['o_s5_p'])
        S.barrier()
        for oc in range(8):
            wv, wk = self.load_w(self.w_glu, 0, 8, oc * 128, 128)
            for gi, (t0, tn) in enumerate(groups):
                pb, pk = self.bank()
                for k in range(8):
                    S.op('pe', lambda e: e.matmul(pb[:, 0:tn], lhsT=wv[:, k, :], rhs=ygT[:, k, t0:t0 + tn], start=(k == 0), stop=(k == 7)), reads=[wk, 'ygT'], writes=[pk], signal=(k == 7))
                y2_ = y2[gi % 2][:, 0:tn]
                ky2 = 'y2%d' % (gi % 2)
                S.op('act', lambda e: e.activation(out=y2_, in_=pb[:, 0:tn], func=AF.Sigmoid, bias=self.col("s5bglu", oc), scale=1.0), reads=[pk, 'cols'], writes=[ky2])
                S.op('dve', lambda e: e.tensor_tensor(out=uT[:, oc, t0:t0 + tn], in0=ygT[:, oc, t0:t0 + tn], in1=y2_, op=ALU.mult), reads=[ky2, 'ygT'], writes=['srcT'])
        self.out_proj(self.w_ab_out, 1024, 8, uT)
        S.barrier()

    def mixer_ml(self, blk):
        S = self.S
        self.gen_banks = [0, 1, 2]
        NT = self.NT
        T = self.T
        W = self.w_ml_in
        last = (blk == self.dbg.get('nblk', NBLK) - 1)
        o = [9216]

        def alloc(n):
            r = o[0]
            o[0] += n
            assert o[0] <= self.ARW, o[0]
            return r
        E1 = self.af(alloc(1152), 1152)
        A0 = alloc(2304)
        ipT = self.af(A0, 1152)
        cum = self.af(A0 + 1152, 1152)
        qT = self.ab(A0, 2 * 1152).rearrange("p (c t) -> p c t", c=2)
        kT = self.ab(A0 + 1152, 2 * 1152).rearrange("p (c t) -> p c t", c=2)
        vtok = self.ab(alloc(NT * 256), NT * 512).rearrange("p (i v) -> p i v", i=NT)
        osT = self.ab(alloc(2304), 4 * 1152).rearrange("p (c t) -> p c t", c=4)
        CT = self.af(alloc(1024), 1024).rearrange("p (c v) -> p c v", c=2)
        CTbf = self.ab(alloc(512), 1024).rearrange("p (c v) -> p c v", c=2)
        E2C = self.af(alloc(36), 36).rearrange("p (c h) -> p c h", h=4)
        E3C = self.af(alloc(36), 36).rearrange("p (c h) -> p c h", h=4)
        U = [self.af(alloc(128), 128) for _ in range(6)]
        Wm = self.af(alloc(128), 128)
        attm = self.ab(alloc(64), 128)
        qtl = self.ab(alloc(128), 256).rearrange("p (c t) -> p c t", c=2)
        kdtok = self.ab(alloc(128), 256)
        nrep = self.ab(alloc(128), 256).rearrange("p (c t) -> p c t", c=2)
        rden = self.af(alloc(128), 128)
        sq = self.ab(alloc(256), 512).rearrange("p (c t) -> p c t", c=4)
        rs_ = self.af(alloc(128), 128)
        t1 = self.af(alloc(128), 128)
        Cin = self.af(alloc(512), 512)
        Cout = self.af(alloc(512), 512)
        Cbf = self.ab(alloc(256), 512)
        km = self.ab(alloc(64), 128)
        N0 = self.af(alloc(32), 32).rearrange("p (c j) -> p c j", c=2)
        NOUT = self.af(alloc(32), 32).rearrange("p (c j) -> p c j", c=2)
        colS = self.af(alloc(32), 32)
        self.rtmp = [Cin[:, 0:256], Cin[:, 256:512]]
        self.rt_rr = 0
        maskP = self.cst[:, 128:256]
        maskS = self.cst[:, 256:384]
        selS = self.cst[:, 384:400]
        onesb = self.onesb[:]
        id4 = self.cst[0:4, 0:4]
        R = lambda a: a[0:4, :]
        mrun = self.sm3[0:4, 0:1]
        MS = self.sm3[0:4, 1:9]
        BLS = self.sm3[0:4, 9:17]
        M0 = self.sm3[0:4, 17:33]
        MFS = self.sm3[0:4, 33:49]
        DEC = self.sm3[0:4, 49:65]
        EM0 = self.sm3[0:4, 65:81]
        tt = self.sm3[0:4, 81:97]
        EMF = self.sm3[0:4, 97:98]
        ncol = self.sm3[:, 98:100]
        emfcol = self.sm3[:, 100:101]
        negbf = self.sm3[0:4, 101:102]
        S.barrier(pool=False)
        S.op('dve', lambda e: e.tensor_scalar(out=negbf, in0=self.col("mlbf")[0:4, :], scalar1=-1.0, scalar2=None, op0=ALU.mult), reads=['cols'], writes=['negbf'])
        if blk == 0:
            S.op('dve', lambda e: e.memset(mrun, 0.0), writes=['mrun'])

        def ev_i(t0, tn, pb, pk):
            S.op('act', lambda e: e.activation(out=ipT[0:4, t0:t0 + tn], in_=pb[0:4, 0:tn], func=AF.Identity, bias=self.col("mlbi")[0:4, :], scale=1.0), reads=[pk, 'cols'], writes=['ipT'])
        self.proj_feat(W, 6144, 4, ev_i)

        def ev_f(t0, tn, pb, pk):
            S.op('act', lambda e: e.activation(out=E1[0:4, t0:t0 + tn], in_=pb[0:4, 0:tn], func=AF.Exp, bias=negbf, scale=-1.0), reads=[pk, 'negbf'], writes=['E1'])
        self.proj_feat(W, 6148, 4, ev_f)
        S.op('act', lambda e: e.activation(out=E1[0:4, 0:T], in_=E1[0:4, 0:T], func=AF.Ln, bias=1.0, scale=1.0), reads=['E1'], writes=['E1'])
        S.op('dve', lambda e: e.tensor_tensor_scan(out=cum[0:4, 0:T], data0=self.rst[0:4, 0:T], data1=E1[0:4, 0:T], initial=0.0, op0=ALU.mult, op1=ALU.add),
             reads=['E1', 'rst'], writes=['cum'])
        if NT == 9:
            S.dma('sp', M0, self.sm_in, writes=['M0'])
        for c in range(NT):
            sl = slice(c * 128, (c + 1) * 128)
            u1, u2, u3, e2, e3, ux = [R(a) for a in U]
            v3 = lambda a: a.rearrange("p (j i) -> p j i", i=8)
            if c < 8:
                if c == 0:
                    S.op('dve', lambda e: e.tensor_copy(out=u1, in_=cum[0:4, sl]), reads=['cum'], writes=['u1'])
                else:
                    S.op('dve', lambda e, sl=sl, c=c: e.tensor_scalar(out=u1, in0=cum[0:4, sl], scalar1=cum[0:4, c * 128 - 1:c * 128], scalar2=None, op0=ALU.subtract), reads=['cum'], writes=['u1'])
                S.op('dve', lambda e, sl=sl: e.tensor_tensor(out=u2, in0=ipT[0:4, sl], in1=u1, op=ALU.add), reads=['ipT', 'u1'], writes=['u2'])
                S.op('dve', lambda e: e.tensor_scalar(out=u3, in0=u2, scalar1=u1[:, 127:128], scalar2=None, op0=ALU.subtract), reads=['u2', 'u1'], writes=['u3'])
                S.op('dve', lambda e, c=c: e.tensor_reduce(out=MS[:, c:c + 1], in_=u3, axis=mybir.AxisListType.X, op=ALU.max), reads=['u3'], writes=['MS'])
                S.op('dve', lambda e, c=c: e.tensor_copy(out=BLS[:, c:c + 1], in_=u1[:, 127:128]), reads=['u1'], writes=['BLS'])
                S.op('dve', lambda e, c=c: e.tensor_tensor(out=tt[:, 0:1], in0=mrun, in1=BLS[:, c:c + 1], op=ALU.subtract), reads=['mrun', 'BLS'], writes=['tt'])
                S.op('dve', lambda e, c=c: e.tensor_tensor(out=mrun, in0=tt[:, 0:1], in1=MS[:, c:c + 1], op=ALU.max), reads=['tt', 'MS'], writes=['mrun'])
            else:
                S.op('dve', lambda e, sl=sl: e.tensor_copy(out=u1, in_=cum[0:4, sl]), reads=['cum'], writes=['u1'])
                S.op('dve', lambda e, sl=sl: e.tensor_tensor(out=u2, in0=ipT[0:4, sl], in1=u1, op=ALU.add), reads=['ipT', 'u1'], writes=['u2'])
                S.op('dve', lambda e: e.tensor_tensor(out=v3(u3), in0=v3(u2), in1=v3(u1)[:, :, 7:8].to_broadcast([4, 16, 8]), op=ALU.subtract), reads=['u2', 'u1'], writes=['u3'])
                S.op('dve', lambda e: e.tensor_reduce(out=MFS, in_=v3(u3), axis=mybir.AxisListType.X, op=ALU.max), reads=['u3'], writes=['MFS'])
                S.op('dve', lambda e: e.tensor_tensor(out=tt, in0=M0, in1=v3(u1)[:, :, 7], op=ALU.subtract), reads=['M0', 'u1'], writes=['tt'])
                S.op('dve', lambda e: e.tensor_tensor(out=MFS, in0=MFS, in1=tt, op=ALU.max), reads=['tt', 'MFS'], writes=['MFS'])
                S.op('dve', lambda e: e.tensor_tensor(out=tt, in0=tt, in1=MFS, op=ALU.subtract), reads=['tt', 'MFS'], writes=['tt'])
                S.op('act', lambda e: e.activation(out=DEC, in_=tt, func=AF.Exp), reads=['tt'], writes=['DEC'])
                S.op('act', lambda e: e.activation(out=EM0, in_=M0, func=AF.Exp), reads=['M0'], writes=['EM0'])
                S.op('dve', lambda e: e.tensor_tensor(out=v3(u3), in0=v3(u3), in1=MFS.rearrange("p (j i) -> p j i", i=1).to_broadcast([4, 16, 8]), op=ALU.subtract), reads=['u3', 'MFS'], writes=['u3'])
                S.dma('sp', self.o_ml_m_s, MFS, reads=['MFS'], writes=['o_ml_m_s'])
            S.op('act', lambda e, sl=sl: e.activation(out=E1[0:4, sl], in_=u1, func=AF.Exp, scale=-1.0), reads=['u1'], writes=['E1'])
            S.op('act', lambda e: e.activation(out=e2, in_=u2, func=AF.Exp), reads=['u2'], writes=['e2'])
            S.op('act', lambda e: e.activation(out=e3, in_=u3, func=AF.Exp), reads=['u3'], writes=['e3'])
            for (src, dst, nm) in ((e2, E2C, 'e2'), (e3, E3C, 'e3')):
                pb, pk = self.bank()
                S.op('pe', lambda e, pb=pb, src=src: e.matmul(pb[:, 0:4], lhsT=src, rhs=id4, start=True, stop=True), reads=[nm, 'cst'], writes=[pk])
                S.op('act', lambda e, pb=pb, dst=dst, c=c: e.copy(out=dst[:, c, :], in_=pb[:, 0:4]), reads=[pk], writes=['EC'])
        if last:
            S.op('act', lambda e: e.activation(out=EMF, in_=mrun, func=AF.Exp, scale=-1.0), reads=['mrun'], writes=['EMF'])
            S.dma('sp', self.o_ml_m_p, mrun, reads=['mrun'], writes=['o_ml_m_p'])
        S.barrier(pool=False)

        for h in self.dbg.get('heads', range(4)):
            selh = self.cst[0:4, 528 + h * 128:528 + (h + 1) * 128]
            evq, evk = [], []
            for dc in range(2):
                def ev_q(t0, tn, pb, pk, dc=dc):
                    S.op('act', lambda e: e.copy(out=qT[:, dc, t0:t0 + tn], in_=pb[:, 0:tn]), reads=[pk], writes=['qT'])
                evq.append(ev_q)

                def ev_k(t0, tn, pb, pk, dc=dc):
                    S.op('act', lambda e: e.activation(out=kT[:, dc, t0:t0 + tn], in_=pb[:, 0:tn], func=AF.Copy, scale=1.0 / 16), reads=[pk], writes=['kT'])
                evk.append(ev_k)
            self.proj_feat_multi(W, h * 256, evq)
            self.proj_feat_multi(W, 1024 + h * 256, evk)
            for half in range(2):
                def ev_v(i, pb, pk, half=half):
                    S.op('act', lambda e: e.copy(out=vtok[:, i, half * 256:(half + 1) * 256], in_=pb[:, 0:256]), reads=[pk], writes=['vtok'])
                self.proj_tok(W, 2048 + h * 512 + half * 256, 256, ev_v)
            for v2 in range(0, 4, 2):
                evs = []
                for vc in (v2, v2 + 1):
                    def ev_o(t0, tn, pb, pk, vc=vc):
                        S.op('act', lambda e: e.activation(out=osT[:, vc, t0:t0 + tn], in_=pb[:, 0:tn], func=AF.Sigmoid), reads=[pk], writes=['srcT'])
                    evs.append(ev_o)
                self.proj_feat_multi(W, 4096 + h * 512 + v2 * 128, evs)
            if blk == 0:
                S.op('dve', lambda e: e.memset(CT[:, 0, :], 0.0), writes=['CT'])
                S.op('dve', lambda e: e.memset(CT[:, 1, :], 0.0), writes=['CT'])
                S.op('dve', lambda e: e.memset(ncol, 0.0), writes=['ncol'])
            else:
                S.dma('sp', CT, self.scr_mlC[h].rearrange("(c p) v -> p c v", p=128), reads=['scr_mlC'], writes=['CT'])
                S.dma('sp', ncol, self.scr_mln[h], reads=['scr_mln'], writes=['ncol'])
            if NT == 9:
                S.dma('sp', N0, self.sn_in[h], writes=['N0'])
                for (rows, c0) in ((DEC, 0), (EM0, 16)):
                    pb, pk = self.bank()
                    S.op('pe', lambda e, pb=pb, rows=rows: e.matmul(pb[:, 0:16], lhsT=selh, rhs=rows, start=True, stop=True), reads=['DEC', 'EM0', 'cst'], writes=[pk])
                    S.op('act', lambda e, pb=pb, c0=c0: e.copy(out=colS[:, c0:c0 + 16], in_=pb[:, 0:16]), reads=[pk], writes=['colS'])
            Wm_, attm_, qtl_, kdtok_, rden_ = Wm, attm, qtl, kdtok, rden
            alt = (U[0], U[4].bitcast(BF16)[:, 0:128], U[1].bitcast(BF16).rearrange("p (c t) -> p c t", c=2), U[2].bitcast(BF16), U[3])
            for c in range(NT):
                sl = slice(c * 128, (c + 1) * 128)
                cp_ = c % 2
                Wm, attm, qtl, kdtok, rden = (Wm_, attm_, qtl_, kdtok_, rden_) if cp_ == 0 else alt
                X = lambda nm: nm + str(cp_)
                per, perk = self.bank()
                S.op('pe', lambda e, per=per, sl=sl: e.matmul(per[:, 0:128], lhsT=selh, rhs=E1[0:4, sl], start=True, stop=True), reads=['E1', 'cst'], writes=[perk])
                S.op('dve', lambda e, per=per, c=c: e.tensor_tensor(out=Wm, in0=per[:, 0:128], in1=(maskP if c < 8 else maskS), op=ALU.mult), reads=[perk, 'cst'], writes=[X('Wm')])
                for dc in range(2):
                    S.op('dve', lambda e, per=per, dc=dc, sl=sl: e.tensor_tensor(out=qtl[:, dc, :], in0=per[:, 0:128], in1=qT[:, dc, sl], op=ALU.mult), reads=[perk, 'qT'], writes=[X('qtl')])
                pa, pak = self.bank()
                for dc in range(2):
                    S.op('pe', lambda e, pa=pa, dc=dc, sl=sl: e.matmul(pa[:, 0:128], lhsT=kT[:, dc, sl], rhs=qT[:, dc, sl], start=(dc == 0), stop=(dc == 1)), reads=['kT', 'qT'], writes=[pak], signal=(dc == 1))
                S.op('dve', lambda e, pa=pa, c=c: e.scalar_tensor_tensor(out=attm, in0=pa[:, 0:128], scalar=E2C[:, c, h:h + 1], in1=Wm, op0=ALU.mult, op1=ALU.mult), reads=[pak, 'EC', X('Wm')], writes=[X('attm')])
                pt, ptk = self.bank()
                ptv = pt[:].bitcast(BF16)
                for dc in range(2):
                    S.op('pe', lambda e, ptv=ptv, dc=dc, sl=sl: e.transpose(out=ptv[:, dc * 128:(dc + 1) * 128], in_=kT[:, dc, sl], identity=self.identb[:]), reads=['kT', 'identb'], writes=[ptk], signal=(dc == 1))
                S.op('act', lambda e, ptv=ptv, c=c: e.activation(out=kdtok, in_=ptv[:, 0:256], func=AF.Copy, scale=E3C[:, c, h:h + 1]), reads=[ptk, 'EC'], writes=[X('kdtok')])
                po = [self.lbank(i) for i in range(4)]
                pd, pdk = self.lbank(4)
                for vc in range(4):
                    pb, pk = po[vc]
                    S.op('pe', lambda e, pb=pb, vc=vc, c=c: e.matmul(pb[:, 0:128], lhsT=vtok[:, c, vc * 128:(vc + 1) * 128], rhs=attm, start=True, stop=False), reads=['vtok', X('attm')], writes=[pk], signal=False)
                S.op('pe', lambda e, pd=pd: e.matmul(pd[:, 0:128], lhsT=onesb, rhs=attm, start=True, stop=False), reads=[X('attm'), 'onesb'], writes=[pdk], signal=False)
                if c < 8:
                    S.op('act', lambda e: e.copy(out=CTbf, in_=CT), reads=['CT'], writes=['CTbf'])
                    for dc in range(2):
                        S.op('dve', lambda e, dc=dc: e.tensor_scalar(out=nrep[:, dc, :], in0=onesb, scalar1=ncol[:, dc:dc + 1], scalar2=None, op0=ALU.mult), reads=['ncol', 'onesb'], writes=['nrep'])
                    for dc in range(2):
                        for vc in range(4):
                            pb, pk = po[vc]
                            S.op('pe', lambda e, pb=pb, vc=vc, dc=dc: e.matmul(pb[:, 0:128], lhsT=CTbf[:, dc, vc * 128:(vc + 1) * 128], rhs=qtl[:, dc, :], start=False, stop=(dc == 1)), reads=['CTbf', X('qtl')], writes=[pk], signal=(dc == 1))
                        S.op('pe', lambda e, pd=pd, dc=dc: e.matmul(pd[:, 0:128], lhsT=nrep[:, dc, :], rhs=qtl[:, dc, :], start=False, stop=(dc == 1)), reads=['nrep', X('qtl')], writes=[pdk], signal=(dc == 1))
                    deccol = Wm[:, 127:128]
                    for dc in range(2):
                        pS, pSk = self.bank()
                        S.op('pe', lambda e, pS=pS, dc=dc, c=c: e.matmul(pS[:, 0:512], lhsT=kdtok[:, dc * 128:(dc + 1) * 128], rhs=vtok[:, c, :], start=True, stop=True), reads=[X('kdtok'), 'vtok'], writes=[pSk])
                        S.op('dve', lambda e, pS=pS, dc=dc: e.scalar_tensor_tensor(out=CT[:, dc, :], in0=CT[:, dc, :], scalar=deccol, in1=pS[:, 0:512], op0=ALU.mult, op1=ALU.add), reads=[pSk, X('Wm'), 'CT', 'CTbf'], writes=['CT'])
                        pn, pnk = self.bank()
                        S.op('pe', lambda e, pn=pn, dc=dc: e.matmul(pn[:, 0:2], lhsT=kdtok[:, dc * 128:(dc + 1) * 128], rhs=onesb[:, 0:2], start=True, stop=True), reads=[X('kdtok'), 'onesb'], writes=[pnk])
                        S.op('dve', lambda e, pn=pn, dc=dc: e.scalar_tensor_tensor(out=ncol[:, dc:dc + 1], in0=ncol[:, dc:dc + 1], scalar=deccol, in1=pn[:, 0:1], op0=ALU.mult, op1=ALU.add), reads=[pnk, X('Wm'), 'ncol', 'nrep'], writes=['ncol'])
                else:
                    for j in range(16):
                        for dc in range(2):
                            S.dma('sp', Cin, self.sC0T[j, h, dc * 128:(dc + 1) * 128, :], writes=['Cin'])
                            S.op('act', lambda e, j=j: e.activation(out=Cbf, in_=Cin, func=AF.Copy, scale=colS[:, 16 + j:17 + j]), reads=['Cin', 'colS'], writes=['Cbf'])
                            S.op('dve', lambda e, j=j, dc=dc: e.tensor_scalar(out=nrep[:, dc, :], in0=onesb, scalar1=N0[:, dc, j:j + 1], scalar2=colS[:, 16 + j:17 + j], op0=ALU.mult, op1=ALU.mult), reads=['N0', 'colS', 'onesb'], writes=['nrep'])
                            fin = (j == 15 and dc == 1)
                            for vc in range(4):
                                pb, pk = po[vc]
                                S.op('pe', lambda e, pb=pb, vc=vc, dc=dc, j=j, fin=fin: e.matmul(pb[:, 8 * j:8 * j + 8], lhsT=Cbf[:, vc * 128:(vc + 1) * 128], rhs=qtl[:, dc, 8 * j:8 * j + 8], start=False, stop=fin), reads=['Cbf', X('qtl')], writes=[pk], signal=(vc == 3))
                            S.op('pe', lambda e, pd=pd, dc=dc, j=j, fin=fin: e.matmul(pd[:, 8 * j:8 * j + 8], lhsT=nrep[:, dc, :], rhs=qtl[:, dc, 8 * j:8 * j + 8], start=False, stop=fin), reads=['nrep', X('qtl')], writes=[pdk])
                            S.op('dve', lambda e, j=j, dc=dc: e.tensor_scalar(out=km, in0=kdtok[:, dc * 128:(dc + 1) * 128], scalar1=selS[:, j:j + 1], scalar2=None, op0=ALU.mult), reads=[X('kdtok'), 'cst'], writes=['km'])
                            pS, pSk = self.bank()
                            S.op('pe', lambda e, pS=pS, c=c: e.matmul(pS[:, 0:512], lhsT=km, rhs=vtok[:, c, :], start=True, stop=True), reads=['km', 'vtok'], writes=[pSk])
                            S.op('dve', lambda e, pS=pS, j=j: e.scalar_tensor_tensor(out=Cout, in0=Cin, scalar=colS[:, j:j + 1], in1=pS[:, 0:512], op0=ALU.mult, op1=ALU.add), reads=[pSk, 'colS', 'Cin', 'Cbf'], writes=['Cout'])
                            S.dma('sp', self.o_ml_C_s[j, h, dc * 128:(dc + 1) * 128, :], Cout, reads=['Cout'], writes=['o_ml_C_s'])
                            pn, pnk = self.bank()
                            S.op('pe', lambda e, pn=pn: e.matmul(pn[:, 0:2], lhsT=km, rhs=onesb[:, 0:2], start=True, stop=True), reads=['km', 'onesb'], writes=[pnk])
                            S.op('dve', lambda e, pn=pn, dc=dc, j=j: e.scalar_tensor_tensor(out=NOUT[:, dc, j:j + 1], in0=N0[:, dc, j:j + 1], scalar=colS[:, j:j + 1], in1=pn[:, 0:1], op0=ALU.mult, op1=ALU.add), reads=[pnk, 'colS', 'N0'], writes=['NOUT'])
                    S.dma('sp', self.o_ml_n_s[h], NOUT, reads=['NOUT'], writes=['o_ml_n_s'])
                S.op('act', lambda e, pd=pd: e.activation(out=rden, in_=pd[:, 0:128], func=AF.Abs), reads=[pdk], writes=[X('rden')])
                S.op('dve', lambda e: e.tensor_scalar(out=rden, in0=rden, scalar1=1.0, scalar2=None, op0=ALU.max), reads=[X('rden')], writes=[X('rden')])
                S.op('dve', lambda e: e.reciprocal(out=rden, in_=rden), reads=[X('rden')], writes=[X('rden')])
                pss, pssk = self.bank()
                for vc in range(4):
                    pb, pk = po[vc]
                    S.op('act', lambda e, pb=pb, vc=vc: e.activation(out=sq[:, vc, :], in_=pb[:, 0:128], func=AF.Square), reads=[pk], writes=['sq%d' % vc])
                    S.op('pe', lambda e, pss=pss, vc=vc: e.matmul(pss[:, 0:128], lhsT=onesb, rhs=sq[:, vc, :], start=(vc == 0), stop=(vc == 3)), reads=['sq%d' % vc, 'onesb'], writes=[pssk], signal=(vc == 3))
                S.op('dve', lambda e, pss=pss: e.tensor_tensor(out=rs_, in0=pss[:, 0:128], in1=rden, op=ALU.mult), reads=[pssk, X('rden')], writes=['rs_'])
                S.op('dve', lambda e: e.scalar_tensor_tensor(out=rs_, in0=rs_, scalar=1.0 / 512, in1=rden, op0=ALU.mult, op1=ALU.mult), reads=['rs_', X('rden')], writes=['rs_'])
                S.op('dve', lambda e: e.tensor_scalar(out=rs_, in0=rs_, scalar1=EPS, scalar2=None, op0=ALU.add), reads=['rs_'], writes=['rs_'])
                S.op('act', lambda e: e.activation(out=rs_, in_=rs_, func=AF.Sqrt), reads=['rs_'], writes=['rs_'])
                S.op('dve', lambda e: e.reciprocal(out=rs_, in_=rs_), reads=['rs_'], writes=['rs_'])
                S.op('dve', lambda e: e.tensor_tensor(out=rs_, in0=rs_, in1=rden, op=ALU.mult), reads=['rs_', X('rden')], writes=['rs_'])
                for vc in range(4):
                    pb, pk = po[vc]
                    S.op('dve', lambda e, pb=pb, vc=vc: e.scalar_tensor_tensor(out=t1, in0=pb[:, 0:128], scalar=self.col("mlng", 4 * h + vc), in1=rs_, op0=ALU.mult, op1=ALU.mult), reads=[pk, 'rs_', 'cols'], writes=['t1'])
                    S.op('dve', lambda e, vc=vc, sl=sl: e.tensor_tensor(out=osT[:, vc, sl], in0=t1, in1=osT[:, vc, sl], op=ALU.mult), reads=['t1', 'srcT'], writes=['srcT'])
            if last:
                pb, pk = self.bank()
                S.op('pe', lambda e, pb=pb: e.matmul(pb[:, 0:2], lhsT=selh, rhs=self.sm3[0:4, 96:98], start=True, stop=True), reads=['EMF', 'cst'], writes=[pk])
                S.op('act', lambda e, pb=pb: e.copy(out=emfcol, in_=pb[:, 1:2]), reads=[pk], writes=['emfcol'])
                S.op('dve', lambda e: e.tensor_scalar(out=CT, in0=CT, scalar1=emfcol, scalar2=None, op0=ALU.mult), reads=['CT', 'emfcol'], writes=['CT'])
                S.op('dve', lambda e: e.tensor_scalar(out=ncol, in0=ncol, scalar1=emfcol, scalar2=None, op0=ALU.mult), reads=['ncol', 'emfcol'], writes=['ncol'])
                S.dma('sp', self.o_ml_C_p[h].rearrange("(c p) v -> p c v", p=128), CT, reads=['CT'], writes=['o_ml_C_p'])
                S.dma('sp', self.o_ml_n_p[h], ncol, reads=['ncol'], writes=['o_ml_n_p'])
            else:
                S.dma('sp', self.scr_mlC[h].rearrange("(c p) v -> p c v", p=128), CT, reads=['CT'], writes=['scr_mlC'])
                S.dma('sp', self.scr_mln[h], ncol, reads=['ncol'], writes=['scr_mln'])
            self.out_proj(self.w_ml_out, h * 512, 4, osT)
            S.barrier(pool=False)


_NC_CACHE = {}


def _consts():
    c = np.zeros((128, 1040), np.float32)
    c[:, 0:128] = np.eye(128, dtype=np.float32)
    s = np.arange(128)[:, None]
    t = np.arange(128)[None, :]
    c[:, 128:256] = (s <= t).astype(np.float32)
    c[:, 256:384] = ((s <= t) & (s // 8 == t // 8)).astype(np.float32)
    c[:, 384:400] = (s // 8 == np.arange(16)[None, :]).astype(np.float32)
    c[:, 400:528] = 1.0
    for k in range(4):
        c[k, 528 + k * 128:528 + (k + 1) * 128] = 1.0
    return c


def make_in_maps(inp, with_mixers=True):
    f = lambda a: np.ascontiguousarray(np.asarray(a, np.float32))
    cols = np.zeros((128, NCOLS), np.float32)

    def put(name, arr):
        o, k = COLOFF[name]
        assert arr.shape == (128, k), (name, arr.shape, k)
        cols[:, o:o + k] = arr
    adab = [inp["ab_ada_b"][0], inp["ffn_ada_b"][0], inp["ml_ada_b"][0], inp["ffn_ada_b"][1]]
    ng = [inp["ab_norm_g"][0], inp["ffn_norm_g"][0], inp["ml_norm_g"][0], inp["ffn_norm_g"][1]]
    adaw = [inp["ab_ada_w"][0], inp["ffn_ada_w"][0], inp["ml_ada_w"][0], inp["ffn_ada_w"][1]]
    for s in range(4):
        put("adab%d" % s, _fm(adab[s]))
        put("normg%d" % s, _fm(ng[s]))
    put("glabg", _fm(inp["gla_b_gate"][0]))
    put("glang", _fm(inp["gla_norm_g"][0]))
    put("s5d", _fm(inp["s5_d"][0]))
    put("s5bglu", _fm(inp["s5_b_glu"][0]))
    put("mlng", _fm(inp["ml_out_norm_g"][0]))
    bi = np.zeros((128, 1), np.float32)
    bi[0:4, 0] = np.asarray(inp["ml_b_i"][0], np.float32)
    bf = np.zeros((128, 1), np.float32)
    bf[0:4, 0] = np.asarray(inp["ml_b_f"][0], np.float32)
    put("mlbi", bi)
    put("mlbf", bf)
    r = lambda a: np.ascontiguousarray(np.asarray(a, np.float32).reshape(32, 2, 64).transpose(1, 2, 0).reshape(128, 32))
    put("lamre", r(inp["s5_lam_re"][0]))
    put("lamim", r(inp["s5_lam_im"][0]))
    put("logdt", r(np.broadcast_to(np.asarray(inp["s5_log_dt"][0], np.float32)[:, None], (64, 64))))
    rst = np.ones((128, 1152), np.float32)
    rst[:, 1024::8] = 0.0
    shared = {"cols": cols, "cst": _consts(), "rst": rst,
              "w_ab_in": f(inp["ab_w_in"][0]), "w_gate": f(inp["gla_w_gate"][0]), "w_ab_out": f(inp["ab_w_out"][0]), "w_glu": f(inp["s5_w_glu"][0]),
              "w_ml_in": f(inp["ml_w_in"][0]), "w_ml_out": f(inp["ml_w_out"][0]),
              "gfin": np.ascontiguousarray(np.broadcast_to(np.asarray(inp["final_norm_g"], np.float32)[None, :], (128, D)))}
    for s in range(4):
        shared["ada_w%d" % s] = f(adaw[s])
    for l in range(2):
        shared["ffn_w1_%d" % l] = f(inp["ffn_w1"][l])
        shared["ffn_w3_%d" % l] = f(inp["ffn_w3"][l])
        shared["ffn_w2_%d" % l] = f(inp["ffn_w2"][l])
    bpad = np.zeros((32, 2, 128, 128), np.float32)
    cpad = np.zeros((32, 2, 128, 128), np.float32)
    for q in range(32):
        for g2 in range(2):
            g = 2 * q + g2
            c0 = (2 * (q % 4) + g2) * 16
            bpad[q, 0, g2 * 64:(g2 + 1) * 64, c0:c0 + 16] = inp["s5_b_re"][0][g]
            bpad[q, 1, g2 * 64:(g2 + 1) * 64, c0:c0 + 16] = inp["s5_b_im"][0][g]
            cpad[q, 0, g2 * 64:(g2 + 1) * 64, c0:c0 + 16] = np.asarray(inp["s5_c_re"][0][g]).T
            cpad[q, 1, g2 * 64:(g2 + 1) * 64, c0:c0 + 16] = np.asarray(inp["s5_c_im"][0][g]).T
    shared["bpad"] = bpad
    shared["cpad"] = cpad
    maps = []
    for c in range(8):
        b = c % 4
        m = dict(shared)
        m["xp"] = f(inp["x_prompt"][b])
        m["xs"] = f(np.asarray(inp["x_sample"][16 * c:16 * c + 16]).reshape(128, D))
        crep = np.concatenate([np.repeat(np.asarray(inp["c_prompt"][b:b + 1], np.float32), 128, axis=0),
                               np.repeat(np.asarray(inp["c_sample"][16 * c:16 * c + 16], np.float32), 8, axis=0)], axis=0)
        m["crep"] = np.ascontiguousarray(crep)
        m["sgla"] = f(inp["state_gla"][0, 16 * c:16 * c + 16])
        sl = slice(16 * c, 16 * c + 16)
        m["sC0T"] = f(np.asarray(inp["state_mlstm_C"][0, sl]).transpose(0, 1, 3, 2))
        m["sn_in"] = f(np.asarray(inp["state_mlstm_n"][0, sl]).reshape(16, 4, 2, 128).transpose(1, 3, 2, 0))
        m["sm_in"] = f(np.asarray(inp["state_mlstm_m"][0, sl]).T)
        sre = np.asarray(inp["state_s5_re"][0, sl], np.float32).reshape(16, 32, 128)
        sim_ = np.asarray(inp["state_s5_im"][0, sl], np.float32).reshape(16, 32, 128)
        m["s5in"] = f(np.stack([sre, sim_], axis=0).transpose(2, 0, 3, 1))
        maps.append(m)
    return maps


def kernel(**inp):
    key = "full"
    if key not in _NC_CACHE:
        _NC_CACHE[key] = K(True).build()
    nc = _NC_CACHE[key]
    maps = make_in_maps(inp)
    res = run_bass_kernel_spmd(nc, maps, core_ids=list(range(8)))
    R = res.results
    f32 = lambda a: np.ascontiguousarray(np.asarray(a, np.float32))
    y_prompt = f32(np.stack([R[b]["yp"] for b in range(4)], axis=0))
    y_sample = f32(np.concatenate([R[c]["ys"].reshape(16, 8, D) for c in range(8)], axis=0))
    unp = lambda arr: np.asarray(arr).reshape(2, 64, 32).transpose(2, 0, 1).reshape(64, 64)
    gla_p = f32(np.stack([R[b]["o_gla_p"] for b in range(4)], axis=0)[None])
    s5re_p = f32(np.stack([unp(R[b]["o_s5_p"][0]) for b in range(4)], axis=0)[None])
    s5im_p = f32(np.stack([unp(R[b]["o_s5_p"][1]) for b in range(4)], axis=0)[None])
    C_p = f32(np.stack([np.asarray(R[b]["o_ml_C_p"]).transpose(0, 2, 1) for b in range(4)], axis=0)[None])
    n_p = f32(np.stack([np.asarray(R[b]["o_ml_n_p"]).transpose(0, 2, 1).reshape(4, 256) for b in range(4)], axis=0)[None])
    m_p = f32(np.stack([np.asarray(R[b]["o_ml_m_p"])[:, 0] for b in range(4)], axis=0)[None])
    gla_s = f32(np.concatenate([R[c]["o_gla_s"] for c in range(8)], axis=0)[None])
    uns = lambda arr, k: np.asarray(arr)[:, k].transpose(2, 0, 1).reshape(16, 64, 64)
    s5re_s = f32(np.concatenate([uns(R[c]["o_s5_s"], 0) for c in range(8)], axis=0)[None])
    s5im_s = f32(np.concatenate([uns(R[c]["o_s5_s"], 1) for c in range(8)], axis=0)[None])
    C_s = f32(np.concatenate([np.asarray(R[c]["o_ml_C_s"]).transpose(0, 1, 3, 2) for c in range(8)], axis=0)[None])
    n_s = f32(np.concatenate([np.asarray(R[c]["o_ml_n_s"]).transpose(3, 0, 2, 1).reshape(16, 4, 256) for c in range(8)], axis=0)[None])
    m_s = f32(np.concatenate([np.asarray(R[c]["o_ml_m_s"]).T for c in range(8)], axis=0)[None])
    return (y_prompt, y_sample, gla_p, s5re_p, s5im_p, C_p, n_p, m_p, gla_s, s5re_s, s5im_s, C_s, n_s, m_s)
```
